# Optimizing a Trainium2 kernel written in Bass

```python
import math
import numpy as np
import jax
import jax.numpy as jnp
from jax import lax

D_MODEL = 1024
BATCH = 2
SEQ = 8192
DEPTH = 2

ATTN_GROUPS = ((128, 1), (512, 4), (2048, 16))
ATTN_HEADS_PER_GROUP = 4
ATTN_HEAD_DIM = 128
ATTN_HEADS = ATTN_HEADS_PER_GROUP * 3
ATTN_WIDTH = ATTN_HEADS * ATTN_HEAD_DIM
ATTN_OUT_WIDTH = ATTN_HEADS_PER_GROUP * ATTN_HEAD_DIM
NUM_BUCKETS = 32
MAX_DISTANCE = 1024
MASK_VALUE = -1e30

SSD_D_INNER = 2 * D_MODEL
SSD_HEAD_DIM = 64
SSD_HEADS = SSD_D_INNER // SSD_HEAD_DIM
SSD_GROUPS = 4
SSD_STATE = 128
SSD_CONV = 7
SSD_CHUNK = 256
SSD_CONV_CH = SSD_D_INNER + 2 * SSD_GROUPS * SSD_STATE

FOURIER_GROUPS = 6
FOURIER_GROUP_DIM = 256
FOURIER_WIDTH = FOURIER_GROUPS * FOURIER_GROUP_DIM

N_BRANCHES = 3
IN_SPLITS = (ATTN_WIDTH, ATTN_WIDTH, ATTN_WIDTH, SSD_D_INNER, SSD_CONV_CH, 2 * SSD_HEADS, FOURIER_WIDTH, N_BRANCHES * D_MODEL)
IN_WIDTH = 3 * ATTN_WIDTH + SSD_D_INNER + SSD_CONV_CH + 2 * SSD_HEADS + FOURIER_WIDTH + N_BRANCHES * D_MODEL

MEM_TOKENS = 256
MEM_HEADS = 4
MEM_HEAD_DIM = D_MODEL // MEM_HEADS

D_FF = ((8 * D_MODEL + 3 * 256 - 1) // (3 * 256)) * 256
NORM_EPS = 1e-6

kernel_name = "hybrid_dilated_ssd_fourier_encoder"


def rmsnorm(x, g):
    xf = x.astype(jnp.float32)
    y = xf * lax.rsqrt(jnp.mean(xf * xf, axis=-1, keepdims=True) + NORM_EPS)
    return (y * g.astype(jnp.float32)).astype(x.dtype)


def t5_bucket(rel):
    half_b = NUM_BUCKETS // 2
    exact = half_b // 2
    dist = np.abs(rel)
    log_ratio = np.log(np.maximum(dist, 1) / exact) / np.log(MAX_DISTANCE / exact)
    far = np.minimum(exact + (log_ratio * (half_b - exact)).astype(np.int32), half_b - 1)
    return np.where(rel > 0, half_b, 0) + np.where(dist < exact, dist, far)


def dilated_window_attention(q, k, v, bias_tab, dilation, half):
    bsz, seq, heads, hd = q.shape
    sub = seq // dilation
    nblk = -(-sub // half)
    padded = nblk * half

    def to_sub(t):
        return t.reshape(bsz, sub, dilation, heads, hd).transpose(0, 2, 1, 3, 4)

    def pad_seq(t, lo, hi):
        return jnp.pad(t, ((0, 0), (0, 0), (lo, hi), (0, 0), (0, 0)))

    qb = pad_seq(to_sub(q), 0, padded - sub).reshape(bsz, dilation, nblk, half, heads, hd)

    def kv_blocks(t):
        t = pad_seq(to_sub(t), half, padded - sub + half).reshape(bsz, dilation, nblk + 2, half, heads, hd)
        return jnp.concatenate([t[:, :, :-2], t[:, :, 1:-1], t[:, :, 2:]], axis=3)

    kb, vb = kv_blocks(k), kv_blocks(v)
    rel = np.arange(3 * half)[None, :] - half - np.arange(half)[:, None]
    kpos = np.arange(nblk)[:, None] * half + np.arange(3 * half)[None, :] - half
    mask = (np.abs(rel) <= half)[None] & ((kpos >= 0) & (kpos < sub))[:, None, :]
    bias = jnp.transpose(bias_tab[t5_bucket(rel * dilation)], (2, 0, 1)).astype(jnp.float32)

    logits = jnp.einsum('bdnqhe,bdnkhe->bdnhqk', qb, kb).astype(jnp.float32) * (hd ** -0.5) + bias
    logits = jnp.where(mask[:, None], logits, MASK_VALUE)
    lse = jax.nn.logsumexp(logits, axis=-1)
    probs = jnp.exp(logits - lse[..., None]).astype(v.dtype)
    out = jnp.einsum('bdnhqk,bdnkhe->bdnqhe', probs, vb)
    out = out.reshape(bsz, dilation, padded, heads, hd)[:, :, :sub]
    out = out.transpose(0, 2, 1, 3, 4).reshape(bsz, seq, heads, hd)
    lse = lse.transpose(0, 1, 2, 4, 3).reshape(bsz, dilation, padded, heads)[:, :, :sub]
    lse = lse.transpose(0, 2, 1, 3).reshape(bsz, seq, heads)
    return out, lse


def depthwise_conv_centred(u, w, b):
    k, c = w.shape
    y = lax.conv_general_dilated(u, w[:, None, :], window_strides=(1,), padding=[(k // 2, k // 2)],
                                 dimension_numbers=('NWC', 'WIO', 'NWC'), feature_group_count=c)
    return y + b


def ssd_chunked(xs, dt, a, b_ssm, c_ssm):
    bsz, seq, heads, hp = xs.shape
    groups, nst = b_ssm.shape[2], b_ssm.shape[3]
    hpg = heads // groups
    t = SSD_CHUNK
    nc = -(-seq // t)
    pad = nc * t - seq

    def padseq(u):
        return jnp.pad(u, [(0, 0), (0, pad)] + [(0, 0)] * (u.ndim - 2))

    f32 = jnp.float32
    xd = padseq(xs.astype(f32) * dt[..., None]).reshape(bsz, nc, t, groups, hpg, hp)
    la = jnp.moveaxis(padseq(dt * a).reshape(bsz, nc, t, groups, hpg), 2, -1)
    a_cs = jnp.cumsum(la, axis=-1)
    bc = padseq(b_ssm.astype(f32)).reshape(bsz, nc, t, groups, nst)
    cc = padseq(c_ssm.astype(f32)).reshape(bsz, nc, t, groups, nst)

    tri = np.tril(np.ones((t, t), dtype=bool))
    lmat = jnp.exp(jnp.where(tri, a_cs[..., :, None] - a_cs[..., None, :], -jnp.inf))
    cb = jnp.einsum('bclgn,bcsgn->bcgls', cc, bc)
    y_diag = jnp.einsum('bcgls,bcgels,bcsgep->bclgep', cb, lmat, xd)

    decay_to_end = jnp.exp(a_cs[..., -1:] - a_cs)
    states = jnp.einsum('bcsgn,bcges,bcsgep->bcgepn', bc, decay_to_end, xd)
    chunk_decay = jnp.exp(a_cs[..., -1])

    def step(h, inp):
        s, da = inp
        return h * da[..., None, None] + s, h

    h0 = jnp.zeros((bsz, groups, hpg, hp, nst), f32)
    _, prev = lax.scan(step, h0, (jnp.moveaxis(states, 1, 0), jnp.moveaxis(chunk_decay, 1, 0)))
    prev = jnp.moveaxis(prev, 0, 1)
    y_off = jnp.einsum('bclgn,bcgepn,bcgel->bclgep', cc, prev, jnp.exp(a_cs))
    y = (y_diag + y_off).reshape(bsz, nc * t, heads, hp)[:, :seq]
    return y.astype(xs.dtype)


def hybrid_mixer(h, rel_bias, w_in, gate_bias, attn_q_g, attn_k_g, conv_w, conv_b, dt_bias, a_log,
                 d_skip, ssd_norm_g, w_b_attn, w_b_ssd, w_b_fourier, w_out):
    bsz, seq, _ = h.shape
    f32 = jnp.float32
    proj = h @ w_in
    q, k, v, z, xbc, dt_raw, f_in, gate_logit = jnp.split(proj, np.cumsum(IN_SPLITS)[:-1].tolist(), axis=-1)

    hs = (bsz, seq, ATTN_HEADS, ATTN_HEAD_DIM)
    q = rmsnorm(q.reshape(hs), attn_q_g)
    k = rmsnorm(k.reshape(hs), attn_k_g)
    v = v.reshape(hs)
    outs, lses = [], []
    for gi, (win, dil) in enumerate(ATTN_GROUPS):
        sl = slice(gi * ATTN_HEADS_PER_GROUP, (gi + 1) * ATTN_HEADS_PER_GROUP)
        o, s = dilated_window_attention(q[:, :, sl], k[:, :, sl], v[:, :, sl], rel_bias[:, sl], dil, win // (2 * dil))
        outs.append(o)
        lses.append(s)
    alpha = jax.nn.softmax(jnp.stack(lses), axis=0)[..., None]
    o = jnp.sum(alpha * jnp.stack(outs).astype(f32), axis=0).astype(h.dtype)
    y_attn = o.reshape(bsz, seq, ATTN_OUT_WIDTH) @ w_b_attn

    xbc = jax.nn.silu(depthwise_conv_centred(xbc, conv_w, conv_b))
    xs, b_ssm, c_ssm = jnp.split(xbc, [SSD_D_INNER, SSD_D_INNER + SSD_GROUPS * SSD_STATE], axis=-1)
    xs = xs.reshape(bsz, seq, SSD_HEADS, SSD_HEAD_DIM)
    b_ssm = b_ssm.reshape(bsz, seq, SSD_GROUPS, SSD_STATE)
    c_ssm = c_ssm.reshape(bsz, seq, SSD_GROUPS, SSD_STATE)
    dt = jax.nn.softplus(dt_raw.reshape(bsz, seq, 2, SSD_HEADS).astype(f32) + dt_bias.astype(f32))
    a = -jnp.exp(a_log.astype(f32))
    flip = lambda u: jnp.flip(u, axis=1)
    y_fwd = ssd_chunked(xs, dt[:, :, 0], a[0], b_ssm, c_ssm)
    y_bwd = flip(ssd_chunked(flip(xs), flip(dt[:, :, 1]), a[1], flip(b_ssm), flip(c_ssm)))
    y = (y_fwd + y_bwd + xs * d_skip[:, None]).reshape(bsz, seq, SSD_D_INNER) * jax.nn.silu(z)
    y = rmsnorm(y.reshape(bsz, seq, SSD_GROUPS, SSD_D_INNER // SSD_GROUPS),
                ssd_norm_g.reshape(SSD_GROUPS, SSD_D_INNER // SSD_GROUPS)).reshape(bsz, seq, SSD_D_INNER)
    y_ssd = y @ w_b_ssd

    f = f_in.reshape(bsz, seq, FOURIER_GROUPS, FOURIER_GROUP_DIM).astype(f32)
    f = jnp.fft.fft2(f, axes=(1, 3), norm="ortho").real.astype(h.dtype).reshape(bsz, seq, FOURIER_WIDTH)
    y_fourier = f @ w_b_fourier

    gates = jax.nn.sigmoid((gate_logit + gate_bias).astype(f32)).astype(h.dtype)
    gates = gates.reshape(bsz, seq, N_BRANCHES, D_MODEL)
    merged = gates[:, :, 0] * y_attn + gates[:, :, 1] * y_ssd + gates[:, :, 2] * y_fourier
    return merged @ w_out


def memory_cross_attention(h, m, w_xq, w_xk, w_xv, q_g, k_g, w_xo):
    bsz, seq, _ = h.shape
    q = rmsnorm((h @ w_xq).reshape(bsz, seq, MEM_HEADS, MEM_HEAD_DIM), q_g)
    k = rmsnorm((m @ w_xk).reshape(bsz, -1, MEM_HEADS, MEM_HEAD_DIM), k_g)
    v = (m @ w_xv).reshape(bsz, -1, MEM_HEADS, MEM_HEAD_DIM)
    logits = jnp.einsum('bshe,bmhe->bhsm', q, k).astype(jnp.float32) * (MEM_HEAD_DIM ** -0.5)
    probs = jax.nn.softmax(logits, axis=-1).astype(v.dtype)
    o = jnp.einsum('bhsm,bmhe->bshe', probs, v).reshape(bsz, seq, D_MODEL)
    return o @ w_xo


def swiglu(h, w_gate, w_up, w_down):
    return (jax.nn.silu(h @ w_gate) * (h @ w_up)) @ w_down


def setup_inputs(seed: int = 0) -> dict:
    key = jax.random.key(seed)
    ks = iter(jax.random.split(key, 40))
    L = DEPTH

    def nrm(shape, scale):
        return jax.random.normal(next(ks), shape, jnp.float32) * scale

    def gain(shape):
        return 1.0 + nrm(shape, 0.02)

    dt0 = jnp.exp(jax.random.uniform(next(ks), (L, 2, SSD_HEADS), jnp.float32,
                                     minval=math.log(1e-3), maxval=math.log(1e-1)))
    dt_bias = dt0 + jnp.log(-jnp.expm1(-dt0))
    a_log = jnp.log(jax.random.uniform(next(ks), (L, 2, SSD_HEADS), jnp.float32, minval=1.0, maxval=16.0))
    return {
        "x": nrm((BATCH, SEQ, D_MODEL), 1.0),
        "mem": nrm((BATCH, MEM_TOKENS, D_MODEL), 1.0),
        "rel_bias": nrm((NUM_BUCKETS, ATTN_HEADS), 0.5),
        "mix_norm_g": gain((L, D_MODEL)),
        "w_in": nrm((L, D_MODEL, IN_WIDTH), D_MODEL ** -0.5),
        "gate_bias": nrm((L, N_BRANCHES * D_MODEL), 0.01),
        "attn_q_norm_g": gain((L, ATTN_HEAD_DIM)),
        "attn_k_norm_g": gain((L, ATTN_HEAD_DIM)),
        "conv_w": nrm((L, SSD_CONV, SSD_CONV_CH), SSD_CONV ** -0.5),
        "conv_b": nrm((L, SSD_CONV_CH), 0.01),
        "dt_bias": dt_bias,
        "a_log": a_log,
        "d_skip": gain((L, SSD_HEADS)),
        "ssd_norm_g": gain((L, SSD_D_INNER)),
        "w_branch_attn": nrm((L, ATTN_OUT_WIDTH, D_MODEL), ATTN_OUT_WIDTH ** -0.5),
        "w_branch_ssd": nrm((L, SSD_D_INNER, D_MODEL), SSD_D_INNER ** -0.5),
        "w_branch_fourier": nrm((L, FOURIER_WIDTH, D_MODEL), FOURIER_WIDTH ** -0.5),
        "w_mix_out": nrm((L, D_MODEL, D_MODEL), D_MODEL ** -0.5),
        "xattn_norm_g": gain((L, D_MODEL)),
        "mem_norm_g": gain((L, D_MODEL)),
        "w_xq": nrm((L, D_MODEL, D_MODEL), D_MODEL ** -0.5),
        "w_xk": nrm((L, D_MODEL, D_MODEL), D_MODEL ** -0.5),
        "w_xv": nrm((L, D_MODEL, D_MODEL), D_MODEL ** -0.5),
        "xattn_q_norm_g": gain((L, MEM_HEAD_DIM)),
        "xattn_k_norm_g": gain((L, MEM_HEAD_DIM)),
        "w_xo": nrm((L, D_MODEL, D_MODEL), D_MODEL ** -0.5),
        "ffn_norm_g": gain((L, D_MODEL)),
        "w_ffn_gate": nrm((L, D_MODEL, D_FF), D_MODEL ** -0.5),
        "w_ffn_up": nrm((L, D_MODEL, D_FF), D_MODEL ** -0.5),
        "w_ffn_down": nrm((L, D_FF, D_MODEL), D_FF ** -0.5),
    }


def reference(x, mem, rel_bias, mix_norm_g, w_in, gate_bias, attn_q_norm_g, attn_k_norm_g, conv_w, conv_b,
              dt_bias, a_log, d_skip, ssd_norm_g, w_branch_attn, w_branch_ssd, w_branch_fourier, w_mix_out,
              xattn_norm_g, mem_norm_g, w_xq, w_xk, w_xv, xattn_q_norm_g, xattn_k_norm_g, w_xo,
              ffn_norm_g, w_ffn_gate, w_ffn_up, w_ffn_down):
    for l in range(DEPTH):
        h = rmsnorm(x, mix_norm_g[l])
        x = x + hybrid_mixer(h, rel_bias, w_in[l], gate_bias[l], attn_q_norm_g[l], attn_k_norm_g[l],
                             conv_w[l], conv_b[l], dt_bias[l], a_log[l], d_skip[l], ssd_norm_g[l],
                             w_branch_attn[l], w_branch_ssd[l], w_branch_fourier[l], w_mix_out[l])
        h = rmsnorm(x, xattn_norm_g[l])
        m = rmsnorm(mem, mem_norm_g[l])
        x = x + memory_cross_attention(h, m, w_xq[l], w_xk[l], w_xv[l], xattn_q_norm_g[l],
                                       xattn_k_norm_g[l], w_xo[l])
        h = rmsnorm(x, ffn_norm_g[l])
        x = x + swiglu(h, w_ffn_gate[l], w_ffn_up[l], w_ffn_down[l])
    return x
```

```python
import math
import numpy as np
import concourse.bass as bass
import concourse.mybir as mybir
from concourse.bass_utils import run_bass_kernel_spmd

F32 = mybir.dt.float32
BF16 = mybir.dt.bfloat16
AF = mybir.ActivationFunctionType
ALU = mybir.AluOpType

S = 8192
D = 1024
NT = S // 128
NB = S // 512
DFF = 2816
EPS = 1e-6
NEG = -30000.0

ENGS = ("pe", "act", "dve", "pool", "sp")
NDMASEM = 8


class Op:
    __slots__ = ("eng", "fn", "deps", "dma", "sig", "needed", "idx")

    def __init__(self, eng, fn, dma):
        self.eng, self.fn, self.dma = eng, fn, dma
        self.deps = set()
        self.sig = None
        self.needed = False


class Prog:
    def __init__(self, nc):
        self.nc = nc
        self.ops = []
        self.last_w = {}
        self.readers = {}

    def op(self, eng, fn, reads=(), writes=(), dma=False):
        o = Op(eng, fn, dma)
        o.idx = len(self.ops)
        for k in reads:
            w = self.last_w.get(k)
            if w is not None:
                o.deps.add(w)
        for k in writes:
            w = self.last_w.get(k)
            if w is not None:
                o.deps.add(w)
            for r in self.readers.get(k, ()):
                o.deps.add(r)
        for k in reads:
            self.readers.setdefault(k, []).append(o.idx)
        for k in writes:
            self.last_w[k] = o.idx
            self.readers[k] = []
        o.deps.discard(o.idx)
        self.ops.append(o)
        return o

    def mm(self, out, lhsT, rhs, start, stop, r, w):
        return self.op("pe", lambda e: e.matmul(out, lhsT, rhs, start=start, stop=stop), r, w)

    def tr(self, out, in_, ident, r, w):
        return self.op("pe", lambda e: e.transpose(out, in_, ident), r, w)

    def dma(self, eng, out, in_, r, w, **kw):
        return self.op(eng, lambda e: e.dma_start(out=out, in_=in_, **kw), r, w, dma=True)

    def act(self, out, in_, func, r, w, bias=None, scale=None, accum=None):
        kw = {}
        if bias is not None:
            kw["bias"] = bias
        if scale is not None:
            kw["scale"] = scale
        if accum is not None:
            kw["accum_out"] = accum
        return self.op("act", lambda e: e.activation(out=out, in_=in_, func=func, **kw), r, w)

    def tt(self, eng, out, in0, in1, op, r, w):
        return self.op(eng, lambda e: e.tensor_tensor(out=out, in0=in0, in1=in1, op=op), r, w)

    def ts(self, eng, out, in0, s1, op0, r, w, s2=None, op1=None):
        if op1 is None:
            return self.op(eng, lambda e: e.tensor_scalar(out=out, in0=in0, scalar1=s1, scalar2=None, op0=op0), r, w)
        return self.op(eng, lambda e: e.tensor_scalar(out=out, in0=in0, scalar1=s1, scalar2=s2, op0=op0, op1=op1), r, w)

    def stt(self, out, in0, scalar, in1, op0, op1, r, w):
        return self.op("dve", lambda e: e.scalar_tensor_tensor(out=out, in0=in0, scalar=scalar, in1=in1, op0=op0, op1=op1), r, w)

    def cp(self, eng, out, in_, r, w):
        if eng == "act":
            return self.op("act", lambda e: e.copy(out=out, in_=in_), r, w)
        return self.op(eng, lambda e: e.tensor_copy(out=out, in_=in_), r, w)

    def memset(self, eng, ap, val, w):
        return self.op(eng, lambda e: e.memset(ap, val), (), w)

    def recip(self, out, in_, r, w):
        return self.op("dve", lambda e: e.reciprocal(out=out, in_=in_), r, w)

    def barrier(self):
        deps = set()
        for eng in ENGS:
            last_c = None
            dm = []
            for o in self.ops:
                if o.eng != eng or o.fn is None:
                    continue
                if o.dma:
                    dm.append(o.idx)
                else:
                    last_c = o.idx
            if last_c is not None:
                deps.add(last_c)
            deps.update(dm[-NDMASEM:])
        for eng in ENGS:
            o = Op(eng, None, False)
            o.idx = len(self.ops)
            o.deps = set(deps)
            self.ops.append(o)
        self.last_w = {}
        self.readers = {}

    def emit(self):
        nc = self.nc
        ops = self.ops
        for o in ops:
            if o.eng == "pe" and not o.dma:
                o.deps = {d for d in o.deps if not (ops[d].eng == "pe" and not ops[d].dma)}
            for d in o.deps:
                ops[d].needed = True
        sems = {e: nc.alloc_semaphore("s_" + e) for e in ENGS}
        dsems = {e: [nc.alloc_semaphore("d_%s%d" % (e, i)) for i in range(NDMASEM)] for e in ENGS}
        cnt = {e: 0 for e in ENGS}
        dcnt = {e: 0 for e in ENGS}
        for o in ops:
            if o.fn is None:
                continue
            if o.dma:
                i = dcnt[o.eng]
                dcnt[o.eng] += 1
                o.sig = (dsems[o.eng][i % NDMASEM], 16 * (i // NDMASEM + 1))
            elif o.needed:
                cnt[o.eng] += 1
                o.sig = (sems[o.eng], cnt[o.eng])
        per = {e: [o for o in ops if o.eng == e] for e in ENGS}

        def run(eng, e):
            known = {}
            for o in per[eng]:
                need = {}
                for d in o.deps:
                    s = ops[d].sig
                    if s is None:
                        continue
                    sem, val = s
                    if need.get(id(sem), (None, 0))[1] < val:
                        need[id(sem)] = (sem, val)
                if o.dma:
                    sem, val = o.sig
                    if val - 16 > 0 and need.get(id(sem), (None, 0))[1] < val - 16:
                        need[id(sem)] = (sem, val - 16)
                for k, (sem, val) in need.items():
                    if known.get(k, 0) >= val:
                        continue
                    e.wait_ge(sem, val)
                    known[k] = val
                if o.fn is None:
                    continue
                inst = o.fn(e)
                if o.sig is not None:
                    inst.then_inc(o.sig[0], 16 if o.dma else 1)

        with nc.Block() as blk:
            @blk.tensor
            def _(e):
                run("pe", e)

            @blk.scalar
            def _(e):
                run("act", e)

            @blk.vector
            def _(e):
                run("dve", e)

            @blk.gpsimd
            def _(e):
                run("pool", e)

            @blk.sync
            def _(e):
                run("sp", e)
        return dict(n_ops=len(ops), cnt=cnt, dcnt=dcnt)


class Arena:
    def __init__(self, nc, nbytes):
        self.t = nc.alloc_sbuf_tensor("arena", [128, nbytes // 2], BF16)
        self.cap = nbytes // 2
        self.off = 0
        self.base = 0

    def persist(self):
        self.base = self.off

    def reset(self):
        self.off = self.base

    def alloc(self, shape, dt, parts=128):
        n = int(np.prod(shape))
        e16 = n * (2 if dt == F32 else 1)
        e16 = (e16 + 15) // 16 * 16
        assert self.off + e16 <= self.cap, ("SBUF arena overflow", self.off, e16, self.cap)
        v = self.t[0:parts, self.off:self.off + e16]
        self.off += e16
        if dt == F32:
            v = v.bitcast(F32)
        v = v[:, 0:n]
        if len(shape) == 2:
            v = v.rearrange("p (a b) -> p a b", a=shape[0])
        elif len(shape) == 3:
            v = v.rearrange("p (a b c) -> p a b c", a=shape[0], b=shape[1])
        return v


class K:
    def __init__(self, nc):
        self.nc = nc
        self.P = Prog(nc)
        self.A = Arena(nc, 184 * 1024)
        self.banks = [nc.alloc_psum_tensor("pb%d" % i, [128, 512], F32) for i in range(8)]
        self.uid = 0
        self.scr = {}

    def bank(self, i):
        return self.banks[i][:, :]

    def bank16(self, i):
        return self.banks[i][:, :].bitcast(BF16)

    def key(self, s):
        self.uid += 1
        return (s, self.uid)


def load_cast(k, dst, src, ncols, wkey):
    c = 0
    while c < ncols:
        n = min(2048, ncols - c)
        k.P.dma("pool", dst[:, c:c + n], src[:, c:c + n], [], [wkey])
        c += n


def norm_transpose(k, xtile, xkey, g_bc, consts, hT, hkey, col0, tmp, tkey, pbank, pkey, par):
    P = k.P
    junk, ss, rs, hb = tmp
    P.act(junk, xtile, AF.Square, [xkey], [tkey + "j", tkey + "ss"], accum=ss)
    P.act(rs, ss, AF.Sqrt, [tkey + "ss", "consts"], [tkey + "rs"], bias=consts["eps"], scale=1.0 / D)
    P.recip(rs, rs, [tkey + "rs"], [tkey + "rs"])
    P.stt(hb, xtile, rs, g_bc, ALU.mult, ALU.mult, [xkey, tkey + "rs", "gbc"], [tkey + "hb"])
    p16 = k.bank16(pbank)
    for kc in range(8):
        P.tr(p16[:, kc * 128:(kc + 1) * 128], hb[:, kc * 128:(kc + 1) * 128], consts["ident"], [tkey + "hb", "consts"], [pkey])
    P.cp("act" if par else "dve", hT[:, :, col0:col0 + 128], p16.rearrange("p (a b) -> p a b", a=8), [pkey], [hkey])


def mk_consts(k, cin):
    A, P = k.A, k.P
    c = {}
    c["ident"] = A.alloc([128], BF16)
    c["ones"] = A.alloc([128], BF16)
    c["eps"] = A.alloc([1], F32)
    c["one"] = A.alloc([1], F32)
    c["trif"] = A.alloc([128], BF16)
    c["trib"] = A.alloc([128], BF16)
    c["maskf"] = A.alloc([128], BF16)
    c["maskb"] = A.alloc([128], BF16)
    for nm in ("ident", "trif", "trib", "maskf", "maskb"):
        P.dma("pool", c[nm], cin[nm], [], ["consts"])
    P.memset("dve", c["ones"], 1.0, ["consts"])
    P.memset("dve", c["eps"], EPS, ["consts"])
    P.memset("dve", c["one"], 1.0, ["consts"])
    return c


QK0, V0, Z0, XBC0, DT0, FI0, W1N = 0, 768, 1152, 1664, 2432, 2448, 2832


def build_mixer(k, I, O, c, dbg=None, phases=(1, 2, 3, 4, 5, 6), tag="", gmode="compute"):
    nc, P, A = k.nc, k.P, k.A
    def scr(name, shape, dt):
        if dbg is not None and name in dbg:
            return nc.dram_tensor(name, shape, dt, kind="ExternalOutput").ap()
        if name + tag not in k.scr:
            k.scr[name + tag] = nc.dram_tensor(name + tag, shape, dt).ap()
        return k.scr[name + tag]

    qk_scr = scr("qk_scr", [6, 128, S], BF16)
    v_scr = scr("v_scr", [S, 3, 132], BF16)
    zs_scr = scr("zs_scr", [S, 512], BF16)
    f_scr = scr("f_scr", [S, 384], BF16)
    xbcT_scr = scr("xbcT_scr", [6, 128, S + 8], BF16)
    xsB_scr = scr("xsB_scr", [S, 640], BF16)
    bcT_scr = scr("bcT_scr", [2, 128, S], BF16)
    ybwd_scr = scr("ybwd_scr", [S, 512], F32)
    ynT_scr = scr("ynT_scr", [4, 128, S], BF16)
    un_scr = scr("un_scr", [3, S, 132], F32)
    oT_scr = scr("oT_scr", [128, S], BF16)
    z_scr = scr("z_scr", [128, 128, 384], BF16)
    xT_scr = scr("xT_scr", [6, 128, S], BF16)
    dt_dram = scr("dt_scr", [S, 16], F32)
    g_scr = scr("g_scr", [24, 128, S], BF16) if gmode != "compute" else None

    if 1 in phases:
        A.reset()
        W1 = A.alloc([8, W1N], BF16)
        gbc = A.alloc([D], F32)
        gqk = A.alloc([2], F32)
        dtb = A.alloc([16], F32)
        dt_all = None
        xt = [A.alloc([4, D], F32) for _ in range(2)]
        hT = [A.alloc([8, 512], BF16) for _ in range(2)]
        junk = A.alloc([D], BF16)
        ss = A.alloc([8], F32)
        rs = A.alloc([8], F32)
        hb = [A.alloc([D], BF16) for _ in range(2)]
        sq = [A.alloc([512], BF16) for _ in range(2)]
        lnv = [A.alloc([512], F32) for _ in range(2)]
        qk_st = [A.alloc([6, 512], BF16) for _ in range(2)]
        v_st = [A.alloc([4, 3, 132], BF16) for _ in range(2)]
        z_st = [A.alloc([4, 512], BF16) for _ in range(2)]
        f_st = [A.alloc([4, 384], BF16) for _ in range(2)]
        x_st = [A.alloc([6, 512], BF16) for _ in range(2)]
        dtt = A.alloc([4, 16], F32)
        zpad = A.alloc([8], BF16)
        for kc in range(8):
            load_cast(k, W1[:, kc, :], I["w1"][kc * 128:(kc + 1) * 128, :], W1N, "W1")
        P.dma("sp", gbc, I["g_mix"], [], ["gbc"])
        P.dma("sp", gqk, I["gqk"], [], ["gqk"])
        P.ts("dve", gqk[:, 0:1], gqk[:, 0:1], 128.0 ** -0.5, ALU.mult, ["gqk"], ["gqk"])
        P.dma("sp", dtb, I["dtb"], [], ["dtb"])
        P.memset("dve", zpad, 0.0, ["zpad"])
        for i in range(2):
            P.memset("pool", v_st[i], 0.0, [("v_st", i)])
            P.memset("pool", v_st[i][:, :, :, 128:129], 1.0, [("v_st", i)])
        for cc in range(6):
            P.dma("sp", xbcT_scr[cc, :, 0:4], zpad[:, 0:4], ["zpad"], [("xbcT", cc, -1)])
            P.dma("sp", xbcT_scr[cc, :, S + 4:S + 8], zpad[:, 4:8], ["zpad"], [("xbcT", cc, -2)])
        gen = [3, 4, 5]
        gi = 0
        for b in range(NB):
            pb = b % 2
            P.dma("sp", xt[pb], I["x"][b * 512:(b + 1) * 512, :].rearrange("(t p) d -> p t d", p=128), [], [("xt", pb)])
            for t in range(4):
                norm_transpose(k, xt[pb][:, t, :], ("xt", pb), gbc, c, hT[pb], ("hT", pb), t * 128,
                               (junk, ss[:, t:t + 1], rs[:, t:t + 1], hb[t % 2]), "n%d" % (t % 2), 0, "pT", t % 2)
            hk = ("hT", pb)
            for hd in range(6):
                s2 = hd % 2
                pq = k.bank(1 + s2)
                for kc in range(8):
                    P.mm(pq, W1[:, kc, QK0 + hd * 128:QK0 + (hd + 1) * 128], hT[pb][:, kc, :], kc == 0, kc == 7, ["W1", hk], [("pq", s2)])
                P.act(sq[s2], pq, AF.Square, [("pq", s2)], [("sq", s2)])
                pss = k.bank(6 + s2)
                P.mm(pss, c["ones"], sq[s2], True, True, [("sq", s2), "consts"], [("pss", s2)])
                P.act(lnv[s2], pss, AF.Ln, [("pss", s2), "consts"], [("lnv", s2)], bias=c["eps"], scale=1.0 / 128)
                P.act(lnv[s2], lnv[s2], AF.Exp, [("lnv", s2)], [("lnv", s2)], scale=-0.5)
                P.stt(qk_st[pb][:, hd, :], pq, gqk[:, (hd // 3):(hd // 3) + 1], lnv[s2], ALU.mult, ALU.mult,
                      [("pq", s2), "gqk", ("lnv", s2)], [("qk_st", pb)])
            P.dma("sp", qk_scr[:, :, b * 512:(b + 1) * 512].rearrange("h p t -> p h t"), qk_st[pb], [("qk_st", pb)], [("qk_scr", b)])
            for t in range(4):
                lhs = lambda kc: hT[pb][:, kc, t * 128:(t + 1) * 128]
                g_ = gen[gi % 3]; gi += 1
                pv = k.bank(g_)[:, 0:384]
                for kc in range(8):
                    P.mm(pv, lhs(kc), W1[:, kc, V0:V0 + 384], kc == 0, kc == 7, ["W1", hk], [("pg", g_)])
                P.cp("dve", v_st[pb][:, t, :, 0:128], pv.rearrange("p (g e) -> p g e", g=3), [("pg", g_)], [("v_st", pb)])
                g_ = gen[gi % 3]; gi += 1
                pz = k.bank(g_)
                for kc in range(8):
                    P.mm(pz, lhs(kc), W1[:, kc, Z0:Z0 + 512], kc == 0, kc == 7, ["W1", hk], [("pg", g_)])
                P.act(z_st[pb][:, t, :], pz, AF.Silu, [("pg", g_)], [("z_st", pb)])
                g_ = gen[gi % 3]; gi += 1
                pf = k.bank(g_)[:, 0:384]
                for kc in range(8):
                    P.mm(pf, lhs(kc), W1[:, kc, FI0:FI0 + 384], kc == 0, kc == 7, ["W1", hk], [("pg", g_)])
                P.cp("dve", f_st[pb][:, t, :], pf, [("pg", g_)], [("f_st", pb)])
                g_ = gen[gi % 3]; gi += 1
                pd = k.bank(g_)[:, 0:16]
                for kc in range(8):
                    P.mm(pd, lhs(kc), W1[:, kc, DT0:DT0 + 16], kc == 0, kc == 7, ["W1", hk], [("pg", g_)])
                P.tt("dve", dtt[:, t, :], pd, dtb, ALU.add, [("pg", g_), "dtb"], ["dtt"])
            P.dma("sp", v_scr[b * 512:(b + 1) * 512].rearrange("(t p) g e -> p t g e", p=128), v_st[pb], [("v_st", pb)], [("v_scr", b)])
            P.dma("sp", zs_scr[b * 512:(b + 1) * 512, :].rearrange("(t p) d -> p t d", p=128), z_st[pb], [("z_st", pb)], [("zs_scr", b)])
            P.dma("sp", f_scr[b * 512:(b + 1) * 512, :].rearrange("(t p) d -> p t d", p=128), f_st[pb], [("f_st", pb)], [("f_scr", b)])
            P.dma("sp", dt_dram[b * 512:(b + 1) * 512, :].rearrange("(t p) d -> p t d", p=128), dtt, ["dtt"], [("dt_scr", b)])
            for cc in range(6):
                g_ = gen[gi % 3]; gi += 1
                px = k.bank(g_)
                for kc in range(8):
                    P.mm(px, W1[:, kc, XBC0 + cc * 128:XBC0 + (cc + 1) * 128], hT[pb][:, kc, :], kc == 0, kc == 7, ["W1", hk], [("pg", g_)])
                P.cp("act", x_st[pb][:, cc, :], px, [("pg", g_)], [("x_st", pb)])
            P.dma("sp", xbcT_scr[:, :, 4 + b * 512:4 + (b + 1) * 512].rearrange("h p t -> p h t"), x_st[pb], [("x_st", pb)], [("xbcT", b)])
        P.barrier()

    if 2 in phases:
        A.reset()
        cw = A.alloc([6, 7], F32)
        cb = A.alloc([6], F32)
        U = [A.alloc([518], BF16) for _ in range(2)]
        dg = A.alloc([42, 128], BF16)
        Yc = [A.alloc([512], BF16) for _ in range(2)]
        xs_st = [A.alloc([4, 640], BF16) for _ in range(2)]
        P.dma("sp", cw, I["cw"], [], ["cw"])
        P.dma("sp", cb, I["cb"], [], ["cw"])
        for cc in range(6):
            for kk in range(7):
                P.ts("dve" if kk % 2 else "pool", dg[:, cc * 7 + kk, :], c["ident"], cw[:, cc, kk:kk + 1], ALU.mult, ["consts", "cw"], ["dg"])
        ui = 0
        for b in range(NB):
            pb = b % 2
            for cc in range(6):
                u = ui % 2; ui += 1
                P.dma("sp", U[u], xbcT_scr[cc, :, 1 + b * 512:1 + b * 512 + 518], [], [("U", u)])
                pc = k.bank(2 + u)
                for kk in range(7):
                    P.mm(pc, dg[:, cc * 7 + kk, :], U[u][:, kk:kk + 512], kk == 0, kk == 6, [("U", u), "dg"], [("pc", u)])
                P.act(Yc[u], pc, AF.Silu, [("pc", u), "cw"], [("Yc", u)], bias=cb[:, cc:cc + 1])
                if cc >= 4:
                    P.dma("sp", bcT_scr[cc - 4, :, b * 512:(b + 1) * 512], Yc[u], [("Yc", u)], [("bcT", cc - 4, b)])
                if cc <= 4:
                    p16 = k.bank16(cc % 2)
                    for t in range(4):
                        P.tr(p16[:, t * 128:(t + 1) * 128], Yc[u][:, t * 128:(t + 1) * 128], c["ident"], [("Yc", u), "consts"], [("pt2", cc % 2)])
                    P.cp("act", xs_st[pb][:, :, cc * 128:(cc + 1) * 128], p16[:, 0:512].rearrange("p (t e) -> p t e", t=4), [("pt2", cc % 2)], [("xs_st", pb)])
            P.dma("sp", xsB_scr[b * 512:(b + 1) * 512, :].rearrange("(t p) d -> p t d", p=128), xs_st[pb], [("xs_st", pb)], [("xsB", b)])
        P.barrier()

    if 3 in phases:
        A.reset()
        BT = A.alloc([S], BF16)
        CT = A.alloc([S], BF16)
        dt_sb = A.alloc([NT, 16], F32)
        abc = A.alloc([16], F32)
        dsk = A.alloc([8], F32)
        ngb = A.alloc([512], F32)
        xsB = [A.alloc([640], BF16) for _ in range(2)]
        la = [A.alloc([16], F32) for _ in range(2)]
        lah = [A.alloc([8], BF16) for _ in range(2)]
        lal = [A.alloc([8], BF16) for _ in range(2)]
        lahb = [A.alloc([8, 128], BF16) for _ in range(2)]
        lalb = [A.alloc([8, 128], BF16) for _ in range(2)]
        acs = [A.alloc([16], F32) for _ in range(2)]
        nacs = [A.alloc([8], F32) for _ in range(2)]
        ecs = [A.alloc([8], F32) for _ in range(2)]
        wdec = [A.alloc([8], F32) for _ in range(2)]
        cdec = [A.alloc([8], F32) for _ in range(2)]
        Lt = [A.alloc([8, 128], BF16) for _ in range(2)]
        Mt = [A.alloc([8, 128], BF16) for _ in range(2)]
        xd = [A.alloc([512], BF16) for _ in range(2)]
        xdw = [A.alloc([512], BF16) for _ in range(2)]
        t1 = [A.alloc([512], F32) for _ in range(2)]
        yc = [A.alloc([512], F32) for _ in range(2)]
        H = A.alloc([512], F32)
        Htmp = A.alloc([512], F32)
        Hb = A.alloc([512], BF16)
        ybl = [A.alloc([512], F32) for _ in range(2)]
        zl = [A.alloc([512], BF16) for _ in range(2)]
        yz = [A.alloc([512], F32) for _ in range(2)]
        junk3 = A.alloc([512], BF16)
        ss3 = A.alloc([2], F32)
        ynb = [A.alloc([512], BF16) for _ in range(2)]
        yn_st = [A.alloc([4, 512], BF16) for _ in range(2)]
        P.dma("sp", BT, bcT_scr[0], [], ["BT"])
        P.dma("sp", CT, bcT_scr[1], [], ["CT"])
        P.dma("sp", dt_sb, dt_dram.rearrange("(n p) d -> p n d", p=128), [], ["dt_sb"])
        P.dma("sp", abc, I["alog"], [], ["abc"])
        P.dma("sp", dsk, I["dskip"], [], ["dsk"])
        P.dma("sp", ngb, I["ssd_g"], [], ["ngb"])
        dtf = dt_sb.rearrange("p n d -> p (n d)")
        sp1 = A.alloc([NT * 16], F32)
        P.act(sp1, dtf, AF.Abs, ["dt_sb"], ["sp1"])
        P.act(sp1, sp1, AF.Exp, ["sp1"], ["sp1"], scale=-1.0)
        P.act(sp1, sp1, AF.Ln, ["sp1", "consts"], ["sp1"], bias=c["one"])
        P.ts("dve", dtf, dtf, 0.0, ALU.max, ["dt_sb"], ["dt_sb"])
        P.tt("dve", dtf, dtf, sp1, ALU.add, ["dt_sb", "sp1"], ["dt_sb"])
        P.act(abc, abc, AF.Exp, ["abc"], ["abc"])
        P.ts("dve", abc, abc, -1.0, ALU.mult, ["abc"], ["abc"])
        if dbg is not None and "dtsp_dbg" in dbg:
            dd = nc.dram_tensor("dtsp_dbg", [S, 16], F32, kind="ExternalOutput").ap()
            P.dma("sp", dd.rearrange("(n p) d -> p n d", p=128), dt_sb, ["dt_sb"], ["dd"])

        def ssd_pass(direction):
            d0 = direction * 8
            tri = c["trif"] if direction == 0 else c["trib"]
            msk = c["maskf"] if direction == 0 else c["maskb"]
            order = range(NT) if direction == 0 else range(NT - 1, -1, -1)
            P.memset("dve", H, 0.0, ["H"])
            P.memset("pool", Hb, 0.0, ["Hb"])
            order = list(order)

            def front(it):
                ch = order[it]
                q = it % 2
                cs = slice(ch * 128, (ch + 1) * 128)
                P.dma("sp", xsB[q], xsB_scr[cs, :], [], [("xsB", q)])
                if direction == 0:
                    P.dma("sp", ybl[q], ybwd_scr[cs, :], [("ybwd", ch)], [("ybl", q)])
                    P.dma("sp", zl[q], zs_scr[cs, :], [], [("zl", q)])
                P.tt("dve", la[q][:, 0:8], dt_sb[:, ch, d0:d0 + 8], abc[:, d0:d0 + 8], ALU.mult, ["dt_sb", "abc"], [("la", q)])
                P.cp("dve", lah[q], la[q][:, 0:8], [("la", q)], [("lah", q)])
                P.tt("dve", lal[q], la[q][:, 0:8], lah[q], ALU.subtract, [("la", q), ("lah", q)], [("lal", q)])
                P.cp("act", lahb[q], lah[q].unsqueeze(2).to_broadcast([128, 8, 128]), [("lah", q)], [("lahb", q)])
                P.cp("act", lalb[q], lal[q].unsqueeze(2).to_broadcast([128, 8, 128]), [("lal", q)], [("lalb", q)])
                pa = k.bank(0)
                P.mm(pa[:, 0:8], tri, lah[q], True, False, [("lah", q), "consts"], ["pa"])
                P.mm(pa[:, 0:8], tri, lal[q], False, True, [("lal", q), "consts"], ["pa"])
                P.mm(pa[:, 8:16], c["ones"], lah[q], True, False, [("lah", q), "consts"], ["pa"])
                P.mm(pa[:, 8:16], c["ones"], lal[q], False, True, [("lal", q), "consts"], ["pa"])
                P.cp("dve", acs[q], pa[:, 0:16], ["pa"], [("acs", q)])
                P.ts("dve", nacs[q], acs[q][:, 0:8], -1.0, ALU.mult, [("acs", q)], [("nacs", q)])
                P.act(ecs[q], acs[q][:, 0:8], AF.Exp, [("acs", q)], [("ecs", q)])
                P.tt("dve", wdec[q], acs[q][:, 8:16], acs[q][:, 0:8], ALU.subtract, [("acs", q)], [("wdec", q)])
                P.act(wdec[q], wdec[q], AF.Exp, [("wdec", q)], [("wdec", q)])
                P.act(cdec[q], acs[q][:, 8:16], AF.Exp, [("acs", q)], [("cdec", q)])
                for hb2 in range(2):
                    pk = ("prb", hb2)
                    for h in range(hb2 * 4, hb2 * 4 + 4):
                        pr = k.banks[1 + hb2][:, (h % 4) * 128:(h % 4 + 1) * 128]
                        P.mm(pr, lahb[q][:, h, :], tri, True, False, [("lahb", q), "consts"], [pk])
                        P.mm(pr, lalb[q][:, h, :], tri, False, False, [("lalb", q), "consts"], [pk])
                        P.mm(pr, c["ident"], msk, False, True, ["consts"], [pk])
                    for h in range(hb2 * 4, hb2 * 4 + 4):
                        pr = k.banks[1 + hb2][:, (h % 4) * 128:(h % 4 + 1) * 128]
                        P.act(Lt[q][:, h, :], pr, AF.Exp, [pk, ("nacs", q)], [("Lt", q)], bias=nacs[q][:, h:h + 1])
                pcb = k.bank(3)[:, 0:128]
                P.mm(pcb, BT[:, cs], CT[:, cs], True, True, ["BT", "CT"], ["pcb"])

            def front_b(it):
                ch = order[it]
                q = it % 2
                pcb = k.bank(3)[:, 0:128]
                P.tt("dve", Mt[q], pcb.unsqueeze(1).to_broadcast([128, 8, 128]), Lt[q], ALU.mult, ["pcb", ("Lt", q)], [("Mt", q)])
                xs3 = xsB[q][:, 0:512].rearrange("p (h e) -> p h e", h=8)
                dtb3 = dt_sb[:, ch, d0:d0 + 8].unsqueeze(2).to_broadcast([128, 8, 64])
                P.tt("pool", xd[q].rearrange("p (h e) -> p h e", h=8), xs3, dtb3, ALU.mult, [("xsB", q), "dt_sb"], [("xd", q)])

            def back(it):
                ch = order[it]
                q = it % 2
                cs = slice(ch * 128, (ch + 1) * 128)
                xs3 = xsB[q][:, 0:512].rearrange("p (h e) -> p h e", h=8)
                py = k.bank(4)
                for h in range(8):
                    P.mm(py[:, h * 64:(h + 1) * 64], Mt[q][:, h, :], xd[q][:, h * 64:(h + 1) * 64], True, True, [("Mt", q), ("xd", q)], ["py"])
                pyo = k.bank(5)
                P.mm(pyo, CT[:, cs], Hb, True, True, ["CT", "Hb"], ["pyo"])
                P.tt("dve", t1[q].rearrange("p (h e) -> p h e", h=8), pyo.rearrange("p (h e) -> p h e", h=8),
                     ecs[q].unsqueeze(2).to_broadcast([128, 8, 64]), ALU.mult, ["pyo", ("ecs", q)], [("t1", q)])
                P.tt("dve", yc[q], py, t1[q], ALU.add, ["py", ("t1", q)], [("yc", q)])
                P.tt("pool", xdw[q].rearrange("p (h e) -> p h e", h=8), xd[q].rearrange("p (h e) -> p h e", h=8),
                     wdec[q].unsqueeze(2).to_broadcast([128, 8, 64]), ALU.mult, [("xd", q), ("wdec", q)], [("xdw", q)])
                pst = k.bank(6)
                P.mm(pst, xsB[q][:, 512:640], xdw[q], True, True, [("xsB", q), ("xdw", q)], ["pst"])
                P.tt("pool", Htmp.rearrange("p (h e) -> p h e", h=8), H.rearrange("p (h e) -> p h e", h=8),
                     cdec[q].unsqueeze(2).to_broadcast([128, 8, 64]), ALU.mult, ["H", ("cdec", q)], ["Htmp"])
                P.tt("dve", H, pst, Htmp, ALU.add, ["pst", "Htmp"], ["H"])
                P.cp("pool", Hb, H, ["H"], ["Hb"])
                if direction == 1:
                    P.dma("sp", ybwd_scr[cs, :], yc[q], [("yc", q)], [("ybwd", ch)])
                else:
                    P.tt("pool", yc[q], yc[q], ybl[q], ALU.add, [("yc", q), ("ybl", q)], [("yc", q)])
                    P.tt("pool", yz[q].rearrange("p (h e) -> p h e", h=8), xs3, dsk.unsqueeze(2).to_broadcast([128, 8, 64]), ALU.mult,
                         [("xsB", q), "dsk"], [("yz", q)])
                    P.tt("pool", yz[q], yz[q], yc[q], ALU.add, [("yz", q), ("yc", q)], [("yz", q)])
                    P.tt("dve", yz[q], yz[q], zl[q], ALU.mult, [("yz", q), ("zl", q)], [("yz", q)])
                    P.act(junk3, yz[q], AF.Square, [("yz", q)], ["junk3", ("ss3", q)], accum=ss3[:, q:q + 1])
                    P.act(ss3[:, q:q + 1], ss3[:, q:q + 1], AF.Sqrt, [("ss3", q), "consts"], [("ss3", q)], bias=c["eps"], scale=1.0 / 512)
                    P.recip(ss3[:, q:q + 1], ss3[:, q:q + 1], [("ss3", q)], [("ss3", q)])
                    P.stt(ynb[q], yz[q], ss3[:, q:q + 1], ngb, ALU.mult, ALU.mult, [("yz", q), ("ss3", q), "ngb"], [("ynb", q)])
                    p16 = k.bank16(7)
                    for cc in range(4):
                        P.tr(p16[:, cc * 128:(cc + 1) * 128], ynb[q][:, cc * 128:(cc + 1) * 128], c["ident"], [("ynb", q), "consts"], ["pt3"])
                    sq_ = (ch // 4) % 2
                    P.cp("act", yn_st[sq_][:, :, (ch % 4) * 128:(ch % 4 + 1) * 128], p16[:, 0:512].rearrange("p (c t) -> p c t", c=4), ["pt3"], [("yn_st", sq_)])
                    if ch % 4 == 3:
                        bb = ch // 4
                        P.dma("sp", ynT_scr[:, :, bb * 512:(bb + 1) * 512].rearrange("c p t -> p c t"), yn_st[sq_], [("yn_st", sq_)], [("ynT", bb)])

            front(0)
            front_b(0)
            for it in range(NT):
                if it + 1 < NT:
                    front(it + 1)
                back(it)
                if it + 1 < NT:
                    front_b(it + 1)

        ssd_pass(1)
        ssd_pass(0)
        P.barrier()

    if 4 in phases:
        A.reset()
        QT = A.alloc([S], BF16)
        KT = A.alloc([S], BF16)
        Vc = A.alloc([NT, 132], BF16)
        biasT = A.alloc([9, 128], BF16)
        Pt = [A.alloc([3, 128], BF16) for _ in range(2)]
        o_st = [A.alloc([8, 132], F32) for _ in range(2)]
        un = [A.alloc([3, 132], F32) for _ in range(2)]
        rden = [A.alloc([1], F32) for _ in range(2)]
        ob = [A.alloc([128], BF16) for _ in range(2)]
        oT = A.alloc([S], BF16)
        load_cast(k, biasT.rearrange("p a b -> p (a b)"), I["biasT"], 9 * 128, "biasT")
        flush = 0
        for g, dil in enumerate((1, 4, 16)):
            sub = S // dil
            ntile = sub // 128
            P.dma("sp", QT, qk_scr[g], [], ["QT"])
            P.dma("sp", KT, qk_scr[3 + g], [], ["KT"])
            vsrc = v_scr[:, g, :].rearrange("(m i r) c -> i r m c", r=dil, i=128)
            for r in range(dil):
                P.dma("sp", Vc[:, r * ntile:(r + 1) * ntile, :], vsrc[:, r, :, :], [], ["Vc"])
            tiles = [(r, m) for r in range(dil) for m in range(ntile)]
            gsz = min(8, ntile)

            def s_part(i):
                r, m = tiles[i]
                q = i % 2
                kts = [mm_ for mm_ in (m - 1, m, m + 1) if 0 <= mm_ < ntile]
                psb = k.bank(q)
                qs = QT[:, r + dil * 128 * m: r + dil * 128 * m + dil * 127 + 1: dil]
                for j, m2 in enumerate(kts):
                    ks = KT[:, r + dil * 128 * m2: r + dil * 128 * m2 + dil * 127 + 1: dil]
                    P.mm(psb[:, j * 128:(j + 1) * 128], ks, qs, True, False, ["QT", "KT"], [("ps4", q)])
                    P.mm(psb[:, j * 128:(j + 1) * 128], c["ident"], biasT[:, g * 3 + (m2 - m + 1), :], False, True, ["biasT", "consts"], [("ps4", q)])
                n = len(kts)
                P.act(Pt[q][:, 0:n, :], psb[:, 0:n * 128].rearrange("p (a b) -> p a b", a=n), AF.Exp, [("ps4", q)], [("Pt", q)])

            def pv_part(i):
                nonlocal flush
                r, m = tiles[i]
                q = i % 2
                kts = [mm_ for mm_ in (m - 1, m, m + 1) if 0 <= mm_ < ntile]
                n = len(kts)
                po = k.bank(2 + q)[:, 0:132]
                for j, m2 in enumerate(kts):
                    P.mm(po, Pt[q][:, j, :], Vc[:, r * ntile + m2, :], j == 0, j == n - 1, [("Pt", q), "Vc"], [("po4", q)])
                sq_ = flush % 2
                P.cp("dve", o_st[sq_][:, m % gsz, :], po, [("po4", q)], [("o_st", sq_)])
                if m % gsz == gsz - 1:
                    m0 = m - (gsz - 1)
                    dst = un_scr[g].rearrange("(m i r) c -> i r m c", r=dil, i=128)[:, r, m0:m0 + gsz, :]
                    P.dma("sp", dst, o_st[sq_][:, 0:gsz, :], [("o_st", sq_)], [("un", flush)])
                    flush += 1

            for i in range(len(tiles) + 1):
                if i < len(tiles):
                    s_part(i)
                if i >= 1:
                    pv_part(i - 1)
        P.barrier()
        for t in range(NT):
            q = t % 2
            P.dma("sp", un[q], un_scr[:, t * 128:(t + 1) * 128, :].rearrange("g p c -> p g c"),
                  [], [("unl", q)])
            P.tt("dve", un[q][:, 0, :], un[q][:, 0, :], un[q][:, 1, :], ALU.add, [("unl", q)], [("unl", q)])
            P.tt("dve", un[q][:, 0, :], un[q][:, 0, :], un[q][:, 2, :], ALU.add, [("unl", q)], [("unl", q)])
            P.recip(rden[q], un[q][:, 0, 128:129], [("unl", q)], [("rden", q)])
            P.ts("dve", ob[q], un[q][:, 0, 0:128], rden[q], ALU.mult, [("unl", q), ("rden", q)], [("ob", q)])
            p16 = k.bank16(4 + q)
            P.tr(p16[:, 0:128], ob[q], c["ident"], [("ob", q), "consts"], [("pt4", q)])
            P.cp("act", oT[:, t * 128:(t + 1) * 128], p16[:, 0:128], [("pt4", q)], ["oT"])
        P.dma("sp", oT_scr, oT, ["oT"], ["oT_scr"])
        P.barrier()

    if 5 in phases:
        A.reset()
        E = A.alloc([128, 128], BF16, parts=64)
        T2 = A.alloc([2, 256], BF16)
        Fb = [A.alloc([16, 384], BF16, parts=64) for _ in range(2)]
        Zst = [A.alloc([4, 384], BF16) for _ in range(2)]
        Zl = [A.alloc([2, 384], BF16) for _ in range(2)]
        XT = A.alloc([6, S], BF16)
        for n2 in range(0, 128, 16):
            load_cast(k, E[:, n2:n2 + 16, :].rearrange("p a b -> p (a b)"), I["E1"][:, n2 * 128:(n2 + 16) * 128], 2048, "E")
        load_cast(k, T2.rearrange("p a b -> p (a b)"), I["T2"], 512, "T2")
        fsrc = f_scr.rearrange("(n1 n2) c -> n1 n2 c", n2=128)
        zi = 0
        for nb in range(8):
            q = nb % 2
            P.dma("sp", Fb[q], fsrc[:, nb * 16:(nb + 1) * 16, :], [], [("Fb", q)])
            for i in range(16):
                n2 = nb * 16 + i
                pz_ = k.bank(n2 % 2)[:, 0:384]
                P.mm(pz_, E[:, n2, :], Fb[q][:, i, :], True, True, ["E", ("Fb", q)], [("pz5", n2 % 2)])
                zq = (n2 // 4) % 2
                P.cp("act" if n2 % 2 else "dve", Zst[zq][:, n2 % 4, :], pz_, [("pz5", n2 % 2)], [("Zst", zq)])
                if n2 % 4 == 3:
                    P.dma("sp", z_scr[:, n2 - 3:n2 + 1, :], Zst[zq], [("Zst", zq)], [("z_scr", n2 // 4)])
        zall = [("z_scr", i) for i in range(32)]
        for k1 in range(64):
            q = k1 % 2
            P.dma("sp", Zl[q], z_scr[k1:k1 + 65:64].rearrange("r n c -> n r c"), zall, [("Zl", q)])
            for ch3 in range(3):
                px_ = k.bank(2 + (k1 * 3 + ch3) % 2)[:, 0:256]
                pk_ = ("px5", (k1 * 3 + ch3) % 2)
                P.mm(px_, Zl[q][:, 0, ch3 * 128:(ch3 + 1) * 128], T2[:, 0, :], True, False, [("Zl", q), "T2"], [pk_])
                P.mm(px_, Zl[q][:, 1, ch3 * 128:(ch3 + 1) * 128], T2[:, 1, :], False, True, [("Zl", q), "T2"], [pk_])
                for ri in range(2):
                    P.cp("act" if ri else "dve", XT[:, ri * 3 + ch3, k1:k1 + 64 * 127 + 1:64], px_[:, ri * 128:(ri + 1) * 128], [pk_], ["XT"])
        P.dma("sp", xT_scr.rearrange("c p t -> p c t"), XT, ["XT"], ["xT_scr"])
        P.barrier()

    if 6 in phases:
        A.reset()
        BW = 256
        TPB = BW // 128
        Wg = A.alloc([8, 3072], BF16) if gmode != "load" else None
        Wba = A.alloc([D], BF16)
        Wbs = A.alloc([4, D], BF16)
        Wbf = A.alloc([6, D], BF16)
        Wo = A.alloc([8, D], BF16)
        T256 = A.alloc([3, 2, 256], BF16)
        gbc = A.alloc([D], F32)
        gb = A.alloc([24], F32)
        xt = [A.alloc([2, D], F32) for _ in range(2)]
        hT = [A.alloc([8, BW], BF16) for _ in range(2)]
        junk = A.alloc([D], BF16)
        ss = A.alloc([8], F32)
        rs = A.alloc([8], F32)
        hb = [A.alloc([D], BF16) for _ in range(2)]
        Gs = [A.alloc([24, BW], BF16) for _ in range(2 if gmode == "load" else 1)]
        oTb = [A.alloc([BW], BF16) for _ in range(2)]
        ynTb = [A.alloc([4, BW], BF16) for _ in range(2)]
        xTb = [A.alloc([6, BW], BF16) for _ in range(2)]
        fo = A.alloc([6, BW], BF16)
        ta = [A.alloc([BW], F32) for _ in range(2)]
        tb_ = [A.alloc([BW], F32) for _ in range(2)]
        tc_ = [A.alloc([BW], F32) for _ in range(2)]
        mT = A.alloc([8, BW], BF16)
        osb = [A.alloc([D], F32) for _ in range(2)]
        for kc in range(8):
            if gmode != "load":
                load_cast(k, Wg[:, kc, :], I["wg"][kc * 128:(kc + 1) * 128, :], 3072, "Wg")
            load_cast(k, Wo[:, kc, :], I["wo"][kc * 128:(kc + 1) * 128, :], D, "Wo")
        load_cast(k, Wba, I["wba"], D, "Wb")
        for kc in range(4):
            load_cast(k, Wbs[:, kc, :], I["wbs"][kc * 128:(kc + 1) * 128, :], D, "Wb")
        for kc in range(6):
            load_cast(k, Wbf[:, kc, :], I["wbf"][kc * 128:(kc + 1) * 128, :], D, "Wb")
        load_cast(k, T256.rearrange("p a b c -> p (a b c)"), I["T256"], 1536, "T256")
        P.dma("sp", gbc, I["g_mix"], [], ["gbc"])
        P.dma("sp", gb, I["gate_b"], [], ["gb"])
        gen = [1, 2]
        gi = 0
        for b in range(S // BW):
            pb = b % 2
            bs = slice(b * BW, (b + 1) * BW)
            G = Gs[pb % len(Gs)]
            gk_ = ("G", pb % len(Gs))
            P.dma("sp", oTb[pb], oT_scr[:, bs], [], [("oTb", pb)])
            P.dma("sp", ynTb[pb], ynT_scr[:, :, bs].rearrange("c p t -> p c t"), [], [("ynTb", pb)])
            P.dma("sp", xTb[pb], xT_scr[:, :, bs].rearrange("c p t -> p c t"), [], [("xTb", pb)])
            if gmode == "load":
                P.dma("sp", G, g_scr[:, :, bs].rearrange("c p t -> p c t"), [], [gk_])
            else:
                P.dma("sp", xt[pb], I["x"][bs, :].rearrange("(t p) d -> p t d", p=128), [], [("xt", pb)])
                for t in range(TPB):
                    norm_transpose(k, xt[pb][:, t, :], ("xt", pb), gbc, c, hT[pb], ("hT", pb), t * 128,
                                   (junk, ss[:, t:t + 1], rs[:, t:t + 1], hb[t % 2]), "n%d" % (t % 2), 0, "pT", t % 2)
                hk = ("hT", pb)
                for gc in range(24):
                    g_ = gen[gi % 2]; gi += 1
                    pg = k.bank(g_)[:, 0:BW]
                    for kc in range(8):
                        P.mm(pg, Wg[:, kc, gc * 128:(gc + 1) * 128], hT[pb][:, kc, :], kc == 0, kc == 7, ["Wg", hk], [("pg", g_)])
                    P.act(G[:, gc, :], pg, AF.Sigmoid, [("pg", g_), "gb"], [gk_], bias=gb[:, gc:gc + 1])
                if gmode == "store":
                    P.dma("sp", g_scr[:, :, bs].rearrange("c p t -> p c t"), G, [gk_], [("g_scr", b)])
            for i3 in range(3):
                for kcc in range(2):
                    g_ = gen[gi % 2]; gi += 1
                    pf_ = k.bank(g_)[:, 0:BW]
                    P.mm(pf_, T256[:, i3, 0, kcc * 128:(kcc + 1) * 128], xTb[pb][:, i3, :], True, False, ["T256", ("xTb", pb)], [("pg", g_)])
                    P.mm(pf_, T256[:, i3, 1, kcc * 128:(kcc + 1) * 128], xTb[pb][:, 3 + i3, :], False, True, ["T256", ("xTb", pb)], [("pg", g_)])
                    P.cp("dve", fo[:, i3 * 2 + kcc, :], pf_, [("pg", g_)], ["fo"])
            for dc in range(8):
                q = dc % 2
                ds_ = slice(dc * 128, (dc + 1) * 128)
                pya, pys, pyf = k.bank(3)[:, 0:BW], k.bank(4)[:, 0:BW], k.bank(5)[:, 0:BW]
                P.mm(pya, Wba[:, ds_], oTb[pb], True, True, ["Wb", ("oTb", pb)], ["pya"])
                for kc in range(4):
                    P.mm(pys, Wbs[:, kc, ds_], ynTb[pb][:, kc, :], kc == 0, kc == 3, ["Wb", ("ynTb", pb)], ["pys"])
                for kc in range(6):
                    P.mm(pyf, Wbf[:, kc, ds_], fo[:, kc, :], kc == 0, kc == 5, ["Wb", "fo"], ["pyf"])
                P.tt("dve", ta[q], pya, G[:, dc, :], ALU.mult, ["pya", gk_], [("ta", q)])
                P.tt("dve", tb_[q], pys, G[:, 8 + dc, :], ALU.mult, ["pys", gk_], [("tb", q)])
                P.tt("dve", tc_[q], pyf, G[:, 16 + dc, :], ALU.mult, ["pyf", gk_], [("tc", q)])
                P.tt("pool", ta[q], ta[q], tb_[q], ALU.add, [("ta", q), ("tb", q)], [("ta", q)])
                P.tt("pool", mT[:, dc, :], ta[q], tc_[q], ALU.add, [("ta", q), ("tc", q)], ["mT"])
            for t in range(TPB):
                q = t % 2
                for half in range(2):
                    po_ = k.bank(6 + half)
                    for kc in range(8):
                        P.mm(po_, mT[:, kc, t * 128:(t + 1) * 128], Wo[:, kc, half * 512:(half + 1) * 512], kc == 0, kc == 7, ["mT", "Wo"], [("po6", half)])
                    P.cp("act" if half else "dve", osb[q][:, half * 512:(half + 1) * 512], po_, [("po6", half)], [("osb", q)])
                P.dma("sp", O[b * BW + t * 128:b * BW + (t + 1) * 128, :], osb[q], [("osb", q)], [("O", b, t)])
        P.barrier()


TS = 2048
TT = TS // 128


def rms_feature_major(k, c, praw, keys_raw, g2, gcol0, out3, okey, sqb, lnb, tag, width, pss_bank, inv_n):
    P = k.P
    for ec in range(2):
        P.act(sqb[ec], praw[ec], AF.Square, [keys_raw[ec]], [(tag, "sq", ec)])
    pss = k.bank(pss_bank)[:, 0:width]
    for ec in range(2):
        P.mm(pss, c["ones"], sqb[ec], ec == 0, ec == 1, [(tag, "sq", ec), "consts"], [(tag, "pss")])
    P.act(lnb, pss, AF.Ln, [(tag, "pss"), "consts"], [(tag, "ln")], bias=c["eps"], scale=inv_n)
    P.act(lnb, lnb, AF.Exp, [(tag, "ln")], [(tag, "ln")], scale=-0.5)
    for ec in range(2):
        P.stt(out3[:, ec, :], praw[ec], g2[:, gcol0 + ec:gcol0 + ec + 1], lnb, ALU.mult, ALU.mult,
              [keys_raw[ec], "gx", (tag, "ln")], [okey])


def build_tok(k, I, O, c, parts_aps):
    nc, P, A = k.nc, k.P, k.A
    A.reset()
    base0 = A.base
    xr = A.alloc([TT, D], F32)
    kTn = A.alloc([4, 2, 256], BF16)
    vext = A.alloc([2, 4, 260], BF16)
    gx = A.alloc([4], F32)
    A.persist()
    P.dma("sp", gx, I["gxqk"], [], ["gx"])
    P.ts("dve", gx[:, 0:2], gx[:, 0:2], 1.0 / 16.0, ALU.mult, ["gx"], ["gx"])
    ptmp = [A.alloc([4, D], F32) for _ in range(2)]
    P.dma("sp", xr, I["xres"].rearrange("(t p) d -> p t d", p=128), [], [("xr", t) for t in range(TT)])
    ci = 0
    for pa in parts_aps:
        for t4 in range(TT // 4):
            q = ci % 2
            P.dma("sp", ptmp[q], pa[t4 * 512:(t4 + 1) * 512, :].rearrange("(t p) d -> p t d", p=128), [], [("ptmp", q)])
            kk = [("xr", t) for t in range(t4 * 4, t4 * 4 + 4)]
            P.tt("dve" if ci % 2 else "pool", xr[:, t4 * 4:(t4 + 1) * 4, :], xr[:, t4 * 4:(t4 + 1) * 4, :], ptmp[q], ALU.add, kk + [("ptmp", q)], kk)
            ci += 1
    P.barrier()
    A.reset()
    Wk = A.alloc([8, D], BF16)
    Wv = A.alloc([8, D], BF16)
    gm = A.alloc([D], F32)
    mt = A.alloc([2, D], F32)
    mT = A.alloc([8, 256], BF16)
    junk = A.alloc([D], BF16)
    ss = A.alloc([8], F32)
    rs = A.alloc([8], F32)
    hb = [A.alloc([D], BF16) for _ in range(2)]
    sqb = [A.alloc([512], BF16) for _ in range(2)]
    lnb = A.alloc([512], F32)
    for kc in range(8):
        load_cast(k, Wk[:, kc, :], I["wxk"][kc * 128:(kc + 1) * 128, :], D, "Wk")
        load_cast(k, Wv[:, kc, :], I["wxv"][kc * 128:(kc + 1) * 128, :], D, "Wv")
    P.dma("sp", gm, I["g_m"], [], ["gbc"])
    P.dma("sp", mt, I["mem"].rearrange("(t p) d -> p t d", p=128), [], ["mt"])
    P.memset("pool", vext, 0.0, ["vext"])
    P.memset("pool", vext[:, :, :, 256:257], 1.0, ["vext"])
    for t in range(2):
        norm_transpose(k, mt[:, t, :], "mt", gm, c, mT, "mT", t * 128,
                       (junk, ss[:, t:t + 1], rs[:, t:t + 1], hb[t % 2]), "n%d" % (t % 2), 0, "pT", t % 2)
    for hh in range(4):
        praw = []
        for ec in range(2):
            pk_ = k.bank(1 + ec)[:, 0:256]
            for kc in range(8):
                P.mm(pk_, Wk[:, kc, hh * 256 + ec * 128:hh * 256 + (ec + 1) * 128], mT[:, kc, :], kc == 0, kc == 7, ["Wk", "mT"], [("praw", ec)])
            praw.append(pk_)
        rms_feature_major(k, c, praw, [("praw", 0), ("praw", 1)], gx, 2, kTn[:, hh, :, :], "kTn",
                          [sqb[0][:, 0:256], sqb[1][:, 0:256]], lnb[:, 0:256], "k", 256, 3, 1.0 / 256)
    for mtile in range(2):
        for half in range(2):
            pv = k.bank(4 + half)
            for kc in range(8):
                P.mm(pv, mT[:, kc, mtile * 128:(mtile + 1) * 128], Wv[:, kc, half * 512:(half + 1) * 512], kc == 0, kc == 7, ["Wv", "mT"], [("pv", half)])
            P.cp("dve", vext[:, mtile, half * 2:half * 2 + 2, 0:256], pv.rearrange("p (h e) -> p h e", h=2), [("pv", half)], ["vext"])
    P.barrier()
    A.reset()
    Wq = A.alloc([8, D], BF16)
    Wo = A.alloc([8, D], BF16)
    gxb = A.alloc([D], F32)
    hT = A.alloc([8, 512], BF16)
    junk = A.alloc([D], BF16)
    ss = A.alloc([8], F32)
    rs = A.alloc([8], F32)
    hb = [A.alloc([D], BF16) for _ in range(2)]
    sqb = [A.alloc([512], BF16) for _ in range(2)]
    lnb = A.alloc([512], F32)
    qTn = A.alloc([2, 512], BF16)
    PT = A.alloc([2, 512], BF16)
    rden = [A.alloc([1], F32) for _ in range(2)]
    osb = A.alloc([4, D], BF16)
    oT = A.alloc([8, 512], BF16)
    for kc in range(8):
        load_cast(k, Wq[:, kc, :], I["wxq"][kc * 128:(kc + 1) * 128, :], D, "Wq")
        load_cast(k, Wo[:, kc, :], I["wxo"][kc * 128:(kc + 1) * 128, :], D, "Wo")
    P.dma("sp", gxb, I["g_x"], [], ["gbc"])
    for b in range(TS // 512):
        for t in range(4):
            norm_transpose(k, xr[:, b * 4 + t, :], ("xr", b * 4 + t), gxb, c, hT, "hT", t * 128,
                           (junk, ss[:, t:t + 1], rs[:, t:t + 1], hb[t % 2]), "n%d" % (t % 2), 0, "pT", t % 2)
        for hh in range(4):
            praw = []
            for ec in range(2):
                pq = k.bank(1 + ec)
                for kc in range(8):
                    P.mm(pq, Wq[:, kc, hh * 256 + ec * 128:hh * 256 + (ec + 1) * 128], hT[:, kc, :], kc == 0, kc == 7, ["Wq", "hT"], [("praw", ec)])
                praw.append(pq)
            rms_feature_major(k, c, praw, [("praw", 0), ("praw", 1)], gx, 0, qTn, "qTn", sqb, lnb, "q", 512, 3, 1.0 / 256)
            for mtile in range(2):
                pl = k.bank(4 + mtile)
                for ec in range(2):
                    P.mm(pl, kTn[:, hh, ec, mtile * 128:(mtile + 1) * 128], qTn[:, ec, :], ec == 0, ec == 1, ["kTn", "qTn"], [("pl", mtile)])
                P.act(PT[:, mtile, :], pl, AF.Exp, [("pl", mtile)], [("PT", mtile)])
            for t in range(4):
                q = t % 2
                po = k.bank(6 + q)[:, 0:257]
                for mtile in range(2):
                    P.mm(po, PT[:, mtile, t * 128:(t + 1) * 128], vext[:, mtile, hh, 0:257], mtile == 0, mtile == 1, [("PT", mtile), "vext"], [("po", q)])
                P.recip(rden[q], po[:, 256:257], [("po", q)], [("rden", q)])
                P.ts("dve", osb[:, t, hh * 256:(hh + 1) * 256], po[:, 0:256], rden[q], ALU.mult, [("po", q), ("rden", q)], [("osb", t)])
        for t in range(4):
            p16 = k.bank16(0)
            for kc in range(8):
                P.tr(p16[:, kc * 128:(kc + 1) * 128], osb[:, t, kc * 128:(kc + 1) * 128], c["ident"], [("osb", t), "consts"], ["pT"])
            P.cp("act" if t % 2 else "dve", oT[:, :, t * 128:(t + 1) * 128], p16.rearrange("p (a b) -> p a b", a=8), ["pT"], ["oT"])
        for t in range(4):
            for half in range(2):
                po2 = k.bank(1 + half)
                for kc in range(8):
                    P.mm(po2, oT[:, kc, t * 128:(t + 1) * 128], Wo[:, kc, half * 512:(half + 1) * 512], kc == 0, kc == 7, ["oT", "Wo"], [("praw", half)])
                xs_ = xr[:, b * 4 + t, half * 512:(half + 1) * 512]
                P.tt("dve", xs_, po2, xs_, ALU.add, [("praw", half), ("xr", b * 4 + t)], [("xr", b * 4 + t)])
    P.barrier()
    A.reset()
    Wd = A.alloc([22, D], BF16)
    gfb = A.alloc([D], F32)
    hT = A.alloc([8, 512], BF16)
    junk = A.alloc([D], BF16)
    ss = A.alloc([8], F32)
    rs = A.alloc([8], F32)
    hb = [A.alloc([D], BF16) for _ in range(2)]
    Wgc = [A.alloc([8, 128], BF16) for _ in range(2)]
    Wuc = [A.alloc([8, 128], BF16) for _ in range(2)]
    sg = [A.alloc([512], F32) for _ in range(2)]
    actT = A.alloc([22, 512], BF16)
    xo = [A.alloc([D], F32) for _ in range(2)]
    for fc in range(22):
        load_cast(k, Wd[:, fc, :], I["wfd"][fc * 128:(fc + 1) * 128, :], D, "Wd")
    P.dma("sp", gfb, I["g_f"], [], ["gbc"])
    wi = 0
    for b in range(TS // 512):
        for t in range(4):
            norm_transpose(k, xr[:, b * 4 + t, :], ("xr", b * 4 + t), gfb, c, hT, "hT", t * 128,
                           (junk, ss[:, t:t + 1], rs[:, t:t + 1], hb[t % 2]), "n%d" % (t % 2), 0, "pT", t % 2)
        for fc in range(22):
            q = wi % 2; wi += 1
            P.dma("pool", Wgc[q], I["wfg"][:, fc * 128:(fc + 1) * 128].rearrange("(kc p) c -> p kc c", p=128), [], [("Wgc", q)])
            P.dma("pool", Wuc[q], I["wfu"][:, fc * 128:(fc + 1) * 128].rearrange("(kc p) c -> p kc c", p=128), [], [("Wuc", q)])
            pg = k.bank(1 + q)
            pu = k.bank(3 + q)
            for kc in range(8):
                P.mm(pg, Wgc[q][:, kc, :], hT[:, kc, :], kc == 0, kc == 7, [("Wgc", q), "hT"], [("pgf", q)])
            for kc in range(8):
                P.mm(pu, Wuc[q][:, kc, :], hT[:, kc, :], kc == 0, kc == 7, [("Wuc", q), "hT"], [("puf", q)])
            P.act(sg[q], pg, AF.Silu, [("pgf", q)], [("sg", q)])
            P.tt("dve", actT[:, fc, :], pu, sg[q], ALU.mult, [("puf", q), ("sg", q)], ["actT"])
        for t in range(4):
            q = t % 2
            for half in range(2):
                pd = k.bank(5 + half)
                for fc in range(22):
                    P.mm(pd, actT[:, fc, t * 128:(t + 1) * 128], Wd[:, fc, half * 512:(half + 1) * 512], fc == 0, fc == 21, ["actT", "Wd"], [("pd", half)])
                P.tt("dve", xo[q][:, half * 512:(half + 1) * 512], pd, xr[:, b * 4 + t, half * 512:(half + 1) * 512], ALU.add,
                     [("pd", half), ("xr", b * 4 + t)], [("xo", q)])
            P.dma("sp", O[(b * 4 + t) * 128:(b * 4 + t + 1) * 128, :], xo[q], [("xo", q)], [("O", b, t)])
    P.barrier()
    A.base = base0


def tok_inputs(inp, l, b, j, xres, parts):
    d = {}
    d["xres"] = np.ascontiguousarray(xres)
    if parts is not None:
        d["parts"] = np.ascontiguousarray(parts)
    d["mem"] = np.ascontiguousarray(inp["mem"][b])
    d["g_x"] = rep(inp["xattn_norm_g"][l])
    d["g_m"] = rep(inp["mem_norm_g"][l])
    d["g_f"] = rep(inp["ffn_norm_g"][l])
    gq = inp["xattn_q_norm_g"][l].reshape(2, 128).T
    gk = inp["xattn_k_norm_g"][l].reshape(2, 128).T
    d["gxqk"] = np.ascontiguousarray(np.concatenate([gq, gk], axis=1).astype(np.float32))
    for nm, key in (("wxq", "w_xq"), ("wxk", "w_xk"), ("wxv", "w_xv"), ("wxo", "w_xo"), ("wfg", "w_ffn_gate"), ("wfu", "w_ffn_up"), ("wfd", "w_ffn_down")):
        d[nm] = np.ascontiguousarray(inp[key][l])
    hc = host_consts()
    for nm in ("ident", "trif", "trib", "maskf", "maskb"):
        d[nm] = hc[nm]
    return d


TOK_IN = dict(xres=[TS, D], parts=[4, TS, D], mem=[256, D], g_x=[128, D], g_m=[128, D], g_f=[128, D], gxqk=[128, 4],
              wxq=[D, D], wxk=[D, D], wxv=[D, D], wxo=[D, D], wfg=[D, DFF], wfu=[D, DFF], wfd=[DFF, D],
              ident=[128, 128], trif=[128, 128], trib=[128, 128], maskf=[128, 128], maskb=[128, 128])


def build_tok_program():
    nc = bass.Bass("TRN2", target_bir_lowering=False)
    I = {n: nc.dram_tensor(n, shp, F32, kind="ExternalInput").ap() for n, shp in TOK_IN.items()}
    O = nc.dram_tensor("xout", [TS, D], F32, kind="ExternalOutput").ap()
    k = K(nc)
    c = mk_consts(k, I)
    k.A.persist()
    build_tok(k, I, O, c, [I["parts"][i] for i in range(4)])
    info = k.P.emit()
    return nc, info


def t5_bucket(rel):
    half_b, exact = 16, 8
    dist = np.abs(rel)
    log_ratio = np.log(np.maximum(dist, 1) / exact) / np.log(1024 / exact)
    far = np.minimum(exact + (log_ratio * (half_b - exact)).astype(np.int32), half_b - 1)
    return np.where(rel > 0, half_b, 0) + np.where(dist < exact, dist, far)


def host_consts():
    i = np.arange(128)
    c = {}
    c["ident"] = np.eye(128, dtype=np.float32)
    c["trif"] = (i[:, None] <= i[None, :]).astype(np.float32)
    c["trib"] = (i[:, None] >= i[None, :]).astype(np.float32)
    c["maskf"] = np.where(i[:, None] <= i[None, :], 0.0, NEG).astype(np.float32)
    c["maskb"] = np.where(i[:, None] >= i[None, :], 0.0, NEG).astype(np.float32)
    n1 = np.arange(64)[:, None, None]
    n2 = np.arange(128)[None, :, None]
    k1 = np.arange(64)[None, None, :]
    ang = 2 * np.pi * ((k1 * (128 * n1 + n2)) % 8192) / 8192.0
    E = np.concatenate([np.cos(ang), -np.sin(ang)], axis=2) / 8.0
    c["E1"] = E.reshape(64, 128 * 128).astype(np.float32)
    a2 = 2 * np.pi * ((i[:, None] * i[None, :]) % 128) / 128.0
    C2, S2 = np.cos(a2) / math.sqrt(128), np.sin(a2) / math.sqrt(128)
    c["T2"] = np.concatenate([C2, -S2, S2, C2], axis=1).astype(np.float32)
    return c


def core_consts(j):
    kc = np.arange(256)[None, :]
    out = np.zeros((128, 3, 2, 256), np.float32)
    for i3 in range(3):
        hg = 3 * j + i3
        cidx = (hg % 2) * 128 + np.arange(128)[:, None]
        ang = 2 * np.pi * ((cidx * kc) % 256) / 256.0
        out[:, i3, 0, :] = np.cos(ang) / 16.0
        out[:, i3, 1, :] = np.sin(ang) / 16.0
    return out.reshape(128, 1536)


def rep(v, n=128):
    return np.ascontiguousarray(np.broadcast_to(np.asarray(v, np.float32).reshape(1, -1), (n, np.asarray(v).size)))


def mixer_inputs(inp, l, j, xb):
    w_in = inp["w_in"][l]
    cols = []
    for base in (0, 1536):
        for g in range(3):
            cols.append(np.arange(base + (g * 4 + j) * 128, base + (g * 4 + j + 1) * 128))
    for g in range(3):
        cols.append(np.arange(3072 + (g * 4 + j) * 128, 3072 + (g * 4 + j + 1) * 128))
    cols.append(np.arange(4608 + j * 512, 4608 + (j + 1) * 512))
    cols.append(np.arange(6656 + j * 512, 6656 + (j + 1) * 512))
    cols.append(np.arange(6656 + 2048 + j * 128, 6656 + 2048 + (j + 1) * 128))
    cols.append(np.arange(6656 + 2560 + j * 128, 6656 + 2560 + (j + 1) * 128))
    cols.append(np.arange(9728 + j * 8, 9728 + (j + 1) * 8))
    cols.append(np.arange(9728 + 32 + j * 8, 9728 + 32 + (j + 1) * 8))
    cols.append(np.arange(9792 + j * 384, 9792 + (j + 1) * 384))
    cols = np.concatenate(cols)
    assert cols.size == W1N
    d = {}
    d["x"] = xb
    d["w1"] = np.ascontiguousarray(w_in[:, cols])
    d["wg"] = np.ascontiguousarray(w_in[:, 11328:14400])
    d["g_mix"] = rep(inp["mix_norm_g"][l])
    d["gqk"] = np.ascontiguousarray(np.stack([inp["attn_q_norm_g"][l], inp["attn_k_norm_g"][l]], axis=1).astype(np.float32))
    d["dtb"] = rep(np.concatenate([inp["dt_bias"][l][0, j * 8:(j + 1) * 8], inp["dt_bias"][l][1, j * 8:(j + 1) * 8]]))
    d["alog"] = rep(np.concatenate([inp["a_log"][l][0, j * 8:(j + 1) * 8], inp["a_log"][l][1, j * 8:(j + 1) * 8]]))
    d["dskip"] = rep(inp["d_skip"][l][j * 8:(j + 1) * 8])
    d["ssd_g"] = rep(inp["ssd_norm_g"][l][j * 512:(j + 1) * 512])
    xbc_ch = np.concatenate([np.arange(j * 512, (j + 1) * 512), np.arange(2048 + j * 128, 2048 + (j + 1) * 128),
                             np.arange(2560 + j * 128, 2560 + (j + 1) * 128)])
    cwj = inp["conv_w"][l][:, xbc_ch]
    d["cw"] = np.ascontiguousarray(cwj.reshape(7, 6, 128).transpose(2, 1, 0).reshape(128, 42))
    d["cb"] = np.ascontiguousarray(inp["conv_b"][l][xbc_ch].reshape(6, 128).T)
    kk = np.arange(128)[:, None]
    qq = np.arange(128)[None, :]
    bt = np.zeros((128, 9, 128), np.float32)
    for g, dil in enumerate((1, 4, 16)):
        for di, dlt in enumerate((-1, 0, 1)):
            rel = kk + 128 * dlt - qq
            vals = inp["rel_bias"][t5_bucket(rel * dil), g * 4 + j]
            bt[:, g * 3 + di, :] = np.where(np.abs(rel) <= 64, vals, NEG)
    d["biasT"] = bt.reshape(128, 9 * 128)
    d["wba"] = np.ascontiguousarray(inp["w_branch_attn"][l][j * 128:(j + 1) * 128, :])
    d["wbs"] = np.ascontiguousarray(inp["w_branch_ssd"][l][j * 512:(j + 1) * 512, :])
    wbf = inp["w_branch_fourier"][l]
    d["wbf"] = np.ascontiguousarray(np.concatenate([wbf[((3 * j + i3) // 2) * 256:((3 * j + i3) // 2 + 1) * 256, :] for i3 in range(3)], axis=0))
    d["wo"] = np.ascontiguousarray(inp["w_mix_out"][l])
    d["gate_b"] = np.ascontiguousarray(inp["gate_bias"][l].reshape(24, 128).T)
    d["T256"] = core_consts(j)
    d.update(host_consts())
    return d


MIX_IN = dict(x=[S, D], w1=[D, W1N], wg=[D, 3072], g_mix=[128, D], gqk=[128, 2], dtb=[128, 16], alog=[128, 16], dskip=[128, 8],
              ssd_g=[128, 512], cw=[128, 42], cb=[128, 6], biasT=[128, 1152], wba=[128, D], wbs=[512, D], wbf=[768, D], wo=[D, D],
              gate_b=[128, 24], T256=[128, 1536], ident=[128, 128], trif=[128, 128], trib=[128, 128], maskf=[128, 128],
              maskb=[128, 128], E1=[64, 16384], T2=[128, 512])


def build_mixer_program(dbg=None, phases=(1, 2, 3, 4, 5, 6), gmode="compute"):
    nc = bass.Bass("TRN2", target_bir_lowering=False)
    I = {n: nc.dram_tensor(n, shp, F32, kind="ExternalInput").ap() for n, shp in MIX_IN.items()}
    I["cw"] = I["cw"].rearrange("p (a b) -> p a b", a=6)
    O = nc.dram_tensor("mix_out", [S, D], F32, kind="ExternalOutput").ap()
    k = K(nc)
    c = mk_consts(k, I)
    k.A.persist()
    build_mixer(k, I, O, c, dbg, phases, gmode=gmode)
    info = k.P.emit()
    return nc, info


CONST_NAMES = ("ident", "trif", "trib", "maskf", "maskb", "E1", "T2")
SLOT_DEP = ("w1", "dtb", "alog", "dskip", "ssd_g", "cw", "cb", "biasT", "wba", "wbs", "wbf")
LAYER_DEP = ("wg", "g_mix", "gqk", "wo", "gate_b")
TOK_LAYER = ("g_x", "g_m", "g_f", "gxqk", "wxq", "wxk", "wxv", "wxo", "wfg", "wfu", "wfd")


def build_fused_program():
    nc = bass.Bass("TRN2", target_bir_lowering=False)
    I = {}

    def ext(name, shp):
        I[name] = nc.dram_tensor(name, shp, F32, kind="ExternalInput").ap()

    ext("x", [S, D])
    ext("mem", [256, D])
    for n in CONST_NAMES:
        ext(n, MIX_IN[n])
    for j in range(4):
        ext("T256_%d" % j, MIX_IN["T256"])
    for l in range(2):
        for n in LAYER_DEP:
            ext("%s_%d" % (n, l), MIX_IN[n])
        for n in SLOT_DEP:
            for j in range(4):
                ext("%s_%d_%d" % (n, l, j), MIX_IN[n])
        for n in TOK_LAYER:
            ext("%s_%d" % (n, l), TOK_IN[n])
    out = nc.dram_tensor("xout", [S, D], F32, kind="ExternalOutput").ap()
    mixp = [nc.dram_tensor("mixp%d" % j, [S, D], F32).ap() for j in range(4)]
    xl0 = nc.dram_tensor("xl0", [S, D], F32).ap()
    k = K(nc)
    P = k.P
    c = mk_consts(k, I)
    k.A.persist()
    for l in range(2):
        xin = I["x"] if l == 0 else xl0
        for j in range(4):
            Im = {n: I[n] for n in CONST_NAMES}
            Im["T256"] = I["T256_%d" % j]
            Im["x"] = xin
            for n in LAYER_DEP:
                Im[n] = I["%s_%d" % (n, l)]
            for n in SLOT_DEP:
                Im[n] = I["%s_%d_%d" % (n, l, j)]
            Im["cw"] = Im["cw"].rearrange("p (a b) -> p a b", a=6)
            build_mixer(k, Im, mixp[j], c, gmode="store" if j == 0 else "load")
        It = {n: I[n] for n in CONST_NAMES[:5]}
        It["mem"] = I["mem"]
        for n in TOK_LAYER:
            It[n] = I["%s_%d" % (n, l)]
        xout_l = xl0 if l == 0 else out
        for s_ in range(4):
            rows = slice(s_ * TS, (s_ + 1) * TS)
            It["xres"] = xin[rows, :]
            build_tok(k, It, xout_l[rows, :], c, [mixp[j][rows, :] for j in range(4)])
    info = P.emit()
    return nc, info


def fused_inputs(inp, b):
    d = {}
    x = np.ascontiguousarray(inp["x"][b], dtype=np.float32)
    d["x"] = x
    for l in range(2):
        for j in range(4):
            m = mixer_inputs(inp, l, j, x)
            if l == 0:
                d["T256_%d" % j] = m["T256"]
            if j == 0:
                for n in LAYER_DEP:
                    d["%s_%d" % (n, l)] = m[n]
                if l == 0:
                    for n in CONST_NAMES:
                        d[n] = m[n]
            for n in SLOT_DEP:
                d["%s_%d_%d" % (n, l, j)] = m[n]
        t = tok_inputs(inp, l, b, 0, x[:TS], None)
        if l == 0:
            d["mem"] = t["mem"]
        for n in TOK_LAYER:
            d["%s_%d" % (n, l)] = t[n]
    return d


_PROGS = {}
FUSED = True


def _prog(name):
    if name not in _PROGS:
        _PROGS[name] = {"mix": build_mixer_program, "tok": build_tok_program, "fused": build_fused_program}[name]()[0]
    return _PROGS[name]


def kernel(**inp):
    inp = {k_: np.asarray(v) for k_, v in inp.items()}
    cores = list(range(8))
    if FUSED:
        per_b = [fused_inputs(inp, b) for b in range(2)]
        ins = [per_b[cid // 4] for cid in cores]
        res = run_bass_kernel_spmd(_prog("fused"), ins, core_ids=cores)
        x = np.stack([np.concatenate([np.asarray(res.results[b * 4 + j]["xout"])[j * TS:(j + 1) * TS] for j in range(4)], axis=0)
                      for b in range(2)], axis=0)
        return np.ascontiguousarray(x, dtype=np.float32)
    x = np.ascontiguousarray(inp["x"], dtype=np.float32)
    for l in range(2):
        ins = [mixer_inputs(inp, l, cid % 4, x[cid // 4]) for cid in cores]
        res = run_bass_kernel_spmd(_prog("mix"), ins, core_ids=cores)
        part = [np.asarray(res.results[cid]["mix_out"]) for cid in cores]
        ins = []
        for cid in cores:
            b, j = cid // 4, cid % 4
            sl = slice(j * TS, (j + 1) * TS)
            parts = np.stack([part[b * 4 + i][sl] for i in range(4)], axis=0)
            ins.append(tok_inputs(inp, l, b, j, x[b, sl], parts))
        res = run_bass_kernel_spmd(_prog("tok"), ins, core_ids=cores)
        x = np.stack([np.concatenate([np.asarray(res.results[b * 4 + j]["xout"]) for j in range(4)], axis=0) for b in range(2)], axis=0)
        x = np.ascontiguousarray(x, dtype=np.float32)
    return x
```

```python
import math
import numpy as np
import concourse.bass as bass
import concourse.mybir as mybir
from concourse.bass_utils import run_bass_kernel_spmd

F32 = mybir.dt.float32
BF16 = mybir.dt.bfloat16
AF = mybir.ActivationFunctionType
ALU = mybir.AluOpType

S = 8192
D = 1024
NT = S // 128
NB = S // 512
DFF = 2816
EPS = 1e-6
NEG = -30000.0

ENGS = ("pe", "act", "dve", "pool", "sp")
NDMASEM = 8


class Op:
    __slots__ = ("eng", "fn", "deps", "dma", "sig", "needed", "idx")

    def __init__(self, eng, fn, dma):
        self.eng, self.fn, self.dma = eng, fn, dma
        self.deps = set()
        self.sig = None
        self.needed = False


class Prog:
    def __init__(self, nc):
        self.nc = nc
        self.ops = []
        self.last_w = {}
        self.readers = {}

    def op(self, eng, fn, reads=(), writes=(), dma=False):
        o = Op(eng, fn, dma)
        o.idx = len(self.ops)
        for k in reads:
            w = self.last_w.get(k)
            if w is not None:
                o.deps.add(w)
        for k in writes:
            w = self.last_w.get(k)
            if w is not None:
                o.deps.add(w)
            for r in self.readers.get(k, ()):
                o.deps.add(r)
        for k in reads:
            self.readers.setdefault(k, []).append(o.idx)
        for k in writes:
            self.last_w[k] = o.idx
            self.readers[k] = []
        o.deps.discard(o.idx)
        self.ops.append(o)
        return o

    def mm(self, out, lhsT, rhs, start, stop, r, w):
        return self.op("pe", lambda e: e.matmul(out, lhsT, rhs, start=start, stop=stop), r, w)

    def tr(self, out, in_, ident, r, w):
        return self.op("pe", lambda e: e.transpose(out, in_, ident), r, w)

    def dma(self, eng, out, in_, r, w, **kw):
        return self.op(eng, lambda e: e.dma_start(out=out, in_=in_, **kw), r, w, dma=True)

    def act(self, out, in_, func, r, w, bias=None, scale=None, accum=None):
        kw = {}
        if bias is not None:
            kw["bias"] = bias
        if scale is not None:
            kw["scale"] = scale
        if accum is not None:
            kw["accum_out"] = accum
        return self.op("act", lambda e: e.activation(out=out, in_=in_, func=func, **kw), r, w)

    def tt(self, eng, out, in0, in1, op, r, w):
        return self.op(eng, lambda e: e.tensor_tensor(out=out, in0=in0, in1=in1, op=op), r, w)

    def ts(self, eng, out, in0, s1, op0, r, w, s2=None, op1=None):
        if op1 is None:
            return self.op(eng, lambda e: e.tensor_scalar(out=out, in0=in0, scalar1=s1, scalar2=None, op0=op0), r, w)
        return self.op(eng, lambda e: e.tensor_scalar(out=out, in0=in0, scalar1=s1, scalar2=s2, op0=op0, op1=op1), r, w)

    def stt(self, out, in0, scalar, in1, op0, op1, r, w):
        return self.op("dve", lambda e: e.scalar_tensor_tensor(out=out, in0=in0, scalar=scalar, in1=in1, op0=op0, op1=op1), r, w)

    def cp(self, eng, out, in_, r, w):
        if eng == "act":
            return self.op("act", lambda e: e.copy(out=out, in_=in_), r, w)
        return self.op(eng, lambda e: e.tensor_copy(out=out, in_=in_), r, w)

    def memset(self, eng, ap, val, w):
        return self.op(eng, lambda e: e.memset(ap, val), (), w)

    def recip(self, out, in_, r, w):
        return self.op("dve", lambda e: e.reciprocal(out=out, in_=in_), r, w)

    def barrier(self):
        deps = set()
        for eng in ENGS:
            last_c = None
            dm = []
            for o in self.ops:
                if o.eng != eng or o.fn is None:
                    continue
                if o.dma:
                    dm.append(o.idx)
                else:
                    last_c = o.idx
            if last_c is not None:
                deps.add(last_c)
            deps.update(dm[-NDMASEM:])
        for eng in ENGS:
            o = Op(eng, None, False)
            o.idx = len(self.ops)
            o.deps = set(deps)
            self.ops.append(o)
        self.last_w = {}
        self.readers = {}

    def emit(self):
        nc = self.nc
        ops = self.ops
        for o in ops:
            if o.eng == "pe" and not o.dma:
                o.deps = {d for d in o.deps if not (ops[d].eng == "pe" and not ops[d].dma)}
            for d in o.deps:
                ops[d].needed = True
        sems = {e: nc.alloc_semaphore("s_" + e) for e in ENGS}
        dsems = {e: [nc.alloc_semaphore("d_%s%d" % (e, i)) for i in range(NDMASEM)] for e in ENGS}
        cnt = {e: 0 for e in ENGS}
        dcnt = {e: 0 for e in ENGS}
        for o in ops:
            if o.fn is None:
                continue
            if o.dma:
                i = dcnt[o.eng]
                dcnt[o.eng] += 1
                o.sig = (dsems[o.eng][i % NDMASEM], 16 * (i // NDMASEM + 1))
            elif o.needed:
                cnt[o.eng] += 1
                o.sig = (sems[o.eng], cnt[o.eng])
        per = {e: [o for o in ops if o.eng == e] for e in ENGS}

        def run(eng, e):
            known = {}
            for o in per[eng]:
                need = {}
                for d in o.deps:
                    s = ops[d].sig
                    if s is None:
                        continue
                    sem, val = s
                    if need.get(id(sem), (None, 0))[1] < val:
                        need[id(sem)] = (sem, val)
                if o.dma:
                    sem, val = o.sig
                    if val - 16 > 0 and need.get(id(sem), (None, 0))[1] < val - 16:
                        need[id(sem)] = (sem, val - 16)
                for k, (sem, val) in need.items():
                    if known.get(k, 0) >= val:
                        continue
                    e.wait_ge(sem, val)
                    known[k] = val
                if o.fn is None:
                    continue
                inst = o.fn(e)
                if o.sig is not None:
                    inst.then_inc(o.sig[0], 16 if o.dma else 1)

        with nc.Block() as blk:
            @blk.tensor
            def _(e):
                run("pe", e)

            @blk.scalar
            def _(e):
                run("act", e)

            @blk.vector
            def _(e):
                run("dve", e)

            @blk.gpsimd
            def _(e):
                run("pool", e)

            @blk.sync
            def _(e):
                run("sp", e)
        return dict(n_ops=len(ops), cnt=cnt, dcnt=dcnt)


class Arena:
    def __init__(self, nc, nbytes):
        self.t = nc.alloc_sbuf_tensor("arena", [128, nbytes // 2], BF16)
        self.cap = nbytes // 2
        self.off = 0
        self.base = 0

    def persist(self):
        self.base = self.off

    def reset(self):
        self.off = self.base

    def alloc(self, shape, dt, parts=128):
        n = int(np.prod(shape))
        e16 = n * (2 if dt == F32 else 1)
        e16 = (e16 + 15) // 16 * 16
        assert self.off + e16 <= self.cap, ("SBUF arena overflow", self.off, e16, self.cap)
        v = self.t[0:parts, self.off:self.off + e16]
        self.off += e16
        if dt == F32:
            v = v.bitcast(F32)
        v = v[:, 0:n]
        if len(shape) == 2:
            v = v.rearrange("p (a b) -> p a b", a=shape[0])
        elif len(shape) == 3:
            v = v.rearrange("p (a b c) -> p a b c", a=shape[0], b=shape[1])
        return v


class K:
    def __init__(self, nc):
        self.nc = nc
        self.P = Prog(nc)
        self.A = Arena(nc, 184 * 1024)
        self.banks = [nc.alloc_psum_tensor("pb%d" % i, [128, 512], F32) for i in range(8)]
        self.uid = 0
        self.scr = {}

    def bank(self, i):
        return self.banks[i][:, :]

    def bank16(self, i):
        return self.banks[i][:, :].bitcast(BF16)

    def key(self, s):
        self.uid += 1
        return (s, self.uid)


def load_cast(k, dst, src, ncols, wkey):
    c = 0
    while c < ncols:
        n = min(2048, ncols - c)
        k.P.dma("pool", dst[:, c:c + n], src[:, c:c + n], [], [wkey])
        c += n


def norm_transpose(k, xtile, xkey, g_bc, consts, hT, hkey, col0, tmp, tkey, pbank, pkey, par):
    P = k.P
    junk, ss, rs, hb = tmp
    P.act(junk, xtile, AF.Square, [xkey], [tkey + "j", tkey + "ss"], accum=ss)
    P.act(rs, ss, AF.Sqrt, [tkey + "ss", "consts"], [tkey + "rs"], bias=consts["eps"], scale=1.0 / D)
    P.recip(rs, rs, [tkey + "rs"], [tkey + "rs"])
    P.stt(hb, xtile, rs, g_bc, ALU.mult, ALU.mult, [xkey, tkey + "rs", "gbc"], [tkey + "hb"])
    p16 = k.bank16(pbank)
    for kc in range(8):
        P.tr(p16[:, kc * 128:(kc + 1) * 128], hb[:, kc * 128:(kc + 1) * 128], consts["ident"], [tkey + "hb", "consts"], [pkey])
    P.cp("act" if par else "dve", hT[:, :, col0:col0 + 128], p16.rearrange("p (a b) -> p a b", a=8), [pkey], [hkey])


def mk_consts(k, cin):
    A, P = k.A, k.P
    c = {}
    c["ident"] = A.alloc([128], BF16)
    c["ones"] = A.alloc([128], BF16)
    c["eps"] = A.alloc([1], F32)
    c["one"] = A.alloc([1], F32)
    c["trif"] = A.alloc([128], BF16)
    c["trib"] = A.alloc([128], BF16)
    c["maskf"] = A.alloc([128], BF16)
    c["maskb"] = A.alloc([128], BF16)
    for nm in ("ident", "trif", "trib", "maskf", "maskb"):
        P.dma("pool", c[nm], cin[nm], [], ["consts"])
    P.memset("dve", c["ones"], 1.0, ["consts"])
    P.memset("dve", c["eps"], EPS, ["consts"])
    P.memset("dve", c["one"], 1.0, ["consts"])
    return c


QK0, V0, Z0, XBC0, DT0, FI0, W1N = 0, 768, 1152, 1664, 2432, 2448, 2832


def build_mixer(k, I, O, c, dbg=None, phases=(1, 2, 3, 4, 5, 6), tag="", gmode="compute"):
    nc, P, A = k.nc, k.P, k.A
    def scr(name, shape, dt):
        if dbg is not None and name in dbg:
            return nc.dram_tensor(name, shape, dt, kind="ExternalOutput").ap()
        if name + tag not in k.scr:
            k.scr[name + tag] = nc.dram_tensor(name + tag, shape, dt).ap()
        return k.scr[name + tag]

    qk_scr = scr("qk_scr", [6, 128, S], BF16)
    v_scr = scr("v_scr", [S, 3, 132], BF16)
    zs_scr = scr("zs_scr", [S, 512], BF16)
    f_scr = scr("f_scr", [S, 384], BF16)
    xbcT_scr = scr("xbcT_scr", [6, 128, S + 8], BF16)
    xsB_scr = scr("xsB_scr", [S, 640], BF16)
    bcT_scr = scr("bcT_scr", [2, 128, S], BF16)
    ybwd_scr = scr("ybwd_scr", [S, 512], F32)
    ynT_scr = scr("ynT_scr", [4, 128, S], BF16)
    un_scr = scr("un_scr", [3, S, 132], F32)
    oT_scr = scr("oT_scr", [128, S], BF16)
    z_scr = scr("z_scr", [128, 128, 384], BF16)
    xT_scr = scr("xT_scr", [6, 128, S], BF16)
    dt_dram = scr("dt_scr", [S, 16], F32)
    g_scr = scr("g_scr", [24, 128, S], BF16) if gmode != "compute" else None

    if 1 in phases:
        A.reset()
        W1 = A.alloc([8, W1N], BF16)
        gbc = A.alloc([D], F32)
        gqk = A.alloc([2], F32)
        dtb = A.alloc([16], F32)
        dt_all = None
        xt = [A.alloc([4, D], F32) for _ in range(2)]
        hT = [A.alloc([8, 512], BF16) for _ in range(2)]
        junk = A.alloc([D], BF16)
        ss = A.alloc([8], F32)
        rs = A.alloc([8], F32)
        hb = [A.alloc([D], BF16) for _ in range(2)]
        sq = [A.alloc([512], BF16) for _ in range(2)]
        lnv = [A.alloc([512], F32) for _ in range(2)]
        qk_st = [A.alloc([6, 512], BF16) for _ in range(2)]
        v_st = [A.alloc([4, 3, 132], BF16) for _ in range(2)]
        z_st = [A.alloc([4, 512], BF16) for _ in range(2)]
        f_st = [A.alloc([4, 384], BF16) for _ in range(2)]
        x_st = [A.alloc([6, 512], BF16) for _ in range(2)]
        dtt = A.alloc([4, 16], F32)
        zpad = A.alloc([8], BF16)
        for kc in range(8):
            load_cast(k, W1[:, kc, :], I["w1"][kc * 128:(kc + 1) * 128, :], W1N, "W1")
        P.dma("sp", gbc, I["g_mix"], [], ["gbc"])
        P.dma("sp", gqk, I["gqk"], [], ["gqk"])
        P.ts("dve", gqk[:, 0:1], gqk[:, 0:1], 128.0 ** -0.5, ALU.mult, ["gqk"], ["gqk"])
        P.dma("sp", dtb, I["dtb"], [], ["dtb"])
        P.memset("dve", zpad, 0.0, ["zpad"])
        for i in range(2):
            P.memset("pool", v_st[i], 0.0, [("v_st", i)])
            P.memset("pool", v_st[i][:, :, :, 128:129], 1.0, [("v_st", i)])
        for cc in range(6):
            P.dma("sp", xbcT_scr[cc, :, 0:4], zpad[:, 0:4], ["zpad"], [("xbcT", cc, -1)])
            P.dma("sp", xbcT_scr[cc, :, S + 4:S + 8], zpad[:, 4:8], ["zpad"], [("xbcT", cc, -2)])
        gen = [3, 4, 5]
        gi = 0
        for b in range(NB):
            pb = b % 2
            P.dma("sp", xt[pb], I["x"][b * 512:(b + 1) * 512, :].rearrange("(t p) d -> p t d", p=128), [], [("xt", pb)])
            for t in range(4):
                norm_transpose(k, xt[pb][:, t, :], ("xt", pb), gbc, c, hT[pb], ("hT", pb), t * 128,
                               (junk, ss[:, t:t + 1], rs[:, t:t + 1], hb[t % 2]), "n%d" % (t % 2), 0, "pT", t % 2)
            hk = ("hT", pb)
            for hd in range(6):
                s2 = hd % 2
                pq = k.bank(1 + s2)
                for kc in range(8):
                    P.mm(pq, W1[:, kc, QK0 + hd * 128:QK0 + (hd + 1) * 128], hT[pb][:, kc, :], kc == 0, kc == 7, ["W1", hk], [("pq", s2)])
                P.act(sq[s2], pq, AF.Square, [("pq", s2)], [("sq", s2)])
                pss = k.bank(6 + s2)
                P.mm(pss, c["ones"], sq[s2], True, True, [("sq", s2), "consts"], [("pss", s2)])
                P.act(lnv[s2], pss, AF.Ln, [("pss", s2), "consts"], [("lnv", s2)], bias=c["eps"], scale=1.0 / 128)
                P.act(lnv[s2], lnv[s2], AF.Exp, [("lnv", s2)], [("lnv", s2)], scale=-0.5)
                P.stt(qk_st[pb][:, hd, :], pq, gqk[:, (hd // 3):(hd // 3) + 1], lnv[s2], ALU.mult, ALU.mult,
                      [("pq", s2), "gqk", ("lnv", s2)], [("qk_st", pb)])
            P.dma("sp", qk_scr[:, :, b * 512:(b + 1) * 512].rearrange("h p t -> p h t"), qk_st[pb], [("qk_st", pb)], [("qk_scr", b)])
            for t in range(4):
                lhs = lambda kc: hT[pb][:, kc, t * 128:(t + 1) * 128]
                g_ = gen[gi % 3]; gi += 1
                pv = k.bank(g_)[:, 0:384]
                for kc in range(8):
                    P.mm(pv, lhs(kc), W1[:, kc, V0:V0 + 384], kc == 0, kc == 7, ["W1", hk], [("pg", g_)])
                P.cp("dve", v_st[pb][:, t, :, 0:128], pv.rearrange("p (g e) -> p g e", g=3), [("pg", g_)], [("v_st", pb)])
                g_ = gen[gi % 3]; gi += 1
                pz = k.bank(g_)
                for kc in range(8):
                    P.mm(pz, lhs(kc), W1[:, kc, Z0:Z0 + 512], kc == 0, kc == 7, ["W1", hk], [("pg", g_)])
                P.act(z_st[pb][:, t, :], pz, AF.Silu, [("pg", g_)], [("z_st", pb)])
                g_ = gen[gi % 3]; gi += 1
                pf = k.bank(g_)[:, 0:384]
                for kc in range(8):
                    P.mm(pf, lhs(kc), W1[:, kc, FI0:FI0 + 384], kc == 0, kc == 7, ["W1", hk], [("pg", g_)])
                P.cp("dve", f_st[pb][:, t, :], pf, [("pg", g_)], [("f_st", pb)])
                g_ = gen[gi % 3]; gi += 1
                pd = k.bank(g_)[:, 0:16]
                for kc in range(8):
                    P.mm(pd, lhs(kc), W1[:, kc, DT0:DT0 + 16], kc == 0, kc == 7, ["W1", hk], [("pg", g_)])
                P.tt("dve", dtt[:, t, :], pd, dtb, ALU.add, [("pg", g_), "dtb"], ["dtt"])
            P.dma("sp", v_scr[b * 512:(b + 1) * 512].rearrange("(t p) g e -> p t g e", p=128), v_st[pb], [("v_st", pb)], [("v_scr", b)])
            P.dma("sp", zs_scr[b * 512:(b + 1) * 512, :].rearrange("(t p) d -> p t d", p=128), z_st[pb], [("z_st", pb)], [("zs_scr", b)])
            P.dma("sp", f_scr[b * 512:(b + 1) * 512, :].rearrange("(t p) d -> p t d", p=128), f_st[pb], [("f_st", pb)], [("f_scr", b)])
            P.dma("sp", dt_dram[b * 512:(b + 1) * 512, :].rearrange("(t p) d -> p t d", p=128), dtt, ["dtt"], [("dt_scr", b)])
            for cc in range(6):
                g_ = gen[gi % 3]; gi += 1
                px = k.bank(g_)
                for kc in range(8):
                    P.mm(px, W1[:, kc, XBC0 + cc * 128:XBC0 + (cc + 1) * 128], hT[pb][:, kc, :], kc == 0, kc == 7, ["W1", hk], [("pg", g_)])
                P.cp("act", x_st[pb][:, cc, :], px, [("pg", g_)], [("x_st", pb)])
            P.dma("sp", xbcT_scr[:, :, 4 + b * 512:4 + (b + 1) * 512].rearrange("h p t -> p h t"), x_st[pb], [("x_st", pb)], [("xbcT", b)])
        P.barrier()

    if 2 in phases:
        A.reset()
        cw = A.alloc([6, 7], F32)
        cb = A.alloc([6], F32)
        U = [A.alloc([518], BF16) for _ in range(2)]
        dg = A.alloc([42, 128], BF16)
        Yc = [A.alloc([512], BF16) for _ in range(2)]
        xs_st = [A.alloc([4, 640], BF16) for _ in range(2)]
        P.dma("sp", cw, I["cw"], [], ["cw"])
        P.dma("sp", cb, I["cb"], [], ["cw"])
        for cc in range(6):
            for kk in range(7):
                P.ts("dve" if kk % 2 else "pool", dg[:, cc * 7 + kk, :], c["ident"], cw[:, cc, kk:kk + 1], ALU.mult, ["consts", "cw"], ["dg"])
        ui = 0
        for b in range(NB):
            pb = b % 2
            for cc in range(6):
                u = ui % 2; ui += 1
                P.dma("sp", U[u], xbcT_scr[cc, :, 1 + b * 512:1 + b * 512 + 518], [], [("U", u)])
                pc = k.bank(2 + u)
                for kk in range(7):
                    P.mm(pc, dg[:, cc * 7 + kk, :], U[u][:, kk:kk + 512], kk == 0, kk == 6, [("U", u), "dg"], [("pc", u)])
                P.act(Yc[u], pc, AF.Silu, [("pc", u), "cw"], [("Yc", u)], bias=cb[:, cc:cc + 1])
                if cc >= 4:
                    P.dma("sp", bcT_scr[cc - 4, :, b * 512:(b + 1) * 512], Yc[u], [("Yc", u)], [("bcT", cc - 4, b)])
                if cc <= 4:
                    p16 = k.bank16(cc % 2)
                    for t in range(4):
                        P.tr(p16[:, t * 128:(t + 1) * 128], Yc[u][:, t * 128:(t + 1) * 128], c["ident"], [("Yc", u), "consts"], [("pt2", cc % 2)])
                    P.cp("act", xs_st[pb][:, :, cc * 128:(cc + 1) * 128], p16[:, 0:512].rearrange("p (t e) -> p t e", t=4), [("pt2", cc % 2)], [("xs_st", pb)])
            P.dma("sp", xsB_scr[b * 512:(b + 1) * 512, :].rearrange("(t p) d -> p t d", p=128), xs_st[pb], [("xs_st", pb)], [("xsB", b)])
        P.barrier()

    if 3 in phases:
        A.reset()
        BT = A.alloc([S], BF16)
        CT = A.alloc([S], BF16)
        dt_sb = A.alloc([NT, 16], F32)
        abc = A.alloc([16], F32)
        dsk = A.alloc([8], F32)
        ngb = A.alloc([512], F32)
        xsB = [A.alloc([640], BF16) for _ in range(2)]
        la = [A.alloc([16], F32) for _ in range(2)]
        lah = [A.alloc([8], BF16) for _ in range(2)]
        lal = [A.alloc([8], BF16) for _ in range(2)]
        lahb = [A.alloc([8, 128], BF16) for _ in range(2)]
        lalb = [A.alloc([8, 128], BF16) for _ in range(2)]
        acs = [A.alloc([16], F32) for _ in range(2)]
        nacs = [A.alloc([8], F32) for _ in range(2)]
        ecs = [A.alloc([8], F32) for _ in range(2)]
        wdec = [A.alloc([8], F32) for _ in range(2)]
        cdec = [A.alloc([8], F32) for _ in range(2)]
        Lt = [A.alloc([8, 128], BF16) for _ in range(2)]
        Mt = [A.alloc([8, 128], BF16) for _ in range(2)]
        xd = [A.alloc([512], BF16) for _ in range(2)]
        xdw = [A.alloc([512], BF16) for _ in range(2)]
        t1 = [A.alloc([512], F32) for _ in range(2)]
        yc = [A.alloc([512], F32) for _ in range(2)]
        H = A.alloc([512], F32)
        Htmp = A.alloc([512], F32)
        Hb = A.alloc([512], BF16)
        ybl = [A.alloc([512], F32) for _ in range(2)]
        zl = [A.alloc([512], BF16) for _ in range(2)]
        yz = [A.alloc([512], F32) for _ in range(2)]
        junk3 = A.alloc([512], BF16)
        ss3 = A.alloc([2], F32)
        ynb = [A.alloc([512], BF16) for _ in range(2)]
        yn_st = [A.alloc([4, 512], BF16) for _ in range(2)]
        P.dma("sp", BT, bcT_scr[0], [], ["BT"])
        P.dma("sp", CT, bcT_scr[1], [], ["CT"])
        P.dma("sp", dt_sb, dt_dram.rearrange("(n p) d -> p n d", p=128), [], ["dt_sb"])
        P.dma("sp", abc, I["alog"], [], ["abc"])
        P.dma("sp", dsk, I["dskip"], [], ["dsk"])
        P.dma("sp", ngb, I["ssd_g"], [], ["ngb"])
        dtf = dt_sb.rearrange("p n d -> p (n d)")
        sp1 = A.alloc([NT * 16], F32)
        P.act(sp1, dtf, AF.Abs, ["dt_sb"], ["sp1"])
        P.act(sp1, sp1, AF.Exp, ["sp1"], ["sp1"], scale=-1.0)
        P.act(sp1, sp1, AF.Ln, ["sp1", "consts"], ["sp1"], bias=c["one"])
        P.ts("dve", dtf, dtf, 0.0, ALU.max, ["dt_sb"], ["dt_sb"])
        P.tt("dve", dtf, dtf, sp1, ALU.add, ["dt_sb", "sp1"], ["dt_sb"])
        P.act(abc, abc, AF.Exp, ["abc"], ["abc"])
        P.ts("dve", abc, abc, -1.0, ALU.mult, ["abc"], ["abc"])
        if dbg is not None and "dtsp_dbg" in dbg:
            dd = nc.dram_tensor("dtsp_dbg", [S, 16], F32, kind="ExternalOutput").ap()
            P.dma("sp", dd.rearrange("(n p) d -> p n d", p=128), dt_sb, ["dt_sb"], ["dd"])

        def ssd_pass(direction):
            d0 = direction * 8
            tri = c["trif"] if direction == 0 else c["trib"]
            msk = c["maskf"] if direction == 0 else c["maskb"]
            order = range(NT) if direction == 0 else range(NT - 1, -1, -1)
            P.memset("dve", H, 0.0, ["H"])
            P.memset("pool", Hb, 0.0, ["Hb"])
            order = list(order)

            def front(it):
                ch = order[it]
                q = it % 2
                cs = slice(ch * 128, (ch + 1) * 128)
                P.dma("sp", xsB[q], xsB_scr[cs, :], [], [("xsB", q)])
                if direction == 0:
                    P.dma("sp", ybl[q], ybwd_scr[cs, :], [("ybwd", ch)], [("ybl", q)])
                    P.dma("sp", zl[q], zs_scr[cs, :], [], [("zl", q)])
                P.tt("dve", la[q][:, 0:8], dt_sb[:, ch, d0:d0 + 8], abc[:, d0:d0 + 8], ALU.mult, ["dt_sb", "abc"], [("la", q)])
                P.cp("dve", lah[q], la[q][:, 0:8], [("la", q)], [("lah", q)])
                P.tt("dve", lal[q], la[q][:, 0:8], lah[q], ALU.subtract, [("la", q), ("lah", q)], [("lal", q)])
                P.cp("act", lahb[q], lah[q].unsqueeze(2).to_broadcast([128, 8, 128]), [("lah", q)], [("lahb", q)])
                P.cp("act", lalb[q], lal[q].unsqueeze(2).to_broadcast([128, 8, 128]), [("lal", q)], [("lalb", q)])
                pa = k.bank(0)
                P.mm(pa[:, 0:8], tri, lah[q], True, False, [("lah", q), "consts"], ["pa"])
                P.mm(pa[:, 0:8], tri, lal[q], False, True, [("lal", q), "consts"], ["pa"])
                P.mm(pa[:, 8:16], c["ones"], lah[q], True, False, [("lah", q), "consts"], ["pa"])
                P.mm(pa[:, 8:16], c["ones"], lal[q], False, True, [("lal", q), "consts"], ["pa"])
                P.cp("dve", acs[q], pa[:, 0:16], ["pa"], [("acs", q)])
                P.ts("dve", nacs[q], acs[q][:, 0:8], -1.0, ALU.mult, [("acs", q)], [("nacs", q)])
                P.act(ecs[q], acs[q][:, 0:8], AF.Exp, [("acs", q)], [("ecs", q)])
                P.tt("dve", wdec[q], acs[q][:, 8:16], acs[q][:, 0:8], ALU.subtract, [("acs", q)], [("wdec", q)])
                P.act(wdec[q], wdec[q], AF.Exp, [("wdec", q)], [("wdec", q)])
                P.act(cdec[q], acs[q][:, 8:16], AF.Exp, [("acs", q)], [("cdec", q)])
                for hb2 in range(2):
                    pk = ("prb", hb2)
                    for h in range(hb2 * 4, hb2 * 4 + 4):
                        pr = k.banks[1 + hb2][:, (h % 4) * 128:(h % 4 + 1) * 128]
                        P.mm(pr, lahb[q][:, h, :], tri, True, False, [("lahb", q), "consts"], [pk])
                        P.mm(pr, lalb[q][:, h, :], tri, False, False, [("lalb", q), "consts"], [pk])
                        P.mm(pr, c["ident"], msk, False, True, ["consts"], [pk])
                    for h in range(hb2 * 4, hb2 * 4 + 4):
                        pr = k.banks[1 + hb2][:, (h % 4) * 128:(h % 4 + 1) * 128]
                        P.act(Lt[q][:, h, :], pr, AF.Exp, [pk, ("nacs", q)], [("Lt", q)], bias=nacs[q][:, h:h + 1])
                pcb = k.bank(3)[:, 0:128]
                P.mm(pcb, BT[:, cs], CT[:, cs], True, True, ["BT", "CT"], ["pcb"])

            def front_b(it):
                ch = order[it]
                q = it % 2
                pcb = k.bank(3)[:, 0:128]
                P.tt("dve", Mt[q], pcb.unsqueeze(1).to_broadcast([128, 8, 128]), Lt[q], ALU.mult, ["pcb", ("Lt", q)], [("Mt", q)])
                xs3 = xsB[q][:, 0:512].rearrange("p (h e) -> p h e", h=8)
                dtb3 = dt_sb[:, ch, d0:d0 + 8].unsqueeze(2).to_broadcast([128, 8, 64])
                P.tt("pool", xd[q].rearrange("p (h e) -> p h e", h=8), xs3, dtb3, ALU.mult, [("xsB", q), "dt_sb"], [("xd", q)])

            def back(it):
                ch = order[it]
                q = it % 2
                cs = slice(ch * 128, (ch + 1) * 128)
                xs3 = xsB[q][:, 0:512].rearrange("p (h e) -> p h e", h=8)
                py = k.bank(4)
                for h in range(8):
                    P.mm(py[:, h * 64:(h + 1) * 64], Mt[q][:, h, :], xd[q][:, h * 64:(h + 1) * 64], True, True, [("Mt", q), ("xd", q)], ["py"])
                pyo = k.bank(5)
                P.mm(pyo, CT[:, cs], Hb, True, True, ["CT", "Hb"], ["pyo"])
                P.tt("dve", t1[q].rearrange("p (h e) -> p h e", h=8), pyo.rearrange("p (h e) -> p h e", h=8),
                     ecs[q].unsqueeze(2).to_broadcast([128, 8, 64]), ALU.mult, ["pyo", ("ecs", q)], [("t1", q)])
                P.tt("dve", yc[q], py, t1[q], ALU.add, ["py", ("t1", q)], [("yc", q)])
                P.tt("pool", xdw[q].rearrange("p (h e) -> p h e", h=8), xd[q].rearrange("p (h e) -> p h e", h=8),
                     wdec[q].unsqueeze(2).to_broadcast([128, 8, 64]), ALU.mult, [("xd", q), ("wdec", q)], [("xdw", q)])
                pst = k.bank(6)
                P.mm(pst, xsB[q][:, 512:640], xdw[q], True, True, [("xsB", q), ("xdw", q)], ["pst"])
                P.tt("pool", Htmp.rearrange("p (h e) -> p h e", h=8), H.rearrange("p (h e) -> p h e", h=8),
                     cdec[q].unsqueeze(2).to_broadcast([128, 8, 64]), ALU.mult, ["H", ("cdec", q)], ["Htmp"])
                P.tt("dve", H, pst, Htmp, ALU.add, ["pst", "Htmp"], ["H"])
                P.cp("act", Hb, H, ["H"], ["Hb"])
                if direction == 1:
                    P.dma("sp", ybwd_scr[cs, :], yc[q], [("yc", q)], [("ybwd", ch)])
                else:
                    P.tt("pool", yc[q], yc[q], ybl[q], ALU.add, [("yc", q), ("ybl", q)], [("yc", q)])
                    P.tt("pool", yz[q].rearrange("p (h e) -> p h e", h=8), xs3, dsk.unsqueeze(2).to_broadcast([128, 8, 64]), ALU.mult,
                         [("xsB", q), "dsk"], [("yz", q)])
                    P.tt("pool", yz[q], yz[q], yc[q], ALU.add, [("yz", q), ("yc", q)], [("yz", q)])
                    P.tt("dve", yz[q], yz[q], zl[q], ALU.mult, [("yz", q), ("zl", q)], [("yz", q)])
                    P.act(junk3, yz[q], AF.Square, [("yz", q)], ["junk3", ("ss3", q)], accum=ss3[:, q:q + 1])
                    P.act(ss3[:, q:q + 1], ss3[:, q:q + 1], AF.Sqrt, [("ss3", q), "consts"], [("ss3", q)], bias=c["eps"], scale=1.0 / 512)
                    P.recip(ss3[:, q:q + 1], ss3[:, q:q + 1], [("ss3", q)], [("ss3", q)])
                    P.stt(ynb[q], yz[q], ss3[:, q:q + 1], ngb, ALU.mult, ALU.mult, [("yz", q), ("ss3", q), "ngb"], [("ynb", q)])
                    p16 = k.bank16(7)
                    for cc in range(4):
                        P.tr(p16[:, cc * 128:(cc + 1) * 128], ynb[q][:, cc * 128:(cc + 1) * 128], c["ident"], [("ynb", q), "consts"], ["pt3"])
                    sq_ = (ch // 4) % 2
                    P.cp("act", yn_st[sq_][:, :, (ch % 4) * 128:(ch % 4 + 1) * 128], p16[:, 0:512].rearrange("p (c t) -> p c t", c=4), ["pt3"], [("yn_st", sq_)])
                    if ch % 4 == 3:
                        bb = ch // 4
                        P.dma("sp", ynT_scr[:, :, bb * 512:(bb + 1) * 512].rearrange("c p t -> p c t"), yn_st[sq_], [("yn_st", sq_)], [("ynT", bb)])

            front(0)
            front_b(0)
            for it in range(NT):
                if it + 1 < NT:
                    front(it + 1)
                back(it)
                if it + 1 < NT:
                    front_b(it + 1)

        ssd_pass(1)
        ssd_pass(0)
        P.barrier()

    if 4 in phases:
        A.reset()
        QT = A.alloc([S], BF16)
        KT = A.alloc([S], BF16)
        Vc = A.alloc([NT, 132], BF16)
        biasT = A.alloc([9, 128], BF16)
        Pt = [A.alloc([3, 128], BF16) for _ in range(2)]
        o_st = [A.alloc([8, 132], F32) for _ in range(2)]
        un = [A.alloc([3, 132], F32) for _ in range(2)]
        rden = [A.alloc([1], F32) for _ in range(2)]
        ob = [A.alloc([128], BF16) for _ in range(2)]
        oT = A.alloc([S], BF16)
        load_cast(k, biasT.rearrange("p a b -> p (a b)"), I["biasT"], 9 * 128, "biasT")
        flush = 0
        for g, dil in enumerate((1, 4, 16)):
            sub = S // dil
            ntile = sub // 128
            P.dma("sp", QT, qk_scr[g], [], ["QT"])
            P.dma("sp", KT, qk_scr[3 + g], [], ["KT"])
            vsrc = v_scr[:, g, :].rearrange("(m i r) c -> i r m c", r=dil, i=128)
            for r in range(dil):
                P.dma("sp", Vc[:, r * ntile:(r + 1) * ntile, :], vsrc[:, r, :, :], [], ["Vc"])
            tiles = [(r, m) for r in range(dil) for m in range(ntile)]
            gsz = min(8, ntile)

            def s_part(i):
                r, m = tiles[i]
                q = i % 2
                kts = [mm_ for mm_ in (m - 1, m, m + 1) if 0 <= mm_ < ntile]
                psb = k.bank(q)
                qs = QT[:, r + dil * 128 * m: r + dil * 128 * m + dil * 127 + 1: dil]
                for j, m2 in enumerate(kts):
                    ks = KT[:, r + dil * 128 * m2: r + dil * 128 * m2 + dil * 127 + 1: dil]
                    P.mm(psb[:, j * 128:(j + 1) * 128], ks, qs, True, False, ["QT", "KT"], [("ps4", q)])
                    P.mm(psb[:, j * 128:(j + 1) * 128], c["ident"], biasT[:, g * 3 + (m2 - m + 1), :], False, True, ["biasT", "consts"], [("ps4", q)])
                n = len(kts)
                P.act(Pt[q][:, 0:n, :], psb[:, 0:n * 128].rearrange("p (a b) -> p a b", a=n), AF.Exp, [("ps4", q)], [("Pt", q)])

            def pv_part(i):
                nonlocal flush
                r, m = tiles[i]
                q = i % 2
                kts = [mm_ for mm_ in (m - 1, m, m + 1) if 0 <= mm_ < ntile]
                n = len(kts)
                po = k.bank(2 + q)[:, 0:132]
                for j, m2 in enumerate(kts):
                    P.mm(po, Pt[q][:, j, :], Vc[:, r * ntile + m2, :], j == 0, j == n - 1, [("Pt", q), "Vc"], [("po4", q)])
                sq_ = flush % 2
                P.cp("dve", o_st[sq_][:, m % gsz, :], po, [("po4", q)], [("o_st", sq_)])
                if m % gsz == gsz - 1:
                    m0 = m - (gsz - 1)
                    dst = un_scr[g].rearrange("(m i r) c -> i r m c", r=dil, i=128)[:, r, m0:m0 + gsz, :]
                    P.dma("sp", dst, o_st[sq_][:, 0:gsz, :], [("o_st", sq_)], [("un", flush)])
                    flush += 1

            for i in range(len(tiles) + 1):
                if i < len(tiles):
                    s_part(i)
                if i >= 1:
                    pv_part(i - 1)
        P.barrier()
        for t in range(NT):
            q = t % 2
            P.dma("sp", un[q], un_scr[:, t * 128:(t + 1) * 128, :].rearrange("g p c -> p g c"),
                  [], [("unl", q)])
            P.tt("dve", un[q][:, 0, :], un[q][:, 0, :], un[q][:, 1, :], ALU.add, [("unl", q)], [("unl", q)])
            P.tt("dve", un[q][:, 0, :], un[q][:, 0, :], un[q][:, 2, :], ALU.add, [("unl", q)], [("unl", q)])
            P.recip(rden[q], un[q][:, 0, 128:129], [("unl", q)], [("rden", q)])
            P.ts("dve", ob[q], un[q][:, 0, 0:128], rden[q], ALU.mult, [("unl", q), ("rden", q)], [("ob", q)])
            p16 = k.bank16(4 + q)
            P.tr(p16[:, 0:128], ob[q], c["ident"], [("ob", q), "consts"], [("pt4", q)])
            P.cp("act", oT[:, t * 128:(t + 1) * 128], p16[:, 0:128], [("pt4", q)], ["oT"])
        P.dma("sp", oT_scr, oT, ["oT"], ["oT_scr"])
        P.barrier()

    if 5 in phases:
        A.reset()
        E = A.alloc([128, 128], BF16, parts=64)
        T2 = A.alloc([2, 256], BF16)
        Fb = [A.alloc([16, 384], BF16, parts=64) for _ in range(2)]
        Zst = [A.alloc([4, 384], BF16) for _ in range(2)]
        Zl = [A.alloc([2, 384], BF16) for _ in range(3)]
        XT = A.alloc([6, S], BF16)
        for n2 in range(0, 128, 16):
            load_cast(k, E[:, n2:n2 + 16, :].rearrange("p a b -> p (a b)"), I["E1"][:, n2 * 128:(n2 + 16) * 128], 2048, "E")
        load_cast(k, T2.rearrange("p a b -> p (a b)"), I["T2"], 512, "T2")
        fsrc = f_scr.rearrange("(n1 n2) c -> n1 n2 c", n2=128)
        zi = 0
        for nb in range(8):
            q = nb % 2
            P.dma("sp", Fb[q], fsrc[:, nb * 16:(nb + 1) * 16, :], [], [("Fb", q)])
            for i in range(16):
                n2 = nb * 16 + i
                pz_ = k.bank(n2 % 2)[:, 0:384]
                P.mm(pz_, E[:, n2, :], Fb[q][:, i, :], True, True, ["E", ("Fb", q)], [("pz5", n2 % 2)])
                zq = (n2 // 4) % 2
                P.cp("act" if n2 % 2 else "dve", Zst[zq][:, n2 % 4, :], pz_, [("pz5", n2 % 2)], [("Zst", zq)])
                if n2 % 4 == 3:
                    P.dma("sp", z_scr[:, n2 - 3:n2 + 1, :], Zst[zq], [("Zst", zq)], [("z_scr", n2 // 4)])
        zall = [("z_scr", i) for i in range(32)]
        for k1 in range(64):
            q = k1 % 3
            P.dma("sp", Zl[q], z_scr[k1:k1 + 65:64].rearrange("r n c -> n r c"), zall, [("Zl", q)])
            for ch3 in range(3):
                px_ = k.bank(2 + (k1 * 3 + ch3) % 2)[:, 0:256]
                pk_ = ("px5", (k1 * 3 + ch3) % 2)
                P.mm(px_, Zl[q][:, 0, ch3 * 128:(ch3 + 1) * 128], T2[:, 0, :], True, False, [("Zl", q), "T2"], [pk_])
                P.mm(px_, Zl[q][:, 1, ch3 * 128:(ch3 + 1) * 128], T2[:, 1, :], False, True, [("Zl", q), "T2"], [pk_])
                for ri in range(2):
                    P.cp("act" if ri else "dve", XT[:, ri * 3 + ch3, k1:k1 + 64 * 127 + 1:64], px_[:, ri * 128:(ri + 1) * 128], [pk_], ["XT"])
        P.dma("sp", xT_scr.rearrange("c p t -> p c t"), XT, ["XT"], ["xT_scr"])
        P.barrier()

    if 6 in phases:
        A.reset()
        BW = 256
        TPB = BW // 128
        Wg = A.alloc([8, 3072], BF16) if gmode != "load" else None
        Wba = A.alloc([D], BF16)
        Wbs = A.alloc([4, D], BF16)
        Wbf = A.alloc([6, D], BF16)
        Wo = A.alloc([8, D], BF16)
        T256 = A.alloc([3, 2, 256], BF16)
        gbc = A.alloc([D], F32)
        gb = A.alloc([24], F32)
        xt = [A.alloc([2, D], F32) for _ in range(2)]
        hT = [A.alloc([8, BW], BF16) for _ in range(2)]
        junk = A.alloc([D], BF16)
        ss = A.alloc([8], F32)
        rs = A.alloc([8], F32)
        hb = [A.alloc([D], BF16) for _ in range(2)]
        Gs = [A.alloc([24, BW], BF16) for _ in range(2 if gmode == "load" else 1)]
        oTb = [A.alloc([BW], BF16) for _ in range(2)]
        ynTb = [A.alloc([4, BW], BF16) for _ in range(2)]
        xTb = [A.alloc([6, BW], BF16) for _ in range(2)]
        fo = A.alloc([6, BW], BF16)
        ta = [A.alloc([BW], F32) for _ in range(2)]
        tb_ = [A.alloc([BW], F32) for _ in range(2)]
        tc_ = [A.alloc([BW], F32) for _ in range(2)]
        mT = A.alloc([8, BW], BF16)
        osb = [A.alloc([D], F32) for _ in range(2)]
        for kc in range(8):
            if gmode != "load":
                load_cast(k, Wg[:, kc, :], I["wg"][kc * 128:(kc + 1) * 128, :], 3072, "Wg")
            load_cast(k, Wo[:, kc, :], I["wo"][kc * 128:(kc + 1) * 128, :], D, "Wo")
        load_cast(k, Wba, I["wba"], D, "Wb")
        for kc in range(4):
            load_cast(k, Wbs[:, kc, :], I["wbs"][kc * 128:(kc + 1) * 128, :], D, "Wb")
        for kc in range(6):
            load_cast(k, Wbf[:, kc, :], I["wbf"][kc * 128:(kc + 1) * 128, :], D, "Wb")
        load_cast(k, T256.rearrange("p a b c -> p (a b c)"), I["T256"], 1536, "T256")
        P.dma("sp", gbc, I["g_mix"], [], ["gbc"])
        P.dma("sp", gb, I["gate_b"], [], ["gb"])
        gen = [1, 2]
        gi = 0
        for b in range(S // BW):
            pb = b % 2
            bs = slice(b * BW, (b + 1) * BW)
            G = Gs[pb % len(Gs)]
            gk_ = ("G", pb % len(Gs))
            P.dma("sp", oTb[pb], oT_scr[:, bs], [], [("oTb", pb)])
            P.dma("sp", ynTb[pb], ynT_scr[:, :, bs].rearrange("c p t -> p c t"), [], [("ynTb", pb)])
            P.dma("sp", xTb[pb], xT_scr[:, :, bs].rearrange("c p t -> p c t"), [], [("xTb", pb)])
            if gmode == "load":
                P.dma("sp", G, g_scr[:, :, bs].rearrange("c p t -> p c t"), [], [gk_])
            else:
                P.dma("sp", xt[pb], I["x"][bs, :].rearrange("(t p) d -> p t d", p=128), [], [("xt", pb)])
                for t in range(TPB):
                    norm_transpose(k, xt[pb][:, t, :], ("xt", pb), gbc, c, hT[pb], ("hT", pb), t * 128,
                                   (junk, ss[:, t:t + 1], rs[:, t:t + 1], hb[t % 2]), "n%d" % (t % 2), 0, "pT", t % 2)
                hk = ("hT", pb)
                for gc in range(24):
                    g_ = gen[gi % 2]; gi += 1
                    pg = k.bank(g_)[:, 0:BW]
                    for kc in range(8):
                        P.mm(pg, Wg[:, kc, gc * 128:(gc + 1) * 128], hT[pb][:, kc, :], kc == 0, kc == 7, ["Wg", hk], [("pg", g_)])
                    P.act(G[:, gc, :], pg, AF.Sigmoid, [("pg", g_), "gb"], [gk_], bias=gb[:, gc:gc + 1])
                if gmode == "store":
                    P.dma("sp", g_scr[:, :, bs].rearrange("c p t -> p c t"), G, [gk_], [("g_scr", b)])
            for i3 in range(3):
                for kcc in range(2):
                    g_ = gen[gi % 2]; gi += 1
                    pf_ = k.bank(g_)[:, 0:BW]
                    P.mm(pf_, T256[:, i3, 0, kcc * 128:(kcc + 1) * 128], xTb[pb][:, i3, :], True, False, ["T256", ("xTb", pb)], [("pg", g_)])
                    P.mm(pf_, T256[:, i3, 1, kcc * 128:(kcc + 1) * 128], xTb[pb][:, 3 + i3, :], False, True, ["T256", ("xTb", pb)], [("pg", g_)])
                    P.cp("dve", fo[:, i3 * 2 + kcc, :], pf_, [("pg", g_)], ["fo"])
            for dc in range(8):
                q = dc % 2
                ds_ = slice(dc * 128, (dc + 1) * 128)
                pya, pys, pyf = k.bank(3)[:, 0:BW], k.bank(4)[:, 0:BW], k.bank(5)[:, 0:BW]
                P.mm(pya, Wba[:, ds_], oTb[pb], True, True, ["Wb", ("oTb", pb)], ["pya"])
                for kc in range(4):
                    P.mm(pys, Wbs[:, kc, ds_], ynTb[pb][:, kc, :], kc == 0, kc == 3, ["Wb", ("ynTb", pb)], ["pys"])
                for kc in range(6):
                    P.mm(pyf, Wbf[:, kc, ds_], fo[:, kc, :], kc == 0, kc == 5, ["Wb", "fo"], ["pyf"])
                P.tt("dve", ta[q], pya, G[:, dc, :], ALU.mult, ["pya", gk_], [("ta", q)])
                P.tt("dve", tb_[q], pys, G[:, 8 + dc, :], ALU.mult, ["pys", gk_], [("tb", q)])
                P.tt("dve", tc_[q], pyf, G[:, 16 + dc, :], ALU.mult, ["pyf", gk_], [("tc", q)])
                P.tt("pool", ta[q], ta[q], tb_[q], ALU.add, [("ta", q), ("tb", q)], [("ta", q)])
                P.tt("pool", mT[:, dc, :], ta[q], tc_[q], ALU.add, [("ta", q), ("tc", q)], ["mT"])
            for t in range(TPB):
                q = t % 2
                for half in range(2):
                    po_ = k.bank(6 + half)
                    for kc in range(8):
                        P.mm(po_, mT[:, kc, t * 128:(t + 1) * 128], Wo[:, kc, half * 512:(half + 1) * 512], kc == 0, kc == 7, ["mT", "Wo"], [("po6", half)])
                    P.cp("act" if half else "dve", osb[q][:, half * 512:(half + 1) * 512], po_, [("po6", half)], [("osb", q)])
                P.dma("sp", O[b * BW + t * 128:b * BW + (t + 1) * 128, :], osb[q], [("osb", q)], [("O", b, t)])
        P.barrier()


TS = 2048
TT = TS // 128


def rms_feature_major(k, c, praw, keys_raw, g2, gcol0, out3, okey, sqb, lnb, tag, width, pss_bank, inv_n):
    P = k.P
    for ec in range(2):
        P.act(sqb[ec], praw[ec], AF.Square, [keys_raw[ec]], [(tag, "sq", ec)])
    pss = k.bank(pss_bank)[:, 0:width]
    for ec in range(2):
        P.mm(pss, c["ones"], sqb[ec], ec == 0, ec == 1, [(tag, "sq", ec), "consts"], [(tag, "pss")])
    P.act(lnb, pss, AF.Ln, [(tag, "pss"), "consts"], [(tag, "ln")], bias=c["eps"], scale=inv_n)
    P.act(lnb, lnb, AF.Exp, [(tag, "ln")], [(tag, "ln")], scale=-0.5)
    for ec in range(2):
        P.stt(out3[:, ec, :], praw[ec], g2[:, gcol0 + ec:gcol0 + ec + 1], lnb, ALU.mult, ALU.mult,
              [keys_raw[ec], "gx", (tag, "ln")], [okey])


def build_tok(k, I, O, c, parts_aps):
    nc, P, A = k.nc, k.P, k.A
    A.reset()
    base0 = A.base
    xr = A.alloc([TT, D], F32)
    kTn = A.alloc([4, 2, 256], BF16)
    vext = A.alloc([2, 4, 260], BF16)
    gx = A.alloc([4], F32)
    A.persist()
    P.dma("sp", gx, I["gxqk"], [], ["gx"])
    P.ts("dve", gx[:, 0:2], gx[:, 0:2], 1.0 / 16.0, ALU.mult, ["gx"], ["gx"])
    ptmp = [A.alloc([4, D], F32) for _ in range(2)]
    P.dma("sp", xr, I["xres"].rearrange("(t p) d -> p t d", p=128), [], [("xr", t) for t in range(TT)])
    ci = 0
    for pa in parts_aps:
        for t4 in range(TT // 4):
            q = ci % 2
            P.dma("sp", ptmp[q], pa[t4 * 512:(t4 + 1) * 512, :].rearrange("(t p) d -> p t d", p=128), [], [("ptmp", q)])
            kk = [("xr", t) for t in range(t4 * 4, t4 * 4 + 4)]
            P.tt("dve" if ci % 2 else "pool", xr[:, t4 * 4:(t4 + 1) * 4, :], xr[:, t4 * 4:(t4 + 1) * 4, :], ptmp[q], ALU.add, kk + [("ptmp", q)], kk)
            ci += 1
    P.barrier()
    A.reset()
    Wk = A.alloc([8, D], BF16)
    Wv = A.alloc([8, D], BF16)
    gm = A.alloc([D], F32)
    mt = A.alloc([2, D], F32)
    mT = A.alloc([8, 256], BF16)
    junk = A.alloc([D], BF16)
    ss = A.alloc([8], F32)
    rs = A.alloc([8], F32)
    hb = [A.alloc([D], BF16) for _ in range(2)]
    sqb = [A.alloc([512], BF16) for _ in range(2)]
    lnb = A.alloc([512], F32)
    for kc in range(8):
        load_cast(k, Wk[:, kc, :], I["wxk"][kc * 128:(kc + 1) * 128, :], D, "Wk")
        load_cast(k, Wv[:, kc, :], I["wxv"][kc * 128:(kc + 1) * 128, :], D, "Wv")
    P.dma("sp", gm, I["g_m"], [], ["gbc"])
    P.dma("sp", mt, I["mem"].rearrange("(t p) d -> p t d", p=128), [], ["mt"])
    P.memset("pool", vext, 0.0, ["vext"])
    P.memset("pool", vext[:, :, :, 256:257], 1.0, ["vext"])
    for t in range(2):
        norm_transpose(k, mt[:, t, :], "mt", gm, c, mT, "mT", t * 128,
                       (junk, ss[:, t:t + 1], rs[:, t:t + 1], hb[t % 2]), "n%d" % (t % 2), 0, "pT", t % 2)
    for hh in range(4):
        praw = []
        for ec in range(2):
            pk_ = k.bank(1 + ec)[:, 0:256]
            for kc in range(8):
                P.mm(pk_, Wk[:, kc, hh * 256 + ec * 128:hh * 256 + (ec + 1) * 128], mT[:, kc, :], kc == 0, kc == 7, ["Wk", "mT"], [("praw", ec)])
            praw.append(pk_)
        rms_feature_major(k, c, praw, [("praw", 0), ("praw", 1)], gx, 2, kTn[:, hh, :, :], "kTn",
                          [sqb[0][:, 0:256], sqb[1][:, 0:256]], lnb[:, 0:256], "k", 256, 3, 1.0 / 256)
    for mtile in range(2):
        for half in range(2):
            pv = k.bank(4 + half)
            for kc in range(8):
                P.mm(pv, mT[:, kc, mtile * 128:(mtile + 1) * 128], Wv[:, kc, half * 512:(half + 1) * 512], kc == 0, kc == 7, ["Wv", "mT"], [("pv", half)])
            P.cp("dve", vext[:, mtile, half * 2:half * 2 + 2, 0:256], pv.rearrange("p (h e) -> p h e", h=2), [("pv", half)], ["vext"])
    P.barrier()
    A.reset()
    Wq = A.alloc([8, D], BF16)
    Wo = A.alloc([8, D], BF16)
    gxb = A.alloc([D], F32)
    hT = A.alloc([8, 512], BF16)
    junk = A.alloc([D], BF16)
    ss = A.alloc([8], F32)
    rs = A.alloc([8], F32)
    hb = [A.alloc([D], BF16) for _ in range(2)]
    sqb = [A.alloc([512], BF16) for _ in range(2)]
    lnb = A.alloc([512], F32)
    qTn = A.alloc([2, 512], BF16)
    PT = A.alloc([2, 512], BF16)
    rden = [A.alloc([1], F32) for _ in range(2)]
    osb = A.alloc([4, D], BF16)
    oT = A.alloc([8, 512], BF16)
    for kc in range(8):
        load_cast(k, Wq[:, kc, :], I["wxq"][kc * 128:(kc + 1) * 128, :], D, "Wq")
        load_cast(k, Wo[:, kc, :], I["wxo"][kc * 128:(kc + 1) * 128, :], D, "Wo")
    P.dma("sp", gxb, I["g_x"], [], ["gbc"])
    for b in range(TS // 512):
        for t in range(4):
            norm_transpose(k, xr[:, b * 4 + t, :], ("xr", b * 4 + t), gxb, c, hT, "hT", t * 128,
                           (junk, ss[:, t:t + 1], rs[:, t:t + 1], hb[t % 2]), "n%d" % (t % 2), 0, "pT", t % 2)
        for hh in range(4):
            praw = []
            for ec in range(2):
                pq = k.bank(1 + ec)
                for kc in range(8):
                    P.mm(pq, Wq[:, kc, hh * 256 + ec * 128:hh * 256 + (ec + 1) * 128], hT[:, kc, :], kc == 0, kc == 7, ["Wq", "hT"], [("praw", ec)])
                praw.append(pq)
            rms_feature_major(k, c, praw, [("praw", 0), ("praw", 1)], gx, 0, qTn, "qTn", sqb, lnb, "q", 512, 3, 1.0 / 256)
            for mtile in range(2):
                pl = k.bank(4 + mtile)
                for ec in range(2):
                    P.mm(pl, kTn[:, hh, ec, mtile * 128:(mtile + 1) * 128], qTn[:, ec, :], ec == 0, ec == 1, ["kTn", "qTn"], [("pl", mtile)])
                P.act(PT[:, mtile, :], pl, AF.Exp, [("pl", mtile)], [("PT", mtile)])
            for t in range(4):
                q = t % 2
                po = k.bank(6 + q)[:, 0:257]
                for mtile in range(2):
                    P.mm(po, PT[:, mtile, t * 128:(t + 1) * 128], vext[:, mtile, hh, 0:257], mtile == 0, mtile == 1, [("PT", mtile), "vext"], [("po", q)])
                P.recip(rden[q], po[:, 256:257], [("po", q)], [("rden", q)])
                P.ts("dve", osb[:, t, hh * 256:(hh + 1) * 256], po[:, 0:256], rden[q], ALU.mult, [("po", q), ("rden", q)], [("osb", t)])
        for t in range(4):
            p16 = k.bank16(0)
            for kc in range(8):
                P.tr(p16[:, kc * 128:(kc + 1) * 128], osb[:, t, kc * 128:(kc + 1) * 128], c["ident"], [("osb", t), "consts"], ["pT"])
            P.cp("act" if t % 2 else "dve", oT[:, :, t * 128:(t + 1) * 128], p16.rearrange("p (a b) -> p a b", a=8), ["pT"], ["oT"])
        for t in range(4):
            for half in range(2):
                po2 = k.bank(1 + half)
                for kc in range(8):
                    P.mm(po2, oT[:, kc, t * 128:(t + 1) * 128], Wo[:, kc, half * 512:(half + 1) * 512], kc == 0, kc == 7, ["oT", "Wo"], [("praw", half)])
                xs_ = xr[:, b * 4 + t, half * 512:(half + 1) * 512]
                P.tt("dve", xs_, po2, xs_, ALU.add, [("praw", half), ("xr", b * 4 + t)], [("xr", b * 4 + t)])
    P.barrier()
    A.reset()
    Wd = A.alloc([22, D], BF16)
    gfb = A.alloc([D], F32)
    hT = A.alloc([8, 512], BF16)
    junk = A.alloc([D], BF16)
    ss = A.alloc([8], F32)
    rs = A.alloc([8], F32)
    hb = [A.alloc([D], BF16) for _ in range(2)]
    Wgc = [A.alloc([8, 128], BF16) for _ in range(2)]
    Wuc = [A.alloc([8, 128], BF16) for _ in range(2)]
    sg = [A.alloc([512], F32) for _ in range(2)]
    actT = A.alloc([22, 512], BF16)
    xo = [A.alloc([D], F32) for _ in range(2)]
    for fc in range(22):
        load_cast(k, Wd[:, fc, :], I["wfd"][fc * 128:(fc + 1) * 128, :], D, "Wd")
    P.dma("sp", gfb, I["g_f"], [], ["gbc"])
    wi = 0
    for b in range(TS // 512):
        for t in range(4):
            norm_transpose(k, xr[:, b * 4 + t, :], ("xr", b * 4 + t), gfb, c, hT, "hT", t * 128,
                           (junk, ss[:, t:t + 1], rs[:, t:t + 1], hb[t % 2]), "n%d" % (t % 2), 0, "pT", t % 2)
        for fc in range(22):
            q = wi % 2; wi += 1
            P.dma("pool", Wgc[q], I["wfg"][:, fc * 128:(fc + 1) * 128].rearrange("(kc p) c -> p kc c", p=128), [], [("Wgc", q)])
            P.dma("pool", Wuc[q], I["wfu"][:, fc * 128:(fc + 1) * 128].rearrange("(kc p) c -> p kc c", p=128), [], [("Wuc", q)])
            pg = k.bank(1 + q)
            pu = k.bank(3 + q)
            for kc in range(8):
                P.mm(pg, Wgc[q][:, kc, :], hT[:, kc, :], kc == 0, kc == 7, [("Wgc", q), "hT"], [("pgf", q)])
            for kc in range(8):
                P.mm(pu, Wuc[q][:, kc, :], hT[:, kc, :], kc == 0, kc == 7, [("Wuc", q), "hT"], [("puf", q)])
            P.act(sg[q], pg, AF.Silu, [("pgf", q)], [("sg", q)])
            P.tt("dve", actT[:, fc, :], pu, sg[q], ALU.mult, [("puf", q), ("sg", q)], ["actT"])
        for t in range(4):
            q = t % 2
            for half in range(2):
                pd = k.bank(5 + half)
                for fc in range(22):
                    P.mm(pd, actT[:, fc, t * 128:(t + 1) * 128], Wd[:, fc, half * 512:(half + 1) * 512], fc == 0, fc == 21, ["actT", "Wd"], [("pd", half)])
                P.tt("dve", xo[q][:, half * 512:(half + 1) * 512], pd, xr[:, b * 4 + t, half * 512:(half + 1) * 512], ALU.add,
                     [("pd", half), ("xr", b * 4 + t)], [("xo", q)])
            P.dma("sp", O[(b * 4 + t) * 128:(b * 4 + t + 1) * 128, :], xo[q], [("xo", q)], [("O", b, t)])
    P.barrier()
    A.base = base0


def tok_inputs(inp, l, b, j, xres, parts):
    d = {}
    d["xres"] = np.ascontiguousarray(xres)
    if parts is not None:
        d["parts"] = np.ascontiguousarray(parts)
    d["mem"] = np.ascontiguousarray(inp["mem"][b])
    d["g_x"] = rep(inp["xattn_norm_g"][l])
    d["g_m"] = rep(inp["mem_norm_g"][l])
    d["g_f"] = rep(inp["ffn_norm_g"][l])
    gq = inp["xattn_q_norm_g"][l].reshape(2, 128).T
    gk = inp["xattn_k_norm_g"][l].reshape(2, 128).T
    d["gxqk"] = np.ascontiguousarray(np.concatenate([gq, gk], axis=1).astype(np.float32))
    for nm, key in (("wxq", "w_xq"), ("wxk", "w_xk"), ("wxv", "w_xv"), ("wxo", "w_xo"), ("wfg", "w_ffn_gate"), ("wfu", "w_ffn_up"), ("wfd", "w_ffn_down")):
        d[nm] = np.ascontiguousarray(inp[key][l])
    hc = host_consts()
    for nm in ("ident", "trif", "trib", "maskf", "maskb"):
        d[nm] = hc[nm]
    return d


TOK_IN = dict(xres=[TS, D], parts=[4, TS, D], mem=[256, D], g_x=[128, D], g_m=[128, D], g_f=[128, D], gxqk=[128, 4],
              wxq=[D, D], wxk=[D, D], wxv=[D, D], wxo=[D, D], wfg=[D, DFF], wfu=[D, DFF], wfd=[DFF, D],
              ident=[128, 128], trif=[128, 128], trib=[128, 128], maskf=[128, 128], maskb=[128, 128])


def build_tok_program():
    nc = bass.Bass("TRN2", target_bir_lowering=False)
    I = {n: nc.dram_tensor(n, shp, F32, kind="ExternalInput").ap() for n, shp in TOK_IN.items()}
    O = nc.dram_tensor("xout", [TS, D], F32, kind="ExternalOutput").ap()
    k = K(nc)
    c = mk_consts(k, I)
    k.A.persist()
    build_tok(k, I, O, c, [I["parts"][i] for i in range(4)])
    info = k.P.emit()
    return nc, info


def t5_bucket(rel):
    half_b, exact = 16, 8
    dist = np.abs(rel)
    log_ratio = np.log(np.maximum(dist, 1) / exact) / np.log(1024 / exact)
    far = np.minimum(exact + (log_ratio * (half_b - exact)).astype(np.int32), half_b - 1)
    return np.where(rel > 0, half_b, 0) + np.where(dist < exact, dist, far)


def host_consts():
    i = np.arange(128)
    c = {}
    c["ident"] = np.eye(128, dtype=np.float32)
    c["trif"] = (i[:, None] <= i[None, :]).astype(np.float32)
    c["trib"] = (i[:, None] >= i[None, :]).astype(np.float32)
    c["maskf"] = np.where(i[:, None] <= i[None, :], 0.0, NEG).astype(np.float32)
    c["maskb"] = np.where(i[:, None] >= i[None, :], 0.0, NEG).astype(np.float32)
    n1 = np.arange(64)[:, None, None]
    n2 = np.arange(128)[None, :, None]
    k1 = np.arange(64)[None, None, :]
    ang = 2 * np.pi * ((k1 * (128 * n1 + n2)) % 8192) / 8192.0
    E = np.concatenate([np.cos(ang), -np.sin(ang)], axis=2) / 8.0
    c["E1"] = E.reshape(64, 128 * 128).astype(np.float32)
    a2 = 2 * np.pi * ((i[:, None] * i[None, :]) % 128) / 128.0
    C2, S2 = np.cos(a2) / math.sqrt(128), np.sin(a2) / math.sqrt(128)
    c["T2"] = np.concatenate([C2, -S2, S2, C2], axis=1).astype(np.float32)
    return c


def core_consts(j):
    kc = np.arange(256)[None, :]
    out = np.zeros((128, 3, 2, 256), np.float32)
    for i3 in range(3):
        hg = 3 * j + i3
        cidx = (hg % 2) * 128 + np.arange(128)[:, None]
        ang = 2 * np.pi * ((cidx * kc) % 256) / 256.0
        out[:, i3, 0, :] = np.cos(ang) / 16.0
        out[:, i3, 1, :] = np.sin(ang) / 16.0
    return out.reshape(128, 1536)


def rep(v, n=128):
    return np.ascontiguousarray(np.broadcast_to(np.asarray(v, np.float32).reshape(1, -1), (n, np.asarray(v).size)))


def mixer_inputs(inp, l, j, xb):
    w_in = inp["w_in"][l]
    cols = []
    for base in (0, 1536):
        for g in range(3):
            cols.append(np.arange(base + (g * 4 + j) * 128, base + (g * 4 + j + 1) * 128))
    for g in range(3):
        cols.append(np.arange(3072 + (g * 4 + j) * 128, 3072 + (g * 4 + j + 1) * 128))
    cols.append(np.arange(4608 + j * 512, 4608 + (j + 1) * 512))
    cols.append(np.arange(6656 + j * 512, 6656 + (j + 1) * 512))
    cols.append(np.arange(6656 + 2048 + j * 128, 6656 + 2048 + (j + 1) * 128))
    cols.append(np.arange(6656 + 2560 + j * 128, 6656 + 2560 + (j + 1) * 128))
    cols.append(np.arange(9728 + j * 8, 9728 + (j + 1) * 8))
    cols.append(np.arange(9728 + 32 + j * 8, 9728 + 32 + (j + 1) * 8))
    cols.append(np.arange(9792 + j * 384, 9792 + (j + 1) * 384))
    cols = np.concatenate(cols)
    assert cols.size == W1N
    d = {}
    d["x"] = xb
    d["w1"] = np.ascontiguousarray(w_in[:, cols])
    d["wg"] = np.ascontiguousarray(w_in[:, 11328:14400])
    d["g_mix"] = rep(inp["mix_norm_g"][l])
    d["gqk"] = np.ascontiguousarray(np.stack([inp["attn_q_norm_g"][l], inp["attn_k_norm_g"][l]], axis=1).astype(np.float32))
    d["dtb"] = rep(np.concatenate([inp["dt_bias"][l][0, j * 8:(j + 1) * 8], inp["dt_bias"][l][1, j * 8:(j + 1) * 8]]))
    d["alog"] = rep(np.concatenate([inp["a_log"][l][0, j * 8:(j + 1) * 8], inp["a_log"][l][1, j * 8:(j + 1) * 8]]))
    d["dskip"] = rep(inp["d_skip"][l][j * 8:(j + 1) * 8])
    d["ssd_g"] = rep(inp["ssd_norm_g"][l][j * 512:(j + 1) * 512])
    xbc_ch = np.concatenate([np.arange(j * 512, (j + 1) * 512), np.arange(2048 + j * 128, 2048 + (j + 1) * 128),
                             np.arange(2560 + j * 128, 2560 + (j + 1) * 128)])
    cwj = inp["conv_w"][l][:, xbc_ch]
    d["cw"] = np.ascontiguousarray(cwj.reshape(7, 6, 128).transpose(2, 1, 0).reshape(128, 42))
    d["cb"] = np.ascontiguousarray(inp["conv_b"][l][xbc_ch].reshape(6, 128).T)
    kk = np.arange(128)[:, None]
    qq = np.arange(128)[None, :]
    bt = np.zeros((128, 9, 128), np.float32)
    for g, dil in enumerate((1, 4, 16)):
        for di, dlt in enumerate((-1, 0, 1)):
            rel = kk + 128 * dlt - qq
            vals = inp["rel_bias"][t5_bucket(rel * dil), g * 4 + j]
            bt[:, g * 3 + di, :] = np.where(np.abs(rel) <= 64, vals, NEG)
    d["biasT"] = bt.reshape(128, 9 * 128)
    d["wba"] = np.ascontiguousarray(inp["w_branch_attn"][l][j * 128:(j + 1) * 128, :])
    d["wbs"] = np.ascontiguousarray(inp["w_branch_ssd"][l][j * 512:(j + 1) * 512, :])
    wbf = inp["w_branch_fourier"][l]
    d["wbf"] = np.ascontiguousarray(np.concatenate([wbf[((3 * j + i3) // 2) * 256:((3 * j + i3) // 2 + 1) * 256, :] for i3 in range(3)], axis=0))
    d["wo"] = np.ascontiguousarray(inp["w_mix_out"][l])
    d["gate_b"] = np.ascontiguousarray(inp["gate_bias"][l].reshape(24, 128).T)
    d["T256"] = core_consts(j)
    d.update(host_consts())
    return d


MIX_IN = dict(x=[S, D], w1=[D, W1N], wg=[D, 3072], g_mix=[128, D], gqk=[128, 2], dtb=[128, 16], alog=[128, 16], dskip=[128, 8],
              ssd_g=[128, 512], cw=[128, 42], cb=[128, 6], biasT=[128, 1152], wba=[128, D], wbs=[512, D], wbf=[768, D], wo=[D, D],
              gate_b=[128, 24], T256=[128, 1536], ident=[128, 128], trif=[128, 128], trib=[128, 128], maskf=[128, 128],
              maskb=[128, 128], E1=[64, 16384], T2=[128, 512])


def build_mixer_program(dbg=None, phases=(1, 2, 3, 4, 5, 6), gmode="compute"):
    nc = bass.Bass("TRN2", target_bir_lowering=False)
    I = {n: nc.dram_tensor(n, shp, F32, kind="ExternalInput").ap() for n, shp in MIX_IN.items()}
    I["cw"] = I["cw"].rearrange("p (a b) -> p a b", a=6)
    O = nc.dram_tensor("mix_out", [S, D], F32, kind="ExternalOutput").ap()
    k = K(nc)
    c = mk_consts(k, I)
    k.A.persist()
    build_mixer(k, I, O, c, dbg, phases, gmode=gmode)
    info = k.P.emit()
    return nc, info


CONST_NAMES = ("ident", "trif", "trib", "maskf", "maskb", "E1", "T2")
SLOT_DEP = ("w1", "dtb", "alog", "dskip", "ssd_g", "cw", "cb", "biasT", "wba", "wbs", "wbf")
LAYER_DEP = ("wg", "g_mix", "gqk", "wo", "gate_b")
TOK_LAYER = ("g_x", "g_m", "g_f", "gxqk", "wxq", "wxk", "wxv", "wxo", "wfg", "wfu", "wfd")


def build_fused_program():
    nc = bass.Bass("TRN2", target_bir_lowering=False)
    I = {}

    def ext(name, shp):
        I[name] = nc.dram_tensor(name, shp, F32, kind="ExternalInput").ap()

    ext("x", [S, D])
    ext("mem", [256, D])
    for n in CONST_NAMES:
        ext(n, MIX_IN[n])
    for j in range(4):
        ext("T256_%d" % j, MIX_IN["T256"])
    for l in range(2):
        for n in LAYER_DEP:
            ext("%s_%d" % (n, l), MIX_IN[n])
        for n in SLOT_DEP:
            for j in range(4):
                ext("%s_%d_%d" % (n, l, j), MIX_IN[n])
        for n in TOK_LAYER:
            ext("%s_%d" % (n, l), TOK_IN[n])
    out = nc.dram_tensor("xout", [S, D], F32, kind="ExternalOutput").ap()
    mixp = [nc.dram_tensor("mixp%d" % j, [S, D], F32).ap() for j in range(4)]
    xl0 = nc.dram_tensor("xl0", [S, D], F32).ap()
    k = K(nc)
    P = k.P
    c = mk_consts(k, I)
    k.A.persist()
    for l in range(2):
        xin = I["x"] if l == 0 else xl0
        for j in range(4):
            Im = {n: I[n] for n in CONST_NAMES}
            Im["T256"] = I["T256_%d" % j]
            Im["x"] = xin
            for n in LAYER_DEP:
                Im[n] = I["%s_%d" % (n, l)]
            for n in SLOT_DEP:
                Im[n] = I["%s_%d_%d" % (n, l, j)]
            Im["cw"] = Im["cw"].rearrange("p (a b) -> p a b", a=6)
            build_mixer(k, Im, mixp[j], c, gmode="store" if j == 0 else "load")
        It = {n: I[n] for n in CONST_NAMES[:5]}
        It["mem"] = I["mem"]
        for n in TOK_LAYER:
            It[n] = I["%s_%d" % (n, l)]
        xout_l = xl0 if l == 0 else out
        for s_ in range(4):
            rows = slice(s_ * TS, (s_ + 1) * TS)
            It["xres"] = xin[rows, :]
            build_tok(k, It, xout_l[rows, :], c, [mixp[j][rows, :] for j in range(4)])
    info = P.emit()
    return nc, info


def fused_inputs(inp, b):
    d = {}
    x = np.ascontiguousarray(inp["x"][b], dtype=np.float32)
    d["x"] = x
    for l in range(2):
        for j in range(4):
            m = mixer_inputs(inp, l, j, x)
            if l == 0:
                d["T256_%d" % j] = m["T256"]
            if j == 0:
                for n in LAYER_DEP:
                    d["%s_%d" % (n, l)] = m[n]
                if l == 0:
                    for n in CONST_NAMES:
                        d[n] = m[n]
            for n in SLOT_DEP:
                d["%s_%d_%d" % (n, l, j)] = m[n]
        t = tok_inputs(inp, l, b, 0, x[:TS], None)
        if l == 0:
            d["mem"] = t["mem"]
        for n in TOK_LAYER:
            d["%s_%d" % (n, l)] = t[n]
    return d


_PROGS = {}
FUSED = False


def _prog(name):
    if name not in _PROGS:
        _PROGS[name] = {"mix": build_mixer_program, "tok": build_tok_program, "fused": build_fused_program}[name]()[0]
    return _PROGS[name]


def kernel(**inp):
    inp = {k_: np.asarray(v) for k_, v in inp.items()}
    cores = list(range(8))
    if FUSED:
        per_b = [fused_inputs(inp, b) for b in range(2)]
        ins = [per_b[cid // 4] for cid in cores]
        res = run_bass_kernel_spmd(_prog("fused"), ins, core_ids=cores)
        x = np.stack([np.concatenate([np.asarray(res.results[b * 4 + j]["xout"])[j * TS:(j + 1) * TS] for j in range(4)], axis=0)
                      for b in range(2)], axis=0)
        return np.ascontiguousarray(x, dtype=np.float32)
    x = np.ascontiguousarray(inp["x"], dtype=np.float32)
    for l in range(2):
        ins = [mixer_inputs(inp, l, cid % 4, x[cid // 4]) for cid in cores]
        res = run_bass_kernel_spmd(_prog("mix"), ins, core_ids=cores)
        part = [np.asarray(res.results[cid]["mix_out"]) for cid in cores]
        ins = []
        for cid in cores:
            b, j = cid // 4, cid % 4
            sl = slice(j * TS, (j + 1) * TS)
            parts = np.stack([part[b * 4 + i][sl] for i in range(4)], axis=0)
            ins.append(tok_inputs(inp, l, b, j, x[b, sl], parts))
        res = run_bass_kernel_spmd(_prog("tok"), ins, core_ids=cores)
        x = np.stack([np.concatenate([np.asarray(res.results[b * 4 + j]["xout"]) for j in range(4)], axis=0) for b in range(2)], axis=0)
        x = np.ascontiguousarray(x, dtype=np.float32)
    return x
```

```python
import math
import numpy as np
import concourse.bass as bass
import concourse.mybir as mybir
from concourse.bass_utils import run_bass_kernel_spmd

F32 = mybir.dt.float32
BF16 = mybir.dt.bfloat16
AF = mybir.ActivationFunctionType
ALU = mybir.AluOpType

S = 8192
D = 1024
NT = S // 128
NB = S // 512
DFF = 2816
EPS = 1e-6
NEG = -30000.0

ENGS = ("pe", "act", "dve", "pool", "sp")
NDMASEM = 8


class Op:
    __slots__ = ("eng", "fn", "deps", "dma", "sig", "needed", "idx")

    def __init__(self, eng, fn, dma):
        self.eng, self.fn, self.dma = eng, fn, dma
        self.deps = set()
        self.sig = None
        self.needed = False


class Prog:
    def __init__(self, nc):
        self.nc = nc
        self.ops = []
        self.last_w = {}
        self.readers = {}

    def op(self, eng, fn, reads=(), writes=(), dma=False):
        o = Op(eng, fn, dma)
        o.idx = len(self.ops)
        for k in reads:
            w = self.last_w.get(k)
            if w is not None:
                o.deps.add(w)
        for k in writes:
            w = self.last_w.get(k)
            if w is not None:
                o.deps.add(w)
            for r in self.readers.get(k, ()):
                o.deps.add(r)
        for k in reads:
            self.readers.setdefault(k, []).append(o.idx)
        for k in writes:
            self.last_w[k] = o.idx
            self.readers[k] = []
        o.deps.discard(o.idx)
        self.ops.append(o)
        return o

    def mm(self, out, lhsT, rhs, start, stop, r, w):
        return self.op("pe", lambda e: e.matmul(out, lhsT, rhs, start=start, stop=stop), r, w)

    def tr(self, out, in_, ident, r, w):
        return self.op("pe", lambda e: e.transpose(out, in_, ident), r, w)

    def dma(self, eng, out, in_, r, w, **kw):
        return self.op(eng, lambda e: e.dma_start(out=out, in_=in_, **kw), r, w, dma=True)

    def act(self, out, in_, func, r, w, bias=None, scale=None, accum=None):
        kw = {}
        if bias is not None:
            kw["bias"] = bias
        if scale is not None:
            kw["scale"] = scale
        if accum is not None:
            kw["accum_out"] = accum
        return self.op("act", lambda e: e.activation(out=out, in_=in_, func=func, **kw), r, w)

    def tt(self, eng, out, in0, in1, op, r, w):
        return self.op(eng, lambda e: e.tensor_tensor(out=out, in0=in0, in1=in1, op=op), r, w)

    def ts(self, eng, out, in0, s1, op0, r, w, s2=None, op1=None):
        if op1 is None:
            return self.op(eng, lambda e: e.tensor_scalar(out=out, in0=in0, scalar1=s1, scalar2=None, op0=op0), r, w)
        return self.op(eng, lambda e: e.tensor_scalar(out=out, in0=in0, scalar1=s1, scalar2=s2, op0=op0, op1=op1), r, w)

    def stt(self, out, in0, scalar, in1, op0, op1, r, w):
        return self.op("dve", lambda e: e.scalar_tensor_tensor(out=out, in0=in0, scalar=scalar, in1=in1, op0=op0, op1=op1), r, w)

    def cp(self, eng, out, in_, r, w):
        if eng == "act":
            return self.op("act", lambda e: e.copy(out=out, in_=in_), r, w)
        return self.op(eng, lambda e: e.tensor_copy(out=out, in_=in_), r, w)

    def memset(self, eng, ap, val, w):
        return self.op(eng, lambda e: e.memset(ap, val), (), w)

    def recip(self, out, in_, r, w):
        return self.op("dve", lambda e: e.reciprocal(out=out, in_=in_), r, w)

    def barrier(self):
        deps = set()
        for eng in ENGS:
            last_c = None
            dm = []
            for o in self.ops:
                if o.eng != eng or o.fn is None:
                    continue
                if o.dma:
                    dm.append(o.idx)
                else:
                    last_c = o.idx
            if last_c is not None:
                deps.add(last_c)
            deps.update(dm[-NDMASEM:])
        for eng in ENGS:
            o = Op(eng, None, False)
            o.idx = len(self.ops)
            o.deps = set(deps)
            self.ops.append(o)
        self.last_w = {}
        self.readers = {}

    def emit(self):
        nc = self.nc
        ops = self.ops
        for o in ops:
            if o.eng == "pe" and not o.dma:
                o.deps = {d for d in o.deps if not (ops[d].eng == "pe" and not ops[d].dma)}
            for d in o.deps:
                ops[d].needed = True
        sems = {e: nc.alloc_semaphore("s_" + e) for e in ENGS}
        dsems = {e: [nc.alloc_semaphore("d_%s%d" % (e, i)) for i in range(NDMASEM)] for e in ENGS}
        cnt = {e: 0 for e in ENGS}
        dcnt = {e: 0 for e in ENGS}
        for o in ops:
            if o.fn is None:
                continue
            if o.dma:
                i = dcnt[o.eng]
                dcnt[o.eng] += 1
                o.sig = (dsems[o.eng][i % NDMASEM], 16 * (i // NDMASEM + 1))
            elif o.needed:
                cnt[o.eng] += 1
                o.sig = (sems[o.eng], cnt[o.eng])
        per = {e: [o for o in ops if o.eng == e] for e in ENGS}

        def run(eng, e):
            known = {}
            for o in per[eng]:
                need = {}
                for d in o.deps:
                    s = ops[d].sig
                    if s is None:
                        continue
                    sem, val = s
                    if need.get(id(sem), (None, 0))[1] < val:
                        need[id(sem)] = (sem, val)
                if o.dma:
                    sem, val = o.sig
                    if val - 16 > 0 and need.get(id(sem), (None, 0))[1] < val - 16:
                        need[id(sem)] = (sem, val - 16)
                for k, (sem, val) in need.items():
                    if known.get(k, 0) >= val:
                        continue
                    e.wait_ge(sem, val)
                    known[k] = val
                if o.fn is None:
                    continue
                inst = o.fn(e)
                if o.sig is not None:
                    inst.then_inc(o.sig[0], 16 if o.dma else 1)

        with nc.Block() as blk:
            @blk.tensor
            def _(e):
                run("pe", e)

            @blk.scalar
            def _(e):
                run("act", e)

            @blk.vector
            def _(e):
                run("dve", e)

            @blk.gpsimd
            def _(e):
                run("pool", e)

            @blk.sync
            def _(e):
                run("sp", e)
        return dict(n_ops=len(ops), cnt=cnt, dcnt=dcnt)


class Arena:
    def __init__(self, nc, nbytes):
        self.t = nc.alloc_sbuf_tensor("arena", [128, nbytes // 2], BF16)
        self.cap = nbytes // 2
        self.off = 0
        self.base = 0

    def persist(self):
        self.base = self.off

    def reset(self):
        self.off = self.base

    def alloc(self, shape, dt, parts=128):
        n = int(np.prod(shape))
        e16 = n * (2 if dt == F32 else 1)
        e16 = (e16 + 15) // 16 * 16
        assert self.off + e16 <= self.cap, ("SBUF arena overflow", self.off, e16, self.cap)
        v = self.t[0:parts, self.off:self.off + e16]
        self.off += e16
        if dt == F32:
            v = v.bitcast(F32)
        v = v[:, 0:n]
        if len(shape) == 2:
            v = v.rearrange("p (a b) -> p a b", a=shape[0])
        elif len(shape) == 3:
            v = v.rearrange("p (a b c) -> p a b c", a=shape[0], b=shape[1])
        return v


class K:
    def __init__(self, nc):
        self.nc = nc
        self.P = Prog(nc)
        self.A = Arena(nc, 184 * 1024)
        self.banks = [nc.alloc_psum_tensor("pb%d" % i, [128, 512], F32) for i in range(8)]
        self.uid = 0
        self.scr = {}

    def bank(self, i):
        return self.banks[i][:, :]

    def bank16(self, i):
        return self.banks[i][:, :].bitcast(BF16)

    def key(self, s):
        self.uid += 1
        return (s, self.uid)


def load_cast(k, dst, src, ncols, wkey):
    c = 0
    while c < ncols:
        n = min(2048, ncols - c)
        k.P.dma("pool", dst[:, c:c + n], src[:, c:c + n], [], [wkey])
        c += n


def norm_transpose(k, xtile, xkey, g_bc, consts, hT, hkey, col0, tmp, tkey, pbank, pkey, par):
    P = k.P
    junk, ss, rs, hb = tmp
    P.act(junk, xtile, AF.Square, [xkey], [tkey + "j", tkey + "ss"], accum=ss)
    P.act(rs, ss, AF.Sqrt, [tkey + "ss", "consts"], [tkey + "rs"], bias=consts["eps"], scale=1.0 / D)
    P.recip(rs, rs, [tkey + "rs"], [tkey + "rs"])
    P.stt(hb, xtile, rs, g_bc, ALU.mult, ALU.mult, [xkey, tkey + "rs", "gbc"], [tkey + "hb"])
    p16 = k.bank16(pbank)
    for kc in range(8):
        P.tr(p16[:, kc * 128:(kc + 1) * 128], hb[:, kc * 128:(kc + 1) * 128], consts["ident"], [tkey + "hb", "consts"], [pkey])
    P.cp("act" if par else "dve", hT[:, :, col0:col0 + 128], p16.rearrange("p (a b) -> p a b", a=8), [pkey], [hkey])


def mk_consts(k, cin):
    A, P = k.A, k.P
    c = {}
    c["ident"] = A.alloc([128], BF16)
    c["ones"] = A.alloc([128], BF16)
    c["eps"] = A.alloc([1], F32)
    c["one"] = A.alloc([1], F32)
    c["trif"] = A.alloc([128], BF16)
    c["trib"] = A.alloc([128], BF16)
    c["maskf"] = A.alloc([128], BF16)
    c["maskb"] = A.alloc([128], BF16)
    for nm in ("ident", "trif", "trib", "maskf", "maskb"):
        P.dma("pool", c[nm], cin[nm], [], ["consts"])
    P.memset("dve", c["ones"], 1.0, ["consts"])
    P.memset("dve", c["eps"], EPS, ["consts"])
    P.memset("dve", c["one"], 1.0, ["consts"])
    return c


QK0, V0, Z0, XBC0, DT0, FI0, W1N = 0, 768, 1152, 1664, 2432, 2448, 2832


def build_mixer(k, I, O, c, dbg=None, phases=(1, 2, 3, 4, 5, 6), tag="", gmode="compute"):
    nc, P, A = k.nc, k.P, k.A
    def scr(name, shape, dt):
        if dbg is not None and name in dbg:
            return nc.dram_tensor(name, shape, dt, kind="ExternalOutput").ap()
        if name + tag not in k.scr:
            k.scr[name + tag] = nc.dram_tensor(name + tag, shape, dt).ap()
        return k.scr[name + tag]

    qk_scr = scr("qk_scr", [6, 128, S], BF16)
    v_scr = scr("v_scr", [S, 3, 132], BF16)
    zs_scr = scr("zs_scr", [S, 512], BF16)
    f_scr = scr("f_scr", [S, 384], BF16)
    xbcT_scr = scr("xbcT_scr", [6, 128, S + 8], BF16)
    xsB_scr = scr("xsB_scr", [S, 640], BF16)
    bcT_scr = scr("bcT_scr", [2, 128, S], BF16)
    ybwd_scr = scr("ybwd_scr", [S, 512], F32)
    ynT_scr = scr("ynT_scr", [4, 128, S], BF16)
    un_scr = scr("un_scr", [3, S, 132], F32)
    oT_scr = scr("oT_scr", [128, S], BF16)
    z_scr = scr("z_scr", [128, 128, 384], BF16)
    xT_scr = scr("xT_scr", [6, 128, S], BF16)
    dt_dram = scr("dt_scr", [S, 16], F32)
    g_scr = scr("g_scr", [24, 128, S], BF16) if gmode != "compute" else None

    if 1 in phases:
        A.reset()
        W1 = A.alloc([8, W1N], BF16)
        gbc = A.alloc([D], F32)
        gqk = A.alloc([2], F32)
        dtb = A.alloc([16], F32)
        dt_all = None
        xt = [A.alloc([4, D], F32) for _ in range(2)]
        hT = [A.alloc([8, 512], BF16) for _ in range(2)]
        junk = A.alloc([D], BF16)
        ss = A.alloc([8], F32)
        rs = A.alloc([8], F32)
        hb = [A.alloc([D], BF16) for _ in range(2)]
        sq = [A.alloc([512], BF16) for _ in range(2)]
        lnv = [A.alloc([512], F32) for _ in range(2)]
        qk_st = [A.alloc([6, 512], BF16) for _ in range(2)]
        v_st = [A.alloc([4, 3, 132], BF16) for _ in range(2)]
        z_st = [A.alloc([4, 512], BF16) for _ in range(2)]
        f_st = [A.alloc([4, 384], BF16) for _ in range(2)]
        x_st = [A.alloc([6, 512], BF16) for _ in range(2)]
        dtt = A.alloc([4, 16], F32)
        zpad = A.alloc([8], BF16)
        for kc in range(8):
            load_cast(k, W1[:, kc, 0:768], I["w1"][kc * 128:(kc + 1) * 128, 0:768], 768, "W1qk")
        for kc in range(8):
            load_cast(k, W1[:, kc, 768:W1N], I["w1"][kc * 128:(kc + 1) * 128, 768:W1N], W1N - 768, "W1")
        P.dma("sp", gbc, I["g_mix"], [], ["gbc"])
        P.dma("sp", gqk, I["gqk"], [], ["gqk"])
        P.ts("dve", gqk[:, 0:1], gqk[:, 0:1], 128.0 ** -0.5, ALU.mult, ["gqk"], ["gqk"])
        P.dma("sp", dtb, I["dtb"], [], ["dtb"])
        P.memset("dve", zpad, 0.0, ["zpad"])
        for i in range(2):
            P.memset("pool", v_st[i], 0.0, [("v_st", i)])
            P.memset("pool", v_st[i][:, :, :, 128:129], 1.0, [("v_st", i)])
        for cc in range(6):
            P.dma("sp", xbcT_scr[cc, :, 0:4], zpad[:, 0:4], ["zpad"], [("xbcT", cc, -1)])
            P.dma("sp", xbcT_scr[cc, :, S + 4:S + 8], zpad[:, 4:8], ["zpad"], [("xbcT", cc, -2)])
        gen = [3, 4, 5]
        gi = 0
        for b in range(NB):
            pb = b % 2
            P.dma("sp", xt[pb], I["x"][b * 512:(b + 1) * 512, :].rearrange("(t p) d -> p t d", p=128), [], [("xt", pb)])
            for t in range(4):
                norm_transpose(k, xt[pb][:, t, :], ("xt", pb), gbc, c, hT[pb], ("hT", pb), t * 128,
                               (junk, ss[:, t:t + 1], rs[:, t:t + 1], hb[t % 2]), "n%d" % (t % 2), 0, "pT", t % 2)
            hk = ("hT", pb)
            for hd in range(6):
                s2 = hd % 2
                pq = k.bank(1 + s2)
                for kc in range(8):
                    P.mm(pq, W1[:, kc, QK0 + hd * 128:QK0 + (hd + 1) * 128], hT[pb][:, kc, :], kc == 0, kc == 7, ["W1qk", hk], [("pq", s2)])
                P.act(sq[s2], pq, AF.Square, [("pq", s2)], [("sq", s2)])
                pss = k.bank(6 + s2)
                P.mm(pss, c["ones"], sq[s2], True, True, [("sq", s2), "consts"], [("pss", s2)])
                P.act(lnv[s2], pss, AF.Ln, [("pss", s2), "consts"], [("lnv", s2)], bias=c["eps"], scale=1.0 / 128)
                P.act(lnv[s2], lnv[s2], AF.Exp, [("lnv", s2)], [("lnv", s2)], scale=-0.5)
                P.stt(qk_st[pb][:, hd, :], pq, gqk[:, (hd // 3):(hd // 3) + 1], lnv[s2], ALU.mult, ALU.mult,
                      [("pq", s2), "gqk", ("lnv", s2)], [("qk_st", pb)])
            P.dma("sp", qk_scr[:, :, b * 512:(b + 1) * 512].rearrange("h p t -> p h t"), qk_st[pb], [("qk_st", pb)], [("qk_scr", b)])
            for t in range(4):
                lhs = lambda kc: hT[pb][:, kc, t * 128:(t + 1) * 128]
                g_ = gen[gi % 3]; gi += 1
                pv = k.bank(g_)[:, 0:384]
                for kc in range(8):
                    P.mm(pv, lhs(kc), W1[:, kc, V0:V0 + 384], kc == 0, kc == 7, ["W1", hk], [("pg", g_)])
                P.cp("dve", v_st[pb][:, t, :, 0:128], pv.rearrange("p (g e) -> p g e", g=3), [("pg", g_)], [("v_st", pb)])
                g_ = gen[gi % 3]; gi += 1
                pz = k.bank(g_)
                for kc in range(8):
                    P.mm(pz, lhs(kc), W1[:, kc, Z0:Z0 + 512], kc == 0, kc == 7, ["W1", hk], [("pg", g_)])
                P.act(z_st[pb][:, t, :], pz, AF.Silu, [("pg", g_)], [("z_st", pb)])
                g_ = gen[gi % 3]; gi += 1
                pf = k.bank(g_)[:, 0:384]
                for kc in range(8):
                    P.mm(pf, lhs(kc), W1[:, kc, FI0:FI0 + 384], kc == 0, kc == 7, ["W1", hk], [("pg", g_)])
                P.cp("dve", f_st[pb][:, t, :], pf, [("pg", g_)], [("f_st", pb)])
                g_ = gen[gi % 3]; gi += 1
                pd = k.bank(g_)[:, 0:16]
                for kc in range(8):
                    P.mm(pd, lhs(kc), W1[:, kc, DT0:DT0 + 16], kc == 0, kc == 7, ["W1", hk], [("pg", g_)])
                P.tt("dve", dtt[:, t, :], pd, dtb, ALU.add, [("pg", g_), "dtb"], ["dtt"])
            P.dma("sp", v_scr[b * 512:(b + 1) * 512].rearrange("(t p) g e -> p t g e", p=128), v_st[pb], [("v_st", pb)], [("v_scr", b)])
            P.dma("sp", zs_scr[b * 512:(b + 1) * 512, :].rearrange("(t p) d -> p t d", p=128), z_st[pb], [("z_st", pb)], [("zs_scr", b)])
            P.dma("sp", f_scr[b * 512:(b + 1) * 512, :].rearrange("(t p) d -> p t d", p=128), f_st[pb], [("f_st", pb)], [("f_scr", b)])
            P.dma("sp", dt_dram[b * 512:(b + 1) * 512, :].rearrange("(t p) d -> p t d", p=128), dtt, ["dtt"], [("dt_scr", b)])
            for cc in range(6):
                g_ = gen[gi % 3]; gi += 1
                px = k.bank(g_)
                for kc in range(8):
                    P.mm(px, W1[:, kc, XBC0 + cc * 128:XBC0 + (cc + 1) * 128], hT[pb][:, kc, :], kc == 0, kc == 7, ["W1", hk], [("pg", g_)])
                P.cp("act", x_st[pb][:, cc, :], px, [("pg", g_)], [("x_st", pb)])
            P.dma("sp", xbcT_scr[:, :, 4 + b * 512:4 + (b + 1) * 512].rearrange("h p t -> p h t"), x_st[pb], [("x_st", pb)], [("xbcT", b)])
        P.barrier()

    if 2 in phases:
        A.reset()
        cw = A.alloc([6, 7], F32)
        cb = A.alloc([6], F32)
        U = [A.alloc([518], BF16) for _ in range(2)]
        dg = A.alloc([42, 128], BF16)
        Yc = [A.alloc([512], BF16) for _ in range(2)]
        xs_st = [A.alloc([4, 640], BF16) for _ in range(2)]
        P.dma("sp", cw, I["cw"], [], ["cw"])
        P.dma("sp", cb, I["cb"], [], ["cw"])
        for cc in range(6):
            for kk in range(7):
                P.ts("dve" if kk % 2 else "pool", dg[:, cc * 7 + kk, :], c["ident"], cw[:, cc, kk:kk + 1], ALU.mult, ["consts", "cw"], ["dg"])
        ui = 0
        for b in range(NB):
            pb = b % 2
            for cc in range(6):
                u = ui % 2; ui += 1
                P.dma("sp", U[u], xbcT_scr[cc, :, 1 + b * 512:1 + b * 512 + 518], [], [("U", u)])
                pc = k.bank(2 + u)
                for kk in range(7):
                    P.mm(pc, dg[:, cc * 7 + kk, :], U[u][:, kk:kk + 512], kk == 0, kk == 6, [("U", u), "dg"], [("pc", u)])
                P.act(Yc[u], pc, AF.Silu, [("pc", u), "cw"], [("Yc", u)], bias=cb[:, cc:cc + 1])
                if cc >= 4:
                    P.dma("sp", bcT_scr[cc - 4, :, b * 512:(b + 1) * 512], Yc[u], [("Yc", u)], [("bcT", cc - 4, b)])
                if cc <= 4:
                    p16 = k.bank16(cc % 2)
                    for t in range(4):
                        P.tr(p16[:, t * 128:(t + 1) * 128], Yc[u][:, t * 128:(t + 1) * 128], c["ident"], [("Yc", u), "consts"], [("pt2", cc % 2)])
                    P.cp("act", xs_st[pb][:, :, cc * 128:(cc + 1) * 128], p16[:, 0:512].rearrange("p (t e) -> p t e", t=4), [("pt2", cc % 2)], [("xs_st", pb)])
            P.dma("sp", xsB_scr[b * 512:(b + 1) * 512, :].rearrange("(t p) d -> p t d", p=128), xs_st[pb], [("xs_st", pb)], [("xsB", b)])
        P.barrier()

    if 3 in phases:
        A.reset()
        BT = A.alloc([S], BF16)
        CT = A.alloc([S], BF16)
        dt_sb = A.alloc([NT, 16], F32)
        abc = A.alloc([16], F32)
        dsk = A.alloc([8], F32)
        ngb = A.alloc([512], F32)
        xsB = [A.alloc([640], BF16) for _ in range(2)]
        la = [A.alloc([16], F32) for _ in range(2)]
        lah = [A.alloc([8], BF16) for _ in range(2)]
        lal = [A.alloc([8], BF16) for _ in range(2)]
        lahb = [A.alloc([8, 128], BF16) for _ in range(2)]
        lalb = [A.alloc([8, 128], BF16) for _ in range(2)]
        acs = [A.alloc([16], F32) for _ in range(2)]
        nacs = [A.alloc([8], F32) for _ in range(2)]
        ecs = [A.alloc([8], F32) for _ in range(2)]
        wdec = [A.alloc([8], F32) for _ in range(2)]
        cdec = [A.alloc([8], F32) for _ in range(2)]
        Lt = [A.alloc([8, 128], BF16) for _ in range(2)]
        Mt = [A.alloc([8, 128], BF16) for _ in range(2)]
        xd = [A.alloc([512], BF16) for _ in range(2)]
        xdw = [A.alloc([512], BF16) for _ in range(2)]
        t1 = [A.alloc([512], F32) for _ in range(2)]
        yc = [A.alloc([512], F32) for _ in range(2)]
        H = A.alloc([512], F32)
        Htmp = A.alloc([512], F32)
        Hb = A.alloc([512], BF16)
        ybl = [A.alloc([512], F32) for _ in range(2)]
        zl = [A.alloc([512], BF16) for _ in range(2)]
        yz = [A.alloc([512], F32) for _ in range(2)]
        junk3 = A.alloc([512], BF16)
        ss3 = A.alloc([2], F32)
        ynb = [A.alloc([512], BF16) for _ in range(2)]
        yn_st = [A.alloc([4, 512], BF16) for _ in range(2)]
        P.dma("sp", BT, bcT_scr[0], [], ["BT"])
        P.dma("sp", CT, bcT_scr[1], [], ["CT"])
        P.dma("sp", dt_sb, dt_dram.rearrange("(n p) d -> p n d", p=128), [], ["dt_sb"])
        P.dma("sp", abc, I["alog"], [], ["abc"])
        P.dma("sp", dsk, I["dskip"], [], ["dsk"])
        P.dma("sp", ngb, I["ssd_g"], [], ["ngb"])
        dtf = dt_sb.rearrange("p n d -> p (n d)")
        sp1 = A.alloc([NT * 16], F32)
        P.act(sp1, dtf, AF.Abs, ["dt_sb"], ["sp1"])
        P.act(sp1, sp1, AF.Exp, ["sp1"], ["sp1"], scale=-1.0)
        P.act(sp1, sp1, AF.Ln, ["sp1", "consts"], ["sp1"], bias=c["one"])
        P.ts("dve", dtf, dtf, 0.0, ALU.max, ["dt_sb"], ["dt_sb"])
        P.tt("dve", dtf, dtf, sp1, ALU.add, ["dt_sb", "sp1"], ["dt_sb"])
        P.act(abc, abc, AF.Exp, ["abc"], ["abc"])
        P.ts("dve", abc, abc, -1.0, ALU.mult, ["abc"], ["abc"])
        if dbg is not None and "dtsp_dbg" in dbg:
            dd = nc.dram_tensor("dtsp_dbg", [S, 16], F32, kind="ExternalOutput").ap()
            P.dma("sp", dd.rearrange("(n p) d -> p n d", p=128), dt_sb, ["dt_sb"], ["dd"])

        def ssd_pass(direction):
            d0 = direction * 8
            tri = c["trif"] if direction == 0 else c["trib"]
            msk = c["maskf"] if direction == 0 else c["maskb"]
            order = range(NT) if direction == 0 else range(NT - 1, -1, -1)
            P.memset("dve", H, 0.0, ["H"])
            P.memset("pool", Hb, 0.0, ["Hb"])
            order = list(order)

            def front(it):
                ch = order[it]
                q = it % 2
                cs = slice(ch * 128, (ch + 1) * 128)
                P.dma("sp", xsB[q], xsB_scr[cs, :], [], [("xsB", q)])
                if direction == 0:
                    P.dma("sp", ybl[q], ybwd_scr[cs, :], [("ybwd", ch)], [("ybl", q)])
                    P.dma("sp", zl[q], zs_scr[cs, :], [], [("zl", q)])
                P.tt("dve", la[q][:, 0:8], dt_sb[:, ch, d0:d0 + 8], abc[:, d0:d0 + 8], ALU.mult, ["dt_sb", "abc"], [("la", q)])
                P.cp("dve", lah[q], la[q][:, 0:8], [("la", q)], [("lah", q)])
                P.tt("dve", lal[q], la[q][:, 0:8], lah[q], ALU.subtract, [("la", q), ("lah", q)], [("lal", q)])
                P.cp("act", lahb[q], lah[q].unsqueeze(2).to_broadcast([128, 8, 128]), [("lah", q)], [("lahb", q)])
                P.cp("act", lalb[q], lal[q].unsqueeze(2).to_broadcast([128, 8, 128]), [("lal", q)], [("lalb", q)])
                pa = k.bank(0)
                P.mm(pa[:, 0:8], tri, lah[q], True, False, [("lah", q), "consts"], ["pa"])
                P.mm(pa[:, 0:8], tri, lal[q], False, True, [("lal", q), "consts"], ["pa"])
                P.mm(pa[:, 8:16], c["ones"], lah[q], True, False, [("lah", q), "consts"], ["pa"])
                P.mm(pa[:, 8:16], c["ones"], lal[q], False, True, [("lal", q), "consts"], ["pa"])
                P.cp("dve", acs[q], pa[:, 0:16], ["pa"], [("acs", q)])
                P.ts("dve", nacs[q], acs[q][:, 0:8], -1.0, ALU.mult, [("acs", q)], [("nacs", q)])
                P.act(ecs[q], acs[q][:, 0:8], AF.Exp, [("acs", q)], [("ecs", q)])
                P.tt("dve", wdec[q], acs[q][:, 8:16], acs[q][:, 0:8], ALU.subtract, [("acs", q)], [("wdec", q)])
                P.act(wdec[q], wdec[q], AF.Exp, [("wdec", q)], [("wdec", q)])
                P.act(cdec[q], acs[q][:, 8:16], AF.Exp, [("acs", q)], [("cdec", q)])
                for hb2 in range(2):
                    pk = ("prb", hb2)
                    for h in range(hb2 * 4, hb2 * 4 + 4):
                        pr = k.banks[1 + hb2][:, (h % 4) * 128:(h % 4 + 1) * 128]
                        P.mm(pr, lahb[q][:, h, :], tri, True, False, [("lahb", q), "consts"], [pk])
                        P.mm(pr, lalb[q][:, h, :], tri, False, False, [("lalb", q), "consts"], [pk])
                        P.mm(pr, c["ident"], msk, False, True, ["consts"], [pk])
                    for h in range(hb2 * 4, hb2 * 4 + 4):
                        pr = k.banks[1 + hb2][:, (h % 4) * 128:(h % 4 + 1) * 128]
                        P.act(Lt[q][:, h, :], pr, AF.Exp, [pk, ("nacs", q)], [("Lt", q)], bias=nacs[q][:, h:h + 1])
                pcb = k.bank(3)[:, 0:128]
                P.mm(pcb, BT[:, cs], CT[:, cs], True, True, ["BT", "CT"], ["pcb"])

            def front_b(it):
                ch = order[it]
                q = it % 2
                pcb = k.bank(3)[:, 0:128]
                P.tt("dve", Mt[q], pcb.unsqueeze(1).to_broadcast([128, 8, 128]), Lt[q], ALU.mult, ["pcb", ("Lt", q)], [("Mt", q)])
                xs3 = xsB[q][:, 0:512].rearrange("p (h e) -> p h e", h=8)
                dtb3 = dt_sb[:, ch, d0:d0 + 8].unsqueeze(2).to_broadcast([128, 8, 64])
                P.tt("pool", xd[q].rearrange("p (h e) -> p h e", h=8), xs3, dtb3, ALU.mult, [("xsB", q), "dt_sb"], [("xd", q)])

            def back(it):
                ch = order[it]
                q = it % 2
                cs = slice(ch * 128, (ch + 1) * 128)
                xs3 = xsB[q][:, 0:512].rearrange("p (h e) -> p h e", h=8)
                py = k.bank(4)
                for h in range(8):
                    P.mm(py[:, h * 64:(h + 1) * 64], Mt[q][:, h, :], xd[q][:, h * 64:(h + 1) * 64], True, True, [("Mt", q), ("xd", q)], ["py"])
                pyo = k.bank(5)
                P.mm(pyo, CT[:, cs], Hb, True, True, ["CT", "Hb"], ["pyo"])
                P.tt("dve", t1[q].rearrange("p (h e) -> p h e", h=8), pyo.rearrange("p (h e) -> p h e", h=8),
                     ecs[q].unsqueeze(2).to_broadcast([128, 8, 64]), ALU.mult, ["pyo", ("ecs", q)], [("t1", q)])
                P.tt("dve", yc[q], py, t1[q], ALU.add, ["py", ("t1", q)], [("yc", q)])
                P.tt("pool", xdw[q].rearrange("p (h e) -> p h e", h=8), xd[q].rearrange("p (h e) -> p h e", h=8),
                     wdec[q].unsqueeze(2).to_broadcast([128, 8, 64]), ALU.mult, [("xd", q), ("wdec", q)], [("xdw", q)])
                pst = k.bank(6)
                P.mm(pst, xsB[q][:, 512:640], xdw[q], True, True, [("xsB", q), ("xdw", q)], ["pst"])
                P.tt("pool", Htmp.rearrange("p (h e) -> p h e", h=8), H.rearrange("p (h e) -> p h e", h=8),
                     cdec[q].unsqueeze(2).to_broadcast([128, 8, 64]), ALU.mult, ["H", ("cdec", q)], ["Htmp"])
                P.tt("dve", H, pst, Htmp, ALU.add, ["pst", "Htmp"], ["H"])
                P.cp("act", Hb, H, ["H"], ["Hb"])
                if direction == 1:
                    P.dma("sp", ybwd_scr[cs, :], yc[q], [("yc", q)], [("ybwd", ch)])
                else:
                    P.tt("pool", yc[q], yc[q], ybl[q], ALU.add, [("yc", q), ("ybl", q)], [("yc", q)])
                    P.tt("pool", yz[q].rearrange("p (h e) -> p h e", h=8), xs3, dsk.unsqueeze(2).to_broadcast([128, 8, 64]), ALU.mult,
                         [("xsB", q), "dsk"], [("yz", q)])
                    P.tt("pool", yz[q], yz[q], yc[q], ALU.add, [("yz", q), ("yc", q)], [("yz", q)])
                    P.tt("dve", yz[q], yz[q], zl[q], ALU.mult, [("yz", q), ("zl", q)], [("yz", q)])
                    P.act(junk3, yz[q], AF.Square, [("yz", q)], ["junk3", ("ss3", q)], accum=ss3[:, q:q + 1])
                    P.act(ss3[:, q:q + 1], ss3[:, q:q + 1], AF.Sqrt, [("ss3", q), "consts"], [("ss3", q)], bias=c["eps"], scale=1.0 / 512)
                    P.recip(ss3[:, q:q + 1], ss3[:, q:q + 1], [("ss3", q)], [("ss3", q)])
                    P.stt(ynb[q], yz[q], ss3[:, q:q + 1], ngb, ALU.mult, ALU.mult, [("yz", q), ("ss3", q), "ngb"], [("ynb", q)])
                    p16 = k.bank16(7)
                    for cc in range(4):
                        P.tr(p16[:, cc * 128:(cc + 1) * 128], ynb[q][:, cc * 128:(cc + 1) * 128], c["ident"], [("ynb", q), "consts"], ["pt3"])
                    sq_ = (ch // 4) % 2
                    P.cp("act", yn_st[sq_][:, :, (ch % 4) * 128:(ch % 4 + 1) * 128], p16[:, 0:512].rearrange("p (c t) -> p c t", c=4), ["pt3"], [("yn_st", sq_)])
                    if ch % 4 == 3:
                        bb = ch // 4
                        P.dma("sp", ynT_scr[:, :, bb * 512:(bb + 1) * 512].rearrange("c p t -> p c t"), yn_st[sq_], [("yn_st", sq_)], [("ynT", bb)])

            front(0)
            front_b(0)
            for it in range(NT):
                if it + 1 < NT:
                    front(it + 1)
                back(it)
                if it + 1 < NT:
                    front_b(it + 1)

        ssd_pass(1)
        ssd_pass(0)
        P.barrier()

    if 4 in phases:
        A.reset()
        QT = A.alloc([S], BF16)
        KT = A.alloc([S], BF16)
        Vc = A.alloc([NT, 132], BF16)
        biasT = A.alloc([9, 128], BF16)
        Pt = [A.alloc([3, 128], BF16) for _ in range(2)]
        o_st = [A.alloc([8, 132], F32) for _ in range(2)]
        un = [A.alloc([3, 132], F32) for _ in range(2)]
        rden = [A.alloc([1], F32) for _ in range(2)]
        ob = [A.alloc([128], BF16) for _ in range(2)]
        oT = A.alloc([S], BF16)
        load_cast(k, biasT.rearrange("p a b -> p (a b)"), I["biasT"], 9 * 128, "biasT")
        flush = 0
        for g, dil in enumerate((1, 4, 16)):
            sub = S // dil
            ntile = sub // 128
            P.dma("sp", QT, qk_scr[g], [], ["QT"])
            P.dma("sp", KT, qk_scr[3 + g], [], ["KT"])
            vsrc = v_scr[:, g, :].rearrange("(m i r) c -> i r m c", r=dil, i=128)
            for r in range(dil):
                P.dma("sp", Vc[:, r * ntile:(r + 1) * ntile, :], vsrc[:, r, :, :], [], ["Vc"])
            tiles = [(r, m) for r in range(dil) for m in range(ntile)]
            gsz = min(8, ntile)

            def s_part(i):
                r, m = tiles[i]
                q = i % 2
                kts = [mm_ for mm_ in (m - 1, m, m + 1) if 0 <= mm_ < ntile]
                psb = k.bank(q)
                qs = QT[:, r + dil * 128 * m: r + dil * 128 * m + dil * 127 + 1: dil]
                for j, m2 in enumerate(kts):
                    ks = KT[:, r + dil * 128 * m2: r + dil * 128 * m2 + dil * 127 + 1: dil]
                    P.mm(psb[:, j * 128:(j + 1) * 128], ks, qs, True, False, ["QT", "KT"], [("ps4", q)])
                    P.mm(psb[:, j * 128:(j + 1) * 128], c["ident"], biasT[:, g * 3 + (m2 - m + 1), :], False, True, ["biasT", "consts"], [("ps4", q)])
                n = len(kts)
                P.act(Pt[q][:, 0:n, :], psb[:, 0:n * 128].rearrange("p (a b) -> p a b", a=n), AF.Exp, [("ps4", q)], [("Pt", q)])

            def pv_part(i):
                nonlocal flush
                r, m = tiles[i]
                q = i % 2
                kts = [mm_ for mm_ in (m - 1, m, m + 1) if 0 <= mm_ < ntile]
                n = len(kts)
                po = k.bank(2 + q)[:, 0:132]
                for j, m2 in enumerate(kts):
                    P.mm(po, Pt[q][:, j, :], Vc[:, r * ntile + m2, :], j == 0, j == n - 1, [("Pt", q), "Vc"], [("po4", q)])
                sq_ = flush % 2
                P.cp("dve", o_st[sq_][:, m % gsz, :], po, [("po4", q)], [("o_st", sq_)])
                if m % gsz == gsz - 1:
                    m0 = m - (gsz - 1)
                    dst = un_scr[g].rearrange("(m i r) c -> i r m c", r=dil, i=128)[:, r, m0:m0 + gsz, :]
                    P.dma("sp", dst, o_st[sq_][:, 0:gsz, :], [("o_st", sq_)], [("un", flush)])
                    flush += 1

            for i in range(len(tiles) + 1):
                if i < len(tiles):
                    s_part(i)
                if i >= 1:
                    pv_part(i - 1)
        P.barrier()
        for t in range(NT):
            q = t % 2
            P.dma("sp", un[q], un_scr[:, t * 128:(t + 1) * 128, :].rearrange("g p c -> p g c"),
                  [], [("unl", q)])
            P.tt("dve", un[q][:, 0, :], un[q][:, 0, :], un[q][:, 1, :], ALU.add, [("unl", q)], [("unl", q)])
            P.tt("dve", un[q][:, 0, :], un[q][:, 0, :], un[q][:, 2, :], ALU.add, [("unl", q)], [("unl", q)])
            P.recip(rden[q], un[q][:, 0, 128:129], [("unl", q)], [("rden", q)])
            P.ts("dve", ob[q], un[q][:, 0, 0:128], rden[q], ALU.mult, [("unl", q), ("rden", q)], [("ob", q)])
            p16 = k.bank16(4 + q)
            P.tr(p16[:, 0:128], ob[q], c["ident"], [("ob", q), "consts"], [("pt4", q)])
            P.cp("act", oT[:, t * 128:(t + 1) * 128], p16[:, 0:128], [("pt4", q)], ["oT"])
        P.dma("sp", oT_scr, oT, ["oT"], ["oT_scr"])
        P.barrier()

    if 5 in phases:
        A.reset()
        E = A.alloc([128, 128], BF16, parts=64)
        T2 = A.alloc([2, 256], BF16)
        Fb = [A.alloc([16, 384], BF16, parts=64) for _ in range(2)]
        Zst = [A.alloc([4, 384], BF16) for _ in range(2)]
        Zl = [A.alloc([2, 384], BF16) for _ in range(3)]
        XT = A.alloc([6, S], BF16)
        for n2 in range(0, 128, 16):
            load_cast(k, E[:, n2:n2 + 16, :].rearrange("p a b -> p (a b)"), I["E1"][:, n2 * 128:(n2 + 16) * 128], 2048, "E")
        load_cast(k, T2.rearrange("p a b -> p (a b)"), I["T2"], 512, "T2")
        fsrc = f_scr.rearrange("(n1 n2) c -> n1 n2 c", n2=128)
        zi = 0
        for nb in range(8):
            q = nb % 2
            P.dma("sp", Fb[q], fsrc[:, nb * 16:(nb + 1) * 16, :], [], [("Fb", q)])
            for i in range(16):
                n2 = nb * 16 + i
                pz_ = k.bank(n2 % 2)[:, 0:384]
                P.mm(pz_, E[:, n2, :], Fb[q][:, i, :], True, True, ["E", ("Fb", q)], [("pz5", n2 % 2)])
                zq = (n2 // 4) % 2
                P.cp("act" if n2 % 2 else "dve", Zst[zq][:, n2 % 4, :], pz_, [("pz5", n2 % 2)], [("Zst", zq)])
                if n2 % 4 == 3:
                    P.dma("sp", z_scr[:, n2 - 3:n2 + 1, :], Zst[zq], [("Zst", zq)], [("z_scr", n2 // 4)])
        zall = [("z_scr", i) for i in range(32)]
        for k1 in range(64):
            q = k1 % 3
            P.dma("sp", Zl[q], z_scr[k1:k1 + 65:64].rearrange("r n c -> n r c"), zall, [("Zl", q)])
            for ch3 in range(3):
                px_ = k.bank(2 + (k1 * 3 + ch3) % 2)[:, 0:256]
                pk_ = ("px5", (k1 * 3 + ch3) % 2)
                P.mm(px_, Zl[q][:, 0, ch3 * 128:(ch3 + 1) * 128], T2[:, 0, :], True, False, [("Zl", q), "T2"], [pk_])
                P.mm(px_, Zl[q][:, 1, ch3 * 128:(ch3 + 1) * 128], T2[:, 1, :], False, True, [("Zl", q), "T2"], [pk_])
                for ri in range(2):
                    P.cp("act" if ri else "dve", XT[:, ri * 3 + ch3, k1:k1 + 64 * 127 + 1:64], px_[:, ri * 128:(ri + 1) * 128], [pk_], ["XT"])
        P.dma("sp", xT_scr.rearrange("c p t -> p c t"), XT, ["XT"], ["xT_scr"])
        P.barrier()

    if 6 in phases:
        A.reset()
        BW = 256
        TPB = BW // 128
        Wg = A.alloc([8, 3072], BF16) if gmode != "load" else None
        Wba = A.alloc([D], BF16)
        Wbs = A.alloc([4, D], BF16)
        Wbf = A.alloc([6, D], BF16)
        Wo = A.alloc([8, D], BF16)
        T256 = A.alloc([3, 2, 256], BF16)
        gbc = A.alloc([D], F32)
        gb = A.alloc([24], F32)
        xt = [A.alloc([2, D], F32) for _ in range(2)]
        hT = [A.alloc([8, BW], BF16) for _ in range(2)]
        junk = A.alloc([D], BF16)
        ss = A.alloc([8], F32)
        rs = A.alloc([8], F32)
        hb = [A.alloc([D], BF16) for _ in range(2)]
        Gs = [A.alloc([24, BW], BF16) for _ in range(2 if gmode == "load" else 1)]
        oTb = [A.alloc([BW], BF16) for _ in range(2)]
        ynTb = [A.alloc([4, BW], BF16) for _ in range(2)]
        xTb = [A.alloc([6, BW], BF16) for _ in range(2)]
        fo = A.alloc([6, BW], BF16)
        ta = [A.alloc([BW], F32) for _ in range(2)]
        tb_ = [A.alloc([BW], F32) for _ in range(2)]
        tc_ = [A.alloc([BW], F32) for _ in range(2)]
        mT = A.alloc([8, BW], BF16)
        osb = [A.alloc([D], F32) for _ in range(2)]
        for kc in range(8):
            if gmode != "load":
                load_cast(k, Wg[:, kc, :], I["wg"][kc * 128:(kc + 1) * 128, :], 3072, "Wg")
        load_cast(k, Wba, I["wba"], D, "Wb")
        for kc in range(4):
            load_cast(k, Wbs[:, kc, :], I["wbs"][kc * 128:(kc + 1) * 128, :], D, "Wb")
        for kc in range(6):
            load_cast(k, Wbf[:, kc, :], I["wbf"][kc * 128:(kc + 1) * 128, :], D, "Wb")
        load_cast(k, T256.rearrange("p a b c -> p (a b c)"), I["T256"], 1536, "T256")
        P.dma("sp", gbc, I["g_mix"], [], ["gbc"])
        P.dma("sp", gb, I["gate_b"], [], ["gb"])
        for kc in range(8):
            load_cast(k, Wo[:, kc, :], I["wo"][kc * 128:(kc + 1) * 128, :], D, "Wo")
        gen = [1, 2]
        gi = 0
        for b in range(S // BW):
            pb = b % 2
            bs = slice(b * BW, (b + 1) * BW)
            G = Gs[pb % len(Gs)]
            gk_ = ("G", pb % len(Gs))
            P.dma("sp", oTb[pb], oT_scr[:, bs], [], [("oTb", pb)])
            P.dma("sp", ynTb[pb], ynT_scr[:, :, bs].rearrange("c p t -> p c t"), [], [("ynTb", pb)])
            P.dma("sp", xTb[pb], xT_scr[:, :, bs].rearrange("c p t -> p c t"), [], [("xTb", pb)])
            if gmode == "load":
                P.dma("sp", G, g_scr[:, :, bs].rearrange("c p t -> p c t"), [], [gk_])
            else:
                P.dma("sp", xt[pb], I["x"][bs, :].rearrange("(t p) d -> p t d", p=128), [], [("xt", pb)])
                for t in range(TPB):
                    norm_transpose(k, xt[pb][:, t, :], ("xt", pb), gbc, c, hT[pb], ("hT", pb), t * 128,
                                   (junk, ss[:, t:t + 1], rs[:, t:t + 1], hb[t % 2]), "n%d" % (t % 2), 0, "pT", t % 2)
                hk = ("hT", pb)
                for gc in range(24):
                    g_ = gen[gi % 2]; gi += 1
                    pg = k.bank(g_)[:, 0:BW]
                    for kc in range(8):
                        P.mm(pg, Wg[:, kc, gc * 128:(gc + 1) * 128], hT[pb][:, kc, :], kc == 0, kc == 7, ["Wg", hk], [("pg", g_)])
                    P.act(G[:, gc, :], pg, AF.Sigmoid, [("pg", g_), "gb"], [gk_], bias=gb[:, gc:gc + 1])
                if gmode == "store":
                    P.dma("sp", g_scr[:, :, bs].rearrange("c p t -> p c t"), G, [gk_], [("g_scr", b)])
            for i3 in range(3):
                for kcc in range(2):
                    g_ = gen[gi % 2]; gi += 1
                    pf_ = k.bank(g_)[:, 0:BW]
                    P.mm(pf_, T256[:, i3, 0, kcc * 128:(kcc + 1) * 128], xTb[pb][:, i3, :], True, False, ["T256", ("xTb", pb)], [("pg", g_)])
                    P.mm(pf_, T256[:, i3, 1, kcc * 128:(kcc + 1) * 128], xTb[pb][:, 3 + i3, :], False, True, ["T256", ("xTb", pb)], [("pg", g_)])
                    P.cp("dve", fo[:, i3 * 2 + kcc, :], pf_, [("pg", g_)], ["fo"])
            for dc in range(8):
                q = dc % 2
                ds_ = slice(dc * 128, (dc + 1) * 128)
                pya, pys, pyf = k.bank(3)[:, 0:BW], k.bank(4)[:, 0:BW], k.bank(5)[:, 0:BW]
                P.mm(pya, Wba[:, ds_], oTb[pb], True, True, ["Wb", ("oTb", pb)], ["pya"])
                for kc in range(4):
                    P.mm(pys, Wbs[:, kc, ds_], ynTb[pb][:, kc, :], kc == 0, kc == 3, ["Wb", ("ynTb", pb)], ["pys"])
                for kc in range(6):
                    P.mm(pyf, Wbf[:, kc, ds_], fo[:, kc, :], kc == 0, kc == 5, ["Wb", "fo"], ["pyf"])
                P.tt("dve", ta[q], pya, G[:, dc, :], ALU.mult, ["pya", gk_], [("ta", q)])
                P.tt("dve", tb_[q], pys, G[:, 8 + dc, :], ALU.mult, ["pys", gk_], [("tb", q)])
                P.tt("dve", tc_[q], pyf, G[:, 16 + dc, :], ALU.mult, ["pyf", gk_], [("tc", q)])
                P.tt("pool", ta[q], ta[q], tb_[q], ALU.add, [("ta", q), ("tb", q)], [("ta", q)])
                P.tt("pool", mT[:, dc, :], ta[q], tc_[q], ALU.add, [("ta", q), ("tc", q)], ["mT"])
            for t in range(TPB):
                q = t % 2
                for half in range(2):
                    po_ = k.bank(6 + half)
                    for kc in range(8):
                        P.mm(po_, mT[:, kc, t * 128:(t + 1) * 128], Wo[:, kc, half * 512:(half + 1) * 512], kc == 0, kc == 7, ["mT", "Wo"], [("po6", half)])
                    P.cp("act" if half else "dve", osb[q][:, half * 512:(half + 1) * 512], po_, [("po6", half)], [("osb", q)])
                P.dma("sp", O[b * BW + t * 128:b * BW + (t + 1) * 128, :], osb[q], [("osb", q)], [("O", b, t)])
        P.barrier()


TS = 2048
TT = TS // 128


def rms_feature_major(k, c, praw, keys_raw, g2, gcol0, out3, okey, sqb, lnb, tag, width, pss_bank, inv_n):
    P = k.P
    for ec in range(2):
        P.act(sqb[ec], praw[ec], AF.Square, [keys_raw[ec]], [(tag, "sq", ec)])
    pss = k.bank(pss_bank)[:, 0:width]
    for ec in range(2):
        P.mm(pss, c["ones"], sqb[ec], ec == 0, ec == 1, [(tag, "sq", ec), "consts"], [(tag, "pss")])
    P.act(lnb, pss, AF.Ln, [(tag, "pss"), "consts"], [(tag, "ln")], bias=c["eps"], scale=inv_n)
    P.act(lnb, lnb, AF.Exp, [(tag, "ln")], [(tag, "ln")], scale=-0.5)
    for ec in range(2):
        P.stt(out3[:, ec, :], praw[ec], g2[:, gcol0 + ec:gcol0 + ec + 1], lnb, ALU.mult, ALU.mult,
              [keys_raw[ec], "gx", (tag, "ln")], [okey])


def build_tok(k, I, O, c, parts_aps):
    nc, P, A = k.nc, k.P, k.A
    A.reset()
    base0 = A.base
    xr = A.alloc([TT, D], F32)
    kTn = A.alloc([4, 2, 256], BF16)
    vext = A.alloc([2, 4, 260], BF16)
    gx = A.alloc([4], F32)
    A.persist()
    P.dma("sp", gx, I["gxqk"], [], ["gx"])
    P.ts("dve", gx[:, 0:2], gx[:, 0:2], 1.0 / 16.0, ALU.mult, ["gx"], ["gx"])
    ptmp = [A.alloc([4, D], F32) for _ in range(2)]
    P.dma("sp", xr, I["xres"].rearrange("(t p) d -> p t d", p=128), [], [("xr", t) for t in range(TT)])
    ci = 0
    for pa in parts_aps:
        for t4 in range(TT // 4):
            q = ci % 2
            P.dma("sp", ptmp[q], pa[t4 * 512:(t4 + 1) * 512, :].rearrange("(t p) d -> p t d", p=128), [], [("ptmp", q)])
            kk = [("xr", t) for t in range(t4 * 4, t4 * 4 + 4)]
            P.tt("dve" if ci % 2 else "pool", xr[:, t4 * 4:(t4 + 1) * 4, :], xr[:, t4 * 4:(t4 + 1) * 4, :], ptmp[q], ALU.add, kk + [("ptmp", q)], kk)
            ci += 1
    P.barrier()
    A.reset()
    Wk = A.alloc([8, D], BF16)
    Wv = A.alloc([8, D], BF16)
    gm = A.alloc([D], F32)
    mt = A.alloc([2, D], F32)
    mT = A.alloc([8, 256], BF16)
    junk = A.alloc([D], BF16)
    ss = A.alloc([8], F32)
    rs = A.alloc([8], F32)
    hb = [A.alloc([D], BF16) for _ in range(2)]
    sqb = [A.alloc([512], BF16) for _ in range(2)]
    lnb = A.alloc([512], F32)
    for kc in range(8):
        load_cast(k, Wk[:, kc, :], I["wxk"][kc * 128:(kc + 1) * 128, :], D, "Wk")
        load_cast(k, Wv[:, kc, :], I["wxv"][kc * 128:(kc + 1) * 128, :], D, "Wv")
    P.dma("sp", gm, I["g_m"], [], ["gbc"])
    P.dma("sp", mt, I["mem"].rearrange("(t p) d -> p t d", p=128), [], ["mt"])
    P.memset("pool", vext, 0.0, ["vext"])
    P.memset("pool", vext[:, :, :, 256:257], 1.0, ["vext"])
    for t in range(2):
        norm_transpose(k, mt[:, t, :], "mt", gm, c, mT, "mT", t * 128,
                       (junk, ss[:, t:t + 1], rs[:, t:t + 1], hb[t % 2]), "n%d" % (t % 2), 0, "pT", t % 2)
    for hh in range(4):
        praw = []
        for ec in range(2):
            pk_ = k.bank(1 + ec)[:, 0:256]
            for kc in range(8):
                P.mm(pk_, Wk[:, kc, hh * 256 + ec * 128:hh * 256 + (ec + 1) * 128], mT[:, kc, :], kc == 0, kc == 7, ["Wk", "mT"], [("praw", ec)])
            praw.append(pk_)
        rms_feature_major(k, c, praw, [("praw", 0), ("praw", 1)], gx, 2, kTn[:, hh, :, :], "kTn",
                          [sqb[0][:, 0:256], sqb[1][:, 0:256]], lnb[:, 0:256], "k", 256, 3, 1.0 / 256)
    for mtile in range(2):
        for half in range(2):
            pv = k.bank(4 + half)
            for kc in range(8):
                P.mm(pv, mT[:, kc, mtile * 128:(mtile + 1) * 128], Wv[:, kc, half * 512:(half + 1) * 512], kc == 0, kc == 7, ["Wv", "mT"], [("pv", half)])
            P.cp("dve", vext[:, mtile, half * 2:half * 2 + 2, 0:256], pv.rearrange("p (h e) -> p h e", h=2), [("pv", half)], ["vext"])
    P.barrier()
    A.reset()
    Wq = A.alloc([8, D], BF16)
    Wo = A.alloc([8, D], BF16)
    gxb = A.alloc([D], F32)
    hT = A.alloc([8, 512], BF16)
    junk = A.alloc([D], BF16)
    ss = A.alloc([8], F32)
    rs = A.alloc([8], F32)
    hb = [A.alloc([D], BF16) for _ in range(2)]
    sqb = [A.alloc([512], BF16) for _ in range(2)]
    lnb = A.alloc([512], F32)
    qTn = A.alloc([2, 512], BF16)
    PT = A.alloc([2, 512], BF16)
    rden = [A.alloc([1], F32) for _ in range(2)]
    osb = A.alloc([4, D], BF16)
    oT = A.alloc([8, 512], BF16)
    for kc in range(8):
        load_cast(k, Wq[:, kc, :], I["wxq"][kc * 128:(kc + 1) * 128, :], D, "Wq")
        load_cast(k, Wo[:, kc, :], I["wxo"][kc * 128:(kc + 1) * 128, :], D, "Wo")
    P.dma("sp", gxb, I["g_x"], [], ["gbc"])
    for b in range(TS // 512):
        for t in range(4):
            norm_transpose(k, xr[:, b * 4 + t, :], ("xr", b * 4 + t), gxb, c, hT, "hT", t * 128,
                           (junk, ss[:, t:t + 1], rs[:, t:t + 1], hb[t % 2]), "n%d" % (t % 2), 0, "pT", t % 2)
        for hh in range(4):
            praw = []
            for ec in range(2):
                pq = k.bank(1 + ec)
                for kc in range(8):
                    P.mm(pq, Wq[:, kc, hh * 256 + ec * 128:hh * 256 + (ec + 1) * 128], hT[:, kc, :], kc == 0, kc == 7, ["Wq", "hT"], [("praw", ec)])
                praw.append(pq)
            rms_feature_major(k, c, praw, [("praw", 0), ("praw", 1)], gx, 0, qTn, "qTn", sqb, lnb, "q", 512, 3, 1.0 / 256)
            for mtile in range(2):
                pl = k.bank(4 + mtile)
                for ec in range(2):
                    P.mm(pl, kTn[:, hh, ec, mtile * 128:(mtile + 1) * 128], qTn[:, ec, :], ec == 0, ec == 1, ["kTn", "qTn"], [("pl", mtile)])
                P.act(PT[:, mtile, :], pl, AF.Exp, [("pl", mtile)], [("PT", mtile)])
            for t in range(4):
                q = t % 2
                po = k.bank(6 + q)[:, 0:257]
                for mtile in range(2):
                    P.mm(po, PT[:, mtile, t * 128:(t + 1) * 128], vext[:, mtile, hh, 0:257], mtile == 0, mtile == 1, [("PT", mtile), "vext"], [("po", q)])
                P.recip(rden[q], po[:, 256:257], [("po", q)], [("rden", q)])
                P.ts("dve", osb[:, t, hh * 256:(hh + 1) * 256], po[:, 0:256], rden[q], ALU.mult, [("po", q), ("rden", q)], [("osb", t)])
        for t in range(4):
            p16 = k.bank16(0)
            for kc in range(8):
                P.tr(p16[:, kc * 128:(kc + 1) * 128], osb[:, t, kc * 128:(kc + 1) * 128], c["ident"], [("osb", t), "consts"], ["pT"])
            P.cp("act" if t % 2 else "dve", oT[:, :, t * 128:(t + 1) * 128], p16.rearrange("p (a b) -> p a b", a=8), ["pT"], ["oT"])
        for t in range(4):
            for half in range(2):
                po2 = k.bank(1 + half)
                for kc in range(8):
                    P.mm(po2, oT[:, kc, t * 128:(t + 1) * 128], Wo[:, kc, half * 512:(half + 1) * 512], kc == 0, kc == 7, ["oT", "Wo"], [("praw", half)])
                xs_ = xr[:, b * 4 + t, half * 512:(half + 1) * 512]
                P.tt("dve", xs_, po2, xs_, ALU.add, [("praw", half), ("xr", b * 4 + t)], [("xr", b * 4 + t)])
    P.barrier()
    A.reset()
    Wd = A.alloc([22, D], BF16)
    gfb = A.alloc([D], F32)
    hT = A.alloc([8, 512], BF16)
    junk = A.alloc([D], BF16)
    ss = A.alloc([8], F32)
    rs = A.alloc([8], F32)
    hb = [A.alloc([D], BF16) for _ in range(2)]
    Wgc = [A.alloc([8, 128], BF16) for _ in range(2)]
    Wuc = [A.alloc([8, 128], BF16) for _ in range(2)]
    sg = [A.alloc([512], F32) for _ in range(2)]
    actT = A.alloc([22, 512], BF16)
    xo = [A.alloc([D], F32) for _ in range(2)]
    for fc in range(22):
        load_cast(k, Wd[:, fc, :], I["wfd"][fc * 128:(fc + 1) * 128, :], D, "Wd")
    P.dma("sp", gfb, I["g_f"], [], ["gbc"])
    wi = 0
    for b in range(TS // 512):
        for t in range(4):
            norm_transpose(k, xr[:, b * 4 + t, :], ("xr", b * 4 + t), gfb, c, hT, "hT", t * 128,
                           (junk, ss[:, t:t + 1], rs[:, t:t + 1], hb[t % 2]), "n%d" % (t % 2), 0, "pT", t % 2)
        for fc in range(22):
            q = wi % 2; wi += 1
            P.dma("pool", Wgc[q], I["wfg"][:, fc * 128:(fc + 1) * 128].rearrange("(kc p) c -> p kc c", p=128), [], [("Wgc", q)])
            P.dma("pool", Wuc[q], I["wfu"][:, fc * 128:(fc + 1) * 128].rearrange("(kc p) c -> p kc c", p=128), [], [("Wuc", q)])
            pg = k.bank(1 + q)
            pu = k.bank(3 + q)
            for kc in range(8):
                P.mm(pg, Wgc[q][:, kc, :], hT[:, kc, :], kc == 0, kc == 7, [("Wgc", q), "hT"], [("pgf", q)])
            for kc in range(8):
                P.mm(pu, Wuc[q][:, kc, :], hT[:, kc, :], kc == 0, kc == 7, [("Wuc", q), "hT"], [("puf", q)])
            P.act(sg[q], pg, AF.Silu, [("pgf", q)], [("sg", q)])
            P.tt("dve", actT[:, fc, :], pu, sg[q], ALU.mult, [("puf", q), ("sg", q)], ["actT"])
        for t in range(4):
            q = t % 2
            for half in range(2):
                pd = k.bank(5 + half)
                for fc in range(22):
                    P.mm(pd, actT[:, fc, t * 128:(t + 1) * 128], Wd[:, fc, half * 512:(half + 1) * 512], fc == 0, fc == 21, ["actT", "Wd"], [("pd", half)])
                P.tt("dve", xo[q][:, half * 512:(half + 1) * 512], pd, xr[:, b * 4 + t, half * 512:(half + 1) * 512], ALU.add,
                     [("pd", half), ("xr", b * 4 + t)], [("xo", q)])
            P.dma("sp", O[(b * 4 + t) * 128:(b * 4 + t + 1) * 128, :], xo[q], [("xo", q)], [("O", b, t)])
    P.barrier()
    A.base = base0


def tok_inputs(inp, l, b, j, xres, parts):
    d = {}
    d["xres"] = np.ascontiguousarray(xres)
    if parts is not None:
        d["parts"] = np.ascontiguousarray(parts)
    d["mem"] = np.ascontiguousarray(inp["mem"][b])
    d["g_x"] = rep(inp["xattn_norm_g"][l])
    d["g_m"] = rep(inp["mem_norm_g"][l])
    d["g_f"] = rep(inp["ffn_norm_g"][l])
    gq = inp["xattn_q_norm_g"][l].reshape(2, 128).T
    gk = inp["xattn_k_norm_g"][l].reshape(2, 128).T
    d["gxqk"] = np.ascontiguousarray(np.concatenate([gq, gk], axis=1).astype(np.float32))
    for nm, key in (("wxq", "w_xq"), ("wxk", "w_xk"), ("wxv", "w_xv"), ("wxo", "w_xo"), ("wfg", "w_ffn_gate"), ("wfu", "w_ffn_up"), ("wfd", "w_ffn_down")):
        d[nm] = np.ascontiguousarray(inp[key][l])
    hc = host_consts()
    for nm in ("ident", "trif", "trib", "maskf", "maskb"):
        d[nm] = hc[nm]
    return d


TOK_IN = dict(xres=[TS, D], parts=[4, TS, D], mem=[256, D], g_x=[128, D], g_m=[128, D], g_f=[128, D], gxqk=[128, 4],
              wxq=[D, D], wxk=[D, D], wxv=[D, D], wxo=[D, D], wfg=[D, DFF], wfu=[D, DFF], wfd=[DFF, D],
              ident=[128, 128], trif=[128, 128], trib=[128, 128], maskf=[128, 128], maskb=[128, 128])


def build_tok_program():
    nc = bass.Bass("TRN2", target_bir_lowering=False)
    I = {n: nc.dram_tensor(n, shp, F32, kind="ExternalInput").ap() for n, shp in TOK_IN.items()}
    O = nc.dram_tensor("xout", [TS, D], F32, kind="ExternalOutput").ap()
    k = K(nc)
    c = mk_consts(k, I)
    k.A.persist()
    build_tok(k, I, O, c, [I["parts"][i] for i in range(4)])
    info = k.P.emit()
    return nc, info


def t5_bucket(rel):
    half_b, exact = 16, 8
    dist = np.abs(rel)
    log_ratio = np.log(np.maximum(dist, 1) / exact) / np.log(1024 / exact)
    far = np.minimum(exact + (log_ratio * (half_b - exact)).astype(np.int32), half_b - 1)
    return np.where(rel > 0, half_b, 0) + np.where(dist < exact, dist, far)


def host_consts():
    i = np.arange(128)
    c = {}
    c["ident"] = np.eye(128, dtype=np.float32)
    c["trif"] = (i[:, None] <= i[None, :]).astype(np.float32)
    c["trib"] = (i[:, None] >= i[None, :]).astype(np.float32)
    c["maskf"] = np.where(i[:, None] <= i[None, :], 0.0, NEG).astype(np.float32)
    c["maskb"] = np.where(i[:, None] >= i[None, :], 0.0, NEG).astype(np.float32)
    n1 = np.arange(64)[:, None, None]
    n2 = np.arange(128)[None, :, None]
    k1 = np.arange(64)[None, None, :]
    ang = 2 * np.pi * ((k1 * (128 * n1 + n2)) % 8192) / 8192.0
    E = np.concatenate([np.cos(ang), -np.sin(ang)], axis=2) / 8.0
    c["E1"] = E.reshape(64, 128 * 128).astype(np.float32)
    a2 = 2 * np.pi * ((i[:, None] * i[None, :]) % 128) / 128.0
    C2, S2 = np.cos(a2) / math.sqrt(128), np.sin(a2) / math.sqrt(128)
    c["T2"] = np.concatenate([C2, -S2, S2, C2], axis=1).astype(np.float32)
    return c


def core_consts(j):
    kc = np.arange(256)[None, :]
    out = np.zeros((128, 3, 2, 256), np.float32)
    for i3 in range(3):
        hg = 3 * j + i3
        cidx = (hg % 2) * 128 + np.arange(128)[:, None]
        ang = 2 * np.pi * ((cidx * kc) % 256) / 256.0
        out[:, i3, 0, :] = np.cos(ang) / 16.0
        out[:, i3, 1, :] = np.sin(ang) / 16.0
    return out.reshape(128, 1536)


def rep(v, n=128):
    return np.ascontiguousarray(np.broadcast_to(np.asarray(v, np.float32).reshape(1, -1), (n, np.asarray(v).size)))


def mixer_inputs(inp, l, j, xb):
    w_in = inp["w_in"][l]
    cols = []
    for base in (0, 1536):
        for g in range(3):
            cols.append(np.arange(base + (g * 4 + j) * 128, base + (g * 4 + j + 1) * 128))
    for g in range(3):
        cols.append(np.arange(3072 + (g * 4 + j) * 128, 3072 + (g * 4 + j + 1) * 128))
    cols.append(np.arange(4608 + j * 512, 4608 + (j + 1) * 512))
    cols.append(np.arange(6656 + j * 512, 6656 + (j + 1) * 512))
    cols.append(np.arange(6656 + 2048 + j * 128, 6656 + 2048 + (j + 1) * 128))
    cols.append(np.arange(6656 + 2560 + j * 128, 6656 + 2560 + (j + 1) * 128))
    cols.append(np.arange(9728 + j * 8, 9728 + (j + 1) * 8))
    cols.append(np.arange(9728 + 32 + j * 8, 9728 + 32 + (j + 1) * 8))
    cols.append(np.arange(9792 + j * 384, 9792 + (j + 1) * 384))
    cols = np.concatenate(cols)
    assert cols.size == W1N
    d = {}
    d["x"] = xb
    d["w1"] = np.ascontiguousarray(w_in[:, cols])
    d["wg"] = np.ascontiguousarray(w_in[:, 11328:14400])
    d["g_mix"] = rep(inp["mix_norm_g"][l])
    d["gqk"] = np.ascontiguousarray(np.stack([inp["attn_q_norm_g"][l], inp["attn_k_norm_g"][l]], axis=1).astype(np.float32))
    d["dtb"] = rep(np.concatenate([inp["dt_bias"][l][0, j * 8:(j + 1) * 8], inp["dt_bias"][l][1, j * 8:(j + 1) * 8]]))
    d["alog"] = rep(np.concatenate([inp["a_log"][l][0, j * 8:(j + 1) * 8], inp["a_log"][l][1, j * 8:(j + 1) * 8]]))
    d["dskip"] = rep(inp["d_skip"][l][j * 8:(j + 1) * 8])
    d["ssd_g"] = rep(inp["ssd_norm_g"][l][j * 512:(j + 1) * 512])
    xbc_ch = np.concatenate([np.arange(j * 512, (j + 1) * 512), np.arange(2048 + j * 128, 2048 + (j + 1) * 128),
                             np.arange(2560 + j * 128, 2560 + (j + 1) * 128)])
    cwj = inp["conv_w"][l][:, xbc_ch]
    d["cw"] = np.ascontiguousarray(cwj.reshape(7, 6, 128).transpose(2, 1, 0).reshape(128, 42))
    d["cb"] = np.ascontiguousarray(inp["conv_b"][l][xbc_ch].reshape(6, 128).T)
    kk = np.arange(128)[:, None]
    qq = np.arange(128)[None, :]
    bt = np.zeros((128, 9, 128), np.float32)
    for g, dil in enumerate((1, 4, 16)):
        for di, dlt in enumerate((-1, 0, 1)):
            rel = kk + 128 * dlt - qq
            vals = inp["rel_bias"][t5_bucket(rel * dil), g * 4 + j]
            bt[:, g * 3 + di, :] = np.where(np.abs(rel) <= 64, vals, NEG)
    d["biasT"] = bt.reshape(128, 9 * 128)
    d["wba"] = np.ascontiguousarray(inp["w_branch_attn"][l][j * 128:(j + 1) * 128, :])
    d["wbs"] = np.ascontiguousarray(inp["w_branch_ssd"][l][j * 512:(j + 1) * 512, :])
    wbf = inp["w_branch_fourier"][l]
    d["wbf"] = np.ascontiguousarray(np.concatenate([wbf[((3 * j + i3) // 2) * 256:((3 * j + i3) // 2 + 1) * 256, :] for i3 in range(3)], axis=0))
    d["wo"] = np.ascontiguousarray(inp["w_mix_out"][l])
    d["gate_b"] = np.ascontiguousarray(inp["gate_bias"][l].reshape(24, 128).T)
    d["T256"] = core_consts(j)
    d.update(host_consts())
    return d


MIX_IN = dict(x=[S, D], w1=[D, W1N], wg=[D, 3072], g_mix=[128, D], gqk=[128, 2], dtb=[128, 16], alog=[128, 16], dskip=[128, 8],
              ssd_g=[128, 512], cw=[128, 42], cb=[128, 6], biasT=[128, 1152], wba=[128, D], wbs=[512, D], wbf=[768, D], wo=[D, D],
              gate_b=[128, 24], T256=[128, 1536], ident=[128, 128], trif=[128, 128], trib=[128, 128], maskf=[128, 128],
              maskb=[128, 128], E1=[64, 16384], T2=[128, 512])


def build_mixer_program(dbg=None, phases=(1, 2, 3, 4, 5, 6), gmode="compute"):
    nc = bass.Bass("TRN2", target_bir_lowering=False)
    I = {n: nc.dram_tensor(n, shp, F32, kind="ExternalInput").ap() for n, shp in MIX_IN.items()}
    I["cw"] = I["cw"].rearrange("p (a b) -> p a b", a=6)
    O = nc.dram_tensor("mix_out", [S, D], F32, kind="ExternalOutput").ap()
    k = K(nc)
    c = mk_consts(k, I)
    k.A.persist()
    build_mixer(k, I, O, c, dbg, phases, gmode=gmode)
    info = k.P.emit()
    return nc, info


CONST_NAMES = ("ident", "trif", "trib", "maskf", "maskb", "E1", "T2")
SLOT_DEP = ("w1", "dtb", "alog", "dskip", "ssd_g", "cw", "cb", "biasT", "wba", "wbs", "wbf")
LAYER_DEP = ("wg", "g_mix", "gqk", "wo", "gate_b")
TOK_LAYER = ("g_x", "g_m", "g_f", "gxqk", "wxq", "wxk", "wxv", "wxo", "wfg", "wfu", "wfd")


def build_fused_program():
    nc = bass.Bass("TRN2", target_bir_lowering=False)
    I = {}

    def ext(name, shp):
        I[name] = nc.dram_tensor(name, shp, F32, kind="ExternalInput").ap()

    ext("x", [S, D])
    ext("mem", [256, D])
    for n in CONST_NAMES:
        ext(n, MIX_IN[n])
    for j in range(4):
        ext("T256_%d" % j, MIX_IN["T256"])
    for l in range(2):
        for n in LAYER_DEP:
            ext("%s_%d" % (n, l), MIX_IN[n])
        for n in SLOT_DEP:
            for j in range(4):
                ext("%s_%d_%d" % (n, l, j), MIX_IN[n])
        for n in TOK_LAYER:
            ext("%s_%d" % (n, l), TOK_IN[n])
    out = nc.dram_tensor("xout", [S, D], F32, kind="ExternalOutput").ap()
    mixp = [nc.dram_tensor("mixp%d" % j, [S, D], F32).ap() for j in range(4)]
    xl0 = nc.dram_tensor("xl0", [S, D], F32).ap()
    k = K(nc)
    P = k.P
    c = mk_consts(k, I)
    k.A.persist()
    for l in range(2):
        xin = I["x"] if l == 0 else xl0
        for j in range(4):
            Im = {n: I[n] for n in CONST_NAMES}
            Im["T256"] = I["T256_%d" % j]
            Im["x"] = xin
            for n in LAYER_DEP:
                Im[n] = I["%s_%d" % (n, l)]
            for n in SLOT_DEP:
                Im[n] = I["%s_%d_%d" % (n, l, j)]
            Im["cw"] = Im["cw"].rearrange("p (a b) -> p a b", a=6)
            build_mixer(k, Im, mixp[j], c, gmode="store" if j == 0 else "load")
        It = {n: I[n] for n in CONST_NAMES[:5]}
        It["mem"] = I["mem"]
        for n in TOK_LAYER:
            It[n] = I["%s_%d" % (n, l)]
        xout_l = xl0 if l == 0 else out
        for s_ in range(4):
            rows = slice(s_ * TS, (s_ + 1) * TS)
            It["xres"] = xin[rows, :]
            build_tok(k, It, xout_l[rows, :], c, [mixp[j][rows, :] for j in range(4)])
    info = P.emit()
    return nc, info


def fused_inputs(inp, b):
    d = {}
    x = np.ascontiguousarray(inp["x"][b], dtype=np.float32)
    d["x"] = x
    for l in range(2):
        for j in range(4):
            m = mixer_inputs(inp, l, j, x)
            if l == 0:
                d["T256_%d" % j] = m["T256"]
            if j == 0:
                for n in LAYER_DEP:
                    d["%s_%d" % (n, l)] = m[n]
                if l == 0:
                    for n in CONST_NAMES:
                        d[n] = m[n]
            for n in SLOT_DEP:
                d["%s_%d_%d" % (n, l, j)] = m[n]
        t = tok_inputs(inp, l, b, 0, x[:TS], None)
        if l == 0:
            d["mem"] = t["mem"]
        for n in TOK_LAYER:
            d["%s_%d" % (n, l)] = t[n]
    return d


_PROGS = {}
FUSED = False


def _prog(name):
    if name not in _PROGS:
        _PROGS[name] = {"mix": build_mixer_program, "tok": build_tok_program, "fused": build_fused_program}[name]()[0]
    return _PROGS[name]


def kernel(**inp):
    inp = {k_: np.asarray(v) for k_, v in inp.items()}
    cores = list(range(8))
    if FUSED:
        per_b = [fused_inputs(inp, b) for b in range(2)]
        ins = [per_b[cid // 4] for cid in cores]
        res = run_bass_kernel_spmd(_prog("fused"), ins, core_ids=cores)
        x = np.stack([np.concatenate([np.asarray(res.results[b * 4 + j]["xout"])[j * TS:(j + 1) * TS] for j in range(4)], axis=0)
                      for b in range(2)], axis=0)
        return np.ascontiguousarray(x, dtype=np.float32)
    x = np.ascontiguousarray(inp["x"], dtype=np.float32)
    for l in range(2):
        ins = [mixer_inputs(inp, l, cid % 4, x[cid // 4]) for cid in cores]
        res = run_bass_kernel_spmd(_prog("mix"), ins, core_ids=cores)
        part = [np.asarray(res.results[cid]["mix_out"]) for cid in cores]
        ins = []
        for cid in cores:
            b, j = cid // 4, cid % 4
            sl = slice(j * TS, (j + 1) * TS)
            parts = np.stack([part[b * 4 + i][sl] for i in range(4)], axis=0)
            ins.append(tok_inputs(inp, l, b, j, x[b, sl], parts))
        res = run_bass_kernel_spmd(_prog("tok"), ins, core_ids=cores)
        x = np.stack([np.concatenate([np.asarray(res.results[b * 4 + j]["xout"]) for j in range(4)], axis=0) for b in range(2)], axis=0)
        x = np.ascontiguousarray(x, dtype=np.float32)
    return x
```
